# Optimizing a Trainium2 kernel written in Bass

```python
import math
import jax, jax.numpy as jnp
from jax import lax
import numpy as np

D_MODEL = 2048
BATCH = 4
SEQ = 2048
DEPTH = 1
DEC_BATCH = 128
DEC_SEQ = 4
PAST_LEN = 16384
PAGE_SIZE = 128

N_HEADS_A = 8
HEAD_DK = 128
HEAD_DV = 128
QK_WIDTH = N_HEADS_A * HEAD_DK
V_WIDTH = N_HEADS_A * HEAD_DV
CONV_W = 4
CONV_CH = 2 * QK_WIDTH + V_WIDTH
CHUNK = 64
POOL_WINDOWS = (2, 4, 8, 16)
N_POOL_GROUPS = 4
POOL_WIDTH = D_MODEL // 2
POOL_GROUP = POOL_WIDTH // N_POOL_GROUPS
POOL_MAX = 16
D_FF = ((8 * D_MODEL + 3 * 256 - 1) // (3 * 256)) * 256
IN_SIZES = (QK_WIDTH, QK_WIDTH, V_WIDTH, V_WIDTH, N_HEADS_A, N_HEADS_A, POOL_WIDTH, D_MODEL, D_MODEL)
IN_WIDTH = 2 * QK_WIDTH + 2 * V_WIDTH + 2 * N_HEADS_A + POOL_WIDTH + 2 * D_MODEL
EPS = 1e-6

kernel_name = 'gated_delta_pool_hybrid_step'


def rms_norm(x, gain):
    xf = x.astype(jnp.float32)
    y = xf * lax.rsqrt(jnp.mean(xf * xf, axis=-1, keepdims=True) + EPS)
    return (y * gain.astype(jnp.float32)).astype(x.dtype)


def l2_norm(x):
    xf = x.astype(jnp.float32)
    return xf * lax.rsqrt(jnp.sum(xf * xf, axis=-1, keepdims=True) + EPS)


def ada_modulate(h, shift, scale):
    return h * (1.0 + scale[:, None, :]) + shift[:, None, :]


def causal_conv_silu(x, buf, w):
    T = x.shape[1]
    xp = jnp.concatenate([buf.astype(x.dtype), x], axis=1)
    y = xp[:, 0:T] * w[0]
    for j in range(1, CONV_W):
        y = y + xp[:, j:j + T] * w[j]
    return jax.nn.silu(y), xp[:, -(CONV_W - 1):]


def gated_delta_chunked(q, k, v, g, beta, s0):
    B, T, H, DK = q.shape
    DV = v.shape[-1]
    C = CHUNK if T % CHUNK == 0 else T
    N = T // C

    def blk(t):
        t = t.reshape((B, N, C) + t.shape[2:])
        return jnp.moveaxis(t, 3, 2) if t.ndim == 5 else jnp.moveaxis(t, 3, 2)

    q = blk(q) * (DK ** -0.5)
    k = blk(k)
    v = blk(v)
    g = blk(g)
    beta = blk(beta)
    decay = jnp.cumsum(g, axis=-1)
    idx = jnp.arange(C)
    causal = idx[:, None] >= idx[None, :]
    strict = idx[:, None] > idx[None, :]
    diff = decay[..., :, None] - decay[..., None, :]
    lmask = jnp.where(causal, jnp.exp(jnp.where(causal, diff, 0.0)), 0.0)
    kb = k * beta[..., None]
    m = jnp.where(strict, jnp.einsum('bnhrd,bnhsd->bnhrs', kb, k) * lmask, 0.0)
    eye = jnp.eye(C, dtype=jnp.float32)
    tmat = lax.linalg.triangular_solve(eye + m, jnp.broadcast_to(eye, m.shape), left_side=True, lower=True, unit_diagonal=True)
    u_base = jnp.einsum('bnhrs,bnhsv->bnhrv', tmat, v * beta[..., None])
    w_dec = jnp.einsum('bnhrs,bnhsd->bnhrd', tmat, kb * jnp.exp(decay)[..., None])
    qk = jnp.einsum('bnhrd,bnhsd->bnhrs', q, k) * lmask
    last = decay[..., -1:]
    k_tail = k * jnp.exp(last - decay)[..., None]
    q_dec = q * jnp.exp(decay)[..., None]
    chunk_decay = jnp.exp(last[..., 0])
    xs = (u_base.swapaxes(0, 1), w_dec.swapaxes(0, 1), qk.swapaxes(0, 1), k_tail.swapaxes(0, 1), q_dec.swapaxes(0, 1), chunk_decay.swapaxes(0, 1))

    def step(s, inp):
        u_b, w_d, qk_c, k_t, q_d, cd = inp
        u = u_b - jnp.einsum('bhrd,bhdv->bhrv', w_d, s)
        o = jnp.einsum('bhrd,bhdv->bhrv', q_d, s) + jnp.einsum('bhrs,bhsv->bhrv', qk_c, u)
        s = s * cd[..., None, None] + jnp.einsum('bhrd,bhrv->bhdv', k_t, u)
        return s, o

    s_fin, o = lax.scan(step, s0, xs)
    o = jnp.transpose(o, (1, 0, 3, 2, 4)).reshape(B, T, H, DV)
    return o, s_fin


def multiscale_pool(x, buf, pos0):
    B, T, W = x.shape
    P = POOL_MAX - 1
    xp = jnp.concatenate([buf.astype(x.dtype), x], axis=1)
    cs = jnp.cumsum(xp.astype(jnp.float32), axis=1)
    cs = jnp.concatenate([jnp.zeros((B, 1, W), jnp.float32), cs], axis=1)
    pos = pos0 + jnp.arange(T)
    xf = x.astype(jnp.float32)
    outs = []
    for gi in range(N_POOL_GROUPS):
        w = POOL_WINDOWS[gi]
        lo, hi = gi * POOL_GROUP, (gi + 1) * POOL_GROUP
        s = cs[:, P + 1:P + 1 + T, lo:hi] - cs[:, P + 1 - w:P + 1 - w + T, lo:hi]
        cnt = jnp.minimum(w, pos + 1).astype(jnp.float32)
        outs.append(s / cnt[None, :, None] - xf[:, :, lo:hi])
    return jnp.concatenate(outs, axis=-1).astype(x.dtype), xp[:, -P:]


def token_mixers(u, conv_buf, s0, pool_buf, pos0, w_in, conv_w, a_log, dt_bias, o_norm_g, pool_w, pool_scale, w_proj_a, w_proj_b, w_out):
    B, T, _ = u.shape
    f32 = jnp.float32
    proj = u @ w_in
    splits = np.cumsum(IN_SIZES)[:-1].tolist()
    q, k, v, z, a, b, xpool, ga, gb = jnp.split(proj, splits, axis=-1)
    qkv, conv_new = causal_conv_silu(jnp.concatenate([q, k, v], axis=-1), conv_buf, conv_w)
    q, k, v = jnp.split(qkv, [QK_WIDTH, 2 * QK_WIDTH], axis=-1)
    q = l2_norm(q.reshape(B, T, N_HEADS_A, HEAD_DK))
    k = l2_norm(k.reshape(B, T, N_HEADS_A, HEAD_DK))
    v = v.reshape(B, T, N_HEADS_A, HEAD_DV).astype(f32)
    beta = jax.nn.sigmoid(b.astype(f32))
    g = -jnp.exp(a_log.astype(f32)) * jax.nn.softplus(a.astype(f32) + dt_bias.astype(f32))
    o, s_new = gated_delta_chunked(q, k, v, g, beta, s0.astype(f32))
    o = rms_norm(o, o_norm_g) * jax.nn.silu(z.reshape(B, T, N_HEADS_A, HEAD_DV).astype(f32))
    y_a = o.reshape(B, T, V_WIDTH).astype(u.dtype) @ w_proj_a
    y_p, pool_new = multiscale_pool(xpool, pool_buf, pos0)
    y_p = jnp.einsum('btgc,gcd->btgd', y_p.reshape(B, T, N_POOL_GROUPS, POOL_GROUP), pool_w).reshape(B, T, POOL_WIDTH) * pool_scale
    y_b = y_p @ w_proj_b
    merged = jax.nn.sigmoid(ga) * y_a + jax.nn.sigmoid(gb) * y_b
    return merged @ w_out, s_new.astype(s0.dtype), conv_new, pool_new


def decoder_layer(x, c, s0, conv_buf, pool_buf, pos0, w_ada, b_ada, norm1_g, w_in, conv_w, a_log, dt_bias, o_norm_g, pool_w, pool_scale, w_proj_a, w_proj_b, w_out, norm2_g, w_gate_up, w_down):
    mod = jax.nn.silu(c) @ w_ada + b_ada
    sh1, sc1, gt1, sh2, sc2, gt2 = jnp.split(mod, 6, axis=-1)
    u = ada_modulate(rms_norm(x, norm1_g), sh1, sc1)
    mix, s_new, conv_new, pool_new = token_mixers(u, conv_buf, s0, pool_buf, pos0, w_in, conv_w, a_log, dt_bias, o_norm_g, pool_w, pool_scale, w_proj_a, w_proj_b, w_out)
    x = x + gt1[:, None, :] * mix
    h = ada_modulate(rms_norm(x, norm2_g), sh2, sc2)
    gate, up = jnp.split(h @ w_gate_up, 2, axis=-1)
    x = x + gt2[:, None, :] * ((jax.nn.silu(gate) * up) @ w_down)
    return x, s_new, conv_new, pool_new


def setup_inputs(seed: int = 0) -> dict:
    key = jax.random.key(seed)
    ks = jax.random.split(key, 26)
    f32 = jnp.float32
    L = DEPTH

    def nrm(k, shape, s):
        return jax.random.normal(k, shape, f32) * s

    dt = jnp.exp(jax.random.uniform(ks[13], (L, N_HEADS_A), f32, math.log(1e-3), math.log(1e-1)))
    dt_bias = dt + jnp.log(-jnp.expm1(-dt))
    return {
        'x_prompt': nrm(ks[0], (BATCH, SEQ, D_MODEL), 1.0),
        'x_sample': nrm(ks[1], (DEC_BATCH, DEC_SEQ, D_MODEL), 1.0),
        'c_prompt': nrm(ks[2], (BATCH, D_MODEL), 1.0),
        'c_sample': nrm(ks[3], (DEC_BATCH, D_MODEL), 1.0),
        'state_delta': nrm(ks[4], (L, DEC_BATCH, N_HEADS_A, HEAD_DK, HEAD_DV), 0.1),
        'state_conv': nrm(ks[5], (L, DEC_BATCH, CONV_W - 1, CONV_CH), 1.0),
        'state_pool': nrm(ks[6], (L, DEC_BATCH, POOL_MAX - 1, POOL_WIDTH), 1.0),
        'w_ada': nrm(ks[7], (L, D_MODEL, 6 * D_MODEL), 0.5 * D_MODEL ** -0.5),
        'b_ada': nrm(ks[8], (L, 6 * D_MODEL), 0.01),
        'norm1_g': 1.0 + nrm(ks[9], (L, D_MODEL), 0.02),
        'w_in': nrm(ks[10], (L, D_MODEL, IN_WIDTH), D_MODEL ** -0.5),
        'conv_w': nrm(ks[11], (L, CONV_W, CONV_CH), CONV_W ** -0.5),
        'a_log': jnp.log(jax.random.uniform(ks[12], (L, N_HEADS_A), f32, 1.0, 16.0)),
        'dt_bias': dt_bias,
        'o_norm_g': 1.0 + nrm(ks[14], (L, HEAD_DV), 0.02),
        'pool_w': nrm(ks[15], (L, N_POOL_GROUPS, POOL_GROUP, POOL_GROUP), POOL_GROUP ** -0.5),
        'pool_scale': 1.0 + nrm(ks[16], (L, POOL_WIDTH), 0.1),
        'w_proj_a': nrm(ks[17], (L, V_WIDTH, D_MODEL), V_WIDTH ** -0.5),
        'w_proj_b': nrm(ks[18], (L, POOL_WIDTH, D_MODEL), POOL_WIDTH ** -0.5),
        'w_out': nrm(ks[19], (L, D_MODEL, D_MODEL), D_MODEL ** -0.5),
        'norm2_g': 1.0 + nrm(ks[20], (L, D_MODEL), 0.02),
        'w_gate_up': nrm(ks[21], (L, D_MODEL, 2 * D_FF), D_MODEL ** -0.5),
        'w_down': nrm(ks[22], (L, D_FF, D_MODEL), D_FF ** -0.5),
        'final_g': 1.0 + nrm(ks[23], (D_MODEL,), 0.02),
    }


def reference(x_prompt, x_sample, c_prompt, c_sample, state_delta, state_conv, state_pool, w_ada, b_ada, norm1_g, w_in, conv_w, a_log, dt_bias, o_norm_g, pool_w, pool_scale, w_proj_a, w_proj_b, w_out, norm2_g, w_gate_up, w_down, final_g):
    xp, xs = x_prompt, x_sample
    dp, cp, pp, ds, cs, ps = [], [], [], [], [], []
    for l in range(DEPTH):
        lw = (w_ada[l], b_ada[l], norm1_g[l], w_in[l], conv_w[l], a_log[l], dt_bias[l], o_norm_g[l], pool_w[l], pool_scale[l], w_proj_a[l], w_proj_b[l], w_out[l], norm2_g[l], w_gate_up[l], w_down[l])
        s0_p = jnp.zeros((BATCH, N_HEADS_A, HEAD_DK, HEAD_DV), state_delta.dtype)
        conv0_p = jnp.zeros((BATCH, CONV_W - 1, CONV_CH), xp.dtype)
        pool0_p = jnp.zeros((BATCH, POOL_MAX - 1, POOL_WIDTH), xp.dtype)
        xp, s_p, c_p, p_p = decoder_layer(xp, c_prompt, s0_p, conv0_p, pool0_p, 0, *lw)
        xs, s_s, c_s, p_s = decoder_layer(xs, c_sample, state_delta[l], state_conv[l], state_pool[l], PAST_LEN, *lw)
        dp.append(s_p); cp.append(c_p); pp.append(p_p)
        ds.append(s_s); cs.append(c_s); ps.append(p_s)
    y_prompt = rms_norm(xp, final_g)
    y_sample = rms_norm(xs, final_g)
    return (y_prompt, y_sample, jnp.stack(dp), jnp.stack(cp), jnp.stack(pp), jnp.stack(ds), jnp.stack(cs), jnp.stack(ps))
```

```python
import numpy as np
from contextlib import ExitStack
import concourse.bass as bass
import concourse.mybir as mybir
from concourse.bass_utils import run_bass_kernel_spmd

F32 = mybir.dt.float32
BF16 = mybir.dt.bfloat16
I32 = mybir.dt.int32
F32R = mybir.dt.float32r
NEUMANN_F32R = False
AF = mybir.ActivationFunctionType
ALU = mybir.AluOpType
AX = mybir.AxisListType

D = 2048
NH = 8
DFF = 5632
QOFF, KOFF, VOFF, ZOFF, AOFF, BOFF, POFF, GAOFF, GBOFF = 0, 1024, 2048, 3072, 4096, 4104, 4112, 5136, 7184
INW = 9232
EPS = 1e-6
NDS = 24
SAME_ENGINE_SYNC = True
KUNITS = 2
KUNITS_A = 5
RAW_ONLY_SELF = False


class Buf:
    _n = 0

    def __init__(self, ap, nslots=1, name=""):
        self.ap = ap if type(ap).__name__ == 'AP' else ap[:]
        self.n = nslots
        self.name = name
        Buf._n += 1
        self.id = Buf._n

    def __getitem__(self, k):
        return self.ap[k]


class Sched:
    def __init__(self, nc, es):
        self.nc = nc
        self.es = es
        self.E = {"pe": nc.tensor, "act": nc.scalar, "dve": nc.vector, "pool": nc.gpsimd, "sp": nc.sync}
        self.sem = {e: es.enter_context(nc.semaphore("S_" + e)) for e in ["pe", "act", "dve", "pool"]}
        self.cnt = {e: 0 for e in self.sem}
        self.dsem = [es.enter_context(nc.semaphore(f"D{i}")) for i in range(NDS)]
        self.dval = [0] * NDS
        self.dnext = 0
        self.dq = {}
        self.seen = {e: {} for e in self.E}
        self.lastw = {}
        self.readers = {}
        self.out_toks = []
        self.nops = 0
        self.off = False

    def sb(self, name, shape, dt, nslots=1, es=None):
        self._uid = getattr(self, "_uid", 0) + 1
        name = f"{name}_{self._uid}"
        return Buf((es or self.es).enter_context(self.nc.sbuf_tensor(name, shape, dt)), nslots, name)

    def barrier(self):
        if self.off:
            return
        for e in self.E:
            for j in range(NDS):
                if self.dval[j] > 0:
                    self._wait(e, ("dma", j, self.dval[j]))
            for e2 in self.sem:
                if self.cnt[e2] > 0 and e2 != e:
                    self._wait(e, ("eng", e2, self.cnt[e2]))

    def _keys(self, acc):
        ks = []
        for a in acc:
            if isinstance(a, Buf):
                a = (a, None)
            b, s = a
            if s is None:
                ks.extend((b.id, i) for i in range(b.n))
            elif isinstance(s, (list, tuple, range)):
                ks.extend((b.id, i) for i in s)
            else:
                ks.append((b.id, s))
        return ks

    def _wait(self, e, tok):
        kind, key, val = tok
        if self.seen[e].get((kind, key), 0) >= val:
            return
        sem = self.sem[key] if kind == "eng" else self.dsem[key]
        self.E[e].wait_ge(sem, val)
        self.seen[e][(kind, key)] = val

    def _deps(self, e, reads, writes):
        rk, wk = self._keys(reads), self._keys(writes)
        deps = set()
        deps2 = set()
        for k in rk:
            if k in self.lastw:
                deps.add(self.lastw[k])
        for k in wk:
            if k in self.lastw:
                deps2.add(self.lastw[k])
            for t in self.readers.get(k, ()):
                deps2.add(t)
        for t in sorted(deps | deps2, key=lambda t: (t[0], str(t[1]), t[2])):
            if t[0] == "eng" and t[1] == e:
                if e == "pe" or not SAME_ENGINE_SYNC or (RAW_ONLY_SELF and t not in deps):
                    continue
            self._wait(e, t)
        return rk, wk

    def _commit(self, tok, rk, wk):
        for k in rk:
            self.readers.setdefault(k, []).append(tok)
        for k in wk:
            self.lastw[k] = tok
            self.readers[k] = []

    def op(self, e, fn, reads=(), writes=()):
        if self.off:
            return
        rk, wk = self._deps(e, reads, writes)
        inst = fn(self.E[e])
        self.cnt[e] += 1
        inst.then_inc(self.sem[e], 1)
        self._commit(("eng", e, self.cnt[e]), rk, wk)
        self.nops += 1

    def dma(self, q, out, in_, reads=(), writes=(), is_out=False, **kw):
        if self.off:
            return
        rk, wk = self._deps(q, reads, writes)
        half = NDS // 2
        base = 0 if q == "pool" else half
        cur = self.dq.get(q, 0)
        j = base + cur
        self.dq[q] = (cur + 1) % half
        if self.dval[j] > 0:
            self._wait(q, ("dma", j, self.dval[j]))
        inst = self.E[q].dma_start(out=out, in_=in_, **kw)
        self.dval[j] += 16
        inst.then_inc(self.dsem[j], 16)
        tok = ("dma", j, self.dval[j])
        self._commit(tok, rk, wk)
        if is_out:
            self.out_toks.append(tok)
        self.nops += 1

    def finish(self):
        for j in range(NDS):
            if self.dval[j] > 0:
                self._wait("sp", ("dma", j, self.dval[j]))
        for e in self.sem:
            if self.cnt[e] > 0:
                self._wait("sp", ("eng", e, self.cnt[e]))


class Ring:
    def __init__(self, sch, name, shape, dt, n, es=None):
        self.b = sch.sb(name, [shape[0], n] + list(shape[1:]), dt, nslots=n, es=es)
        self.n = n
        self.i = 0

    def next(self):
        s = self.i % self.n
        self.i += 1
        return s


def tok_tiles(n, step=512):
    return [(a, min(a + step, n)) for a in range(0, n, step)]


def build_program(debug=None):
    nc = bass.Bass("TRN2", target_bir_lowering=False)
    dbg = {}

    def din(name, shape, dt=F32):
        return nc.dram_tensor(name, list(shape), dt, kind="ExternalInput").ap()

    def dout(name, shape, dt=F32):
        return nc.dram_tensor(name, list(shape), dt, kind="ExternalOutput").ap()

    xpre = din("xpre", [1024, D])
    xown = din("xown", [1024, D])
    xsm = din("xsm", [64, D])
    cc = din("cc", [17, D])
    sdelta = din("sdelta", [16, NH, 128, 128])
    sconv = din("sconv", [48, 3072])
    spool = din("spool", [240, 1024])
    flag = din("flag", [128, 1])
    pos0 = din("pos0", [128, 1])
    w_ada = din("w_ada", [D, 6 * D])
    b_ada = din("b_ada", [96, 128])
    norm1_g = din("norm1_g", [16, 128])
    w_in = din("w_in", [D, INW])
    conv_w = din("conv_w", [96, 128])
    a_log = din("a_log", [1, 8])
    dt_bias = din("dt_bias", [1, 8])
    o_norm_g = din("o_norm_g", [1, 128])
    pool_w = din("pool_w", [4, 256, 256])
    pool_scale = din("pool_scale", [8, 128])
    w_proj_a = din("w_proj_a", [1024, D])
    w_proj_b = din("w_proj_b", [1024, D])
    w_out = din("w_out", [D, D])
    norm2_g = din("norm2_g", [16, 128])
    w_gate_up = din("w_gate_up", [D, 2 * DFF])
    w_down = din("w_down", [DFF, D])
    final_g = din("final_g", [16, 128])

    y_own = dout("y_own", [1024, D])
    y_sm = dout("y_sm", [64, D])
    o_delta_p = dout("o_delta_p", [NH, 128, 128])
    o_conv = dout("o_conv", [51, 3072])
    o_pool = dout("o_pool", [255, 1024])
    o_delta_s = dout("o_delta_s", [16, NH, 128, 128])

    with ExitStack() as es:
        S = Sched(nc, es)
        E = S.E

        def dbg_out(name, buf, shape, dt=F32, ap=None):
            if debug is None or name not in debug:
                return
            t = dout("dbg_" + name, shape, dt)
            S.dma("sp", t, ap if ap is not None else buf.ap, reads=[buf], is_out=True)
            dbg[name] = shape

        PS = Buf(es.enter_context(nc.psum_tensor("PS", [128, 8, 512], F32)), 8, "PS")
        ps_i = [0]
        psb_i = [0]

        def ps_next():
            s = ps_i[0] % 8
            ps_i[0] += 1
            return s

        def ps_next():
            s = psb_i[0] % 8
            psb_i[0] += 1
            return s

        C = S.sb("consts_f", [128, 8, 128], F32, nslots=8)
        CB = S.sb("consts_b", [128, 2, 128], BF16, nslots=2)
        IDF, ONESF, TRI, STRICT, BTRI, BSTRICT, BLK = 0, 1, 2, 3, 5, 6, 7

        def mk_const(slot, fn):
            S.op("pool", fn, writes=[(C, slot)])

        mk_const(ONESF, lambda e: e.memset(C[:, ONESF, :], 1.0))
        for slot, cmp_, base in ((IDF, ALU.is_equal, 0), (TRI, ALU.is_ge, 0), (STRICT, ALU.is_gt, 0)):
            S.op("pool", lambda e, slot=slot: e.memset(C[:, slot, :], 1.0), writes=[(C, slot)])
            if slot == STRICT:
                S.op("pool", lambda e, slot=slot, cmp_=cmp_: e.affine_select(
                    out=C[:, slot, :], in_=C[:, slot, :], pattern=[[-1, 128]], compare_op=cmp_, fill=0.0,
                    base=0, channel_multiplier=1), reads=[(C, slot)], writes=[(C, slot)])
            else:
                S.op("pool", lambda e, slot=slot, cmp_=cmp_: e.affine_select(
                    out=C[:, slot, :], in_=C[:, slot, :], pattern=[[1, 128]], compare_op=cmp_, fill=0.0,
                    base=0, channel_multiplier=-1), reads=[(C, slot)], writes=[(C, slot)])
        blk_i = S.sb("blk_i", [128, 2, 128], I32, nslots=2)
        S.op("pool", lambda e: e.iota(blk_i[:, 0, :], pattern=[[1, 128]], base=0, channel_multiplier=0),
             writes=[(blk_i, 0)])
        S.op("pool", lambda e: e.iota(blk_i[:, 1, :], pattern=[[0, 128]], base=0, channel_multiplier=1),
             writes=[(blk_i, 1)])
        S.op("dve", lambda e: e.tensor_single_scalar(out=blk_i[:, 0, :], in_=blk_i[:, 0, :], scalar=2,
                                                      op=ALU.arith_shift_right), reads=[(blk_i, 0)], writes=[(blk_i, 0)])
        S.op("dve", lambda e: e.tensor_single_scalar(out=blk_i[:, 1, :], in_=blk_i[:, 1, :], scalar=2,
                                                      op=ALU.arith_shift_right), reads=[(blk_i, 1)], writes=[(blk_i, 1)])
        S.op("dve", lambda e: e.tensor_tensor(out=blk_i[:, 0, :], in0=blk_i[:, 0, :], in1=blk_i[:, 1, :],
                                               op=ALU.is_equal), reads=[blk_i], writes=[(blk_i, 0)])
        S.op("dve", lambda e: e.tensor_copy(out=C[:, BLK, :], in_=blk_i[:, 0, :]), reads=[(blk_i, 0)],
             writes=[(C, BLK)])
        S.op("pool", lambda e: e.tensor_tensor(out=C[:, BTRI, :], in0=C[:, TRI, :], in1=C[:, BLK, :], op=ALU.mult),
             reads=[(C, TRI), (C, BLK)], writes=[(C, BTRI)])
        S.op("pool", lambda e: e.tensor_tensor(out=C[:, BSTRICT, :], in0=C[:, STRICT, :], in1=C[:, BLK, :],
                                               op=ALU.mult), reads=[(C, STRICT), (C, BLK)], writes=[(C, BSTRICT)])
        S.op("pool", lambda e: e.tensor_copy(out=CB[:, 0, :], in_=C[:, IDF, :]), reads=[(C, IDF)], writes=[(CB, 0)])
        S.op("pool", lambda e: e.memset(CB[:, 1, :], 1.0), writes=[(CB, 1)])
        dbg_out("consts", C, [128, 8, 128])

        class _Stop(Exception):
            pass

        def ckpt(k):
            if debug is not None and f'stop{k}' in debug:
                S.off = True

        def body():
            def mm_group(slot, out_ap, terms, reads, psbuf=PS):
                def fn(e):
                    n_ = len(terms)
                    ins = None
                    for i, (l, r) in enumerate(terms):
                        ins = e.matmul(out_ap, lhsT=l, rhs=r, start=(i == 0), stop=(i == n_ - 1))
                    return ins
                S.op("pe", fn, reads=reads, writes=[(psbuf, slot)])

            def transpose_to(psbuf, slot, out_ap, in_ap, ident_ap, reads):
                S.op("pe", lambda e: e.matmul(out_ap, lhsT=in_ap, rhs=ident_ap, start=True, stop=True), reads=reads,
                     writes=[(psbuf, slot)])

            def load_vecT(name, dram, n):
                tmp = S.sb(name + "_tm", [n, 128], F32)
                dst = S.sb(name, [128, n], F32)
                S.dma("sp", tmp.ap, dram, writes=[tmp])
                sl = ps_next()
                transpose_to(PS, sl, PS[:, sl, 0:n], tmp.ap, C[0:n, IDF, 0:n], reads=[tmp, (C, IDF)])
                S.op("act", lambda e: e.copy(out=dst.ap, in_=PS[:, sl, 0:n]), reads=[(PS, sl)], writes=[dst])
                return dst

            def wsrc(dram_ap):
                return dram_ap.rearrange("(k p) c -> p k c", p=128)

            WR = Ring(S, "wr", [128, 16, 128], BF16, 4)

            def load_w(dram_ap, kc, ncols=128, ring=None):
                ring = ring or WR
                s_ = ring.next()
                src = wsrc(dram_ap)
                for k0 in range(0, kc, 4):
                    k1 = min(kc, k0 + 4)
                    S.dma("pool", ring.b[:, s_, k0:k1, 0:ncols], src[:, k0:k1, :], writes=[(ring.b, s_)])
                return s_

            b_adaT = load_vecT("b_adaT", b_ada, 96)
            g1T = load_vecT("g1T", norm1_g, 16)
            g2T = load_vecT("g2T", norm2_g, 16)
            fgT = load_vecT("fgT", final_g, 16)
            pscT = load_vecT("pscT", pool_scale, 8)
            cwT = load_vecT("cwT", conv_w, 96)
            flag_t = S.sb("flag_t", [128, 1], F32)
            S.dma("sp", flag_t.ap, flag, writes=[flag_t])
            pos0_t = S.sb("pos0_t", [128, 1], F32)
            S.dma("sp", pos0_t.ap, pos0, writes=[pos0_t])
            nea_t = S.sb("nea_t", [128, 8], F32)
            S.dma("sp", nea_t.ap, a_log[0:1, :].broadcast_to([128, 8]), writes=[nea_t])
            dtb_t = S.sb("dtb_t", [128, 8], F32)
            S.dma("sp", dtb_t.ap, dt_bias[0:1, :].broadcast_to([128, 8]), writes=[dtb_t])
            ong_t = S.sb("ong_t", [128, 128], F32)
            S.dma("sp", ong_t.ap, o_norm_g[0:1, :].broadcast_to([128, 128]), writes=[ong_t])
            S.op("act", lambda e: e.activation(out=nea_t.ap, in_=nea_t.ap, func=AF.Exp), reads=[nea_t], writes=[nea_t])
            S.op("dve", lambda e: e.tensor_scalar_mul(out=nea_t.ap, in0=nea_t.ap, scalar1=-1.0), reads=[nea_t], writes=[nea_t])

            ckpt(1)
            CM = S.sb("CM", [128, 16, 64], BF16)
            RM = S.sb("RM", [128, 16], BF16)
            scM = ExitStack()
            scM.__enter__()
            CMf = S.sb("CMf", [128, 16, 64], F32, es=scM)
            RMf = S.sb("RMf", [128, 16], F32, es=scM)
            S.op("pool", lambda e: e.memset(CMf.ap, 1.0), writes=[CMf])
            S.op("pool", lambda e: e.affine_select(out=CMf.ap, in_=CMf.ap, pattern=[[-4, 16], [1, 64]], compare_op=ALU.is_ge,
                                                   fill=0.0, base=0, channel_multiplier=0), reads=[CMf], writes=[CMf])
            S.op("pool", lambda e: e.affine_select(out=CMf.ap, in_=CMf.ap, pattern=[[4, 16], [-1, 64]], compare_op=ALU.is_ge,
                                                   fill=0.0, base=3, channel_multiplier=0), reads=[CMf], writes=[CMf])
            S.op("pool", lambda e: e.memset(RMf.ap, 1.0), writes=[RMf])
            S.op("pool", lambda e: e.affine_select(out=RMf.ap, in_=RMf.ap, pattern=[[-4, 16]], compare_op=ALU.is_ge,
                                                   fill=0.0, base=0, channel_multiplier=1), reads=[RMf], writes=[RMf])
            S.op("pool", lambda e: e.affine_select(out=RMf.ap, in_=RMf.ap, pattern=[[4, 16]], compare_op=ALU.is_ge,
                                                   fill=0.0, base=3, channel_multiplier=-1), reads=[RMf], writes=[RMf])
            S.op("pool", lambda e: e.tensor_copy(out=CM.ap, in_=CMf.ap), reads=[CMf], writes=[CM])
            S.op("pool", lambda e: e.tensor_copy(out=RM.ap, in_=RMf.ap), reads=[RMf], writes=[RM])
            S.barrier()
            scM.close()

            ckpt(2)
            modT = S.sb("modT", [128, 96, 17], F32)
            GG = S.sb("GG", [128, 2, 16, 17], F32, nslots=2)
            FV = S.sb("FV", [128, 2, 16], F32, nslots=2)
            with ExitStack() as sc:
                cc_t = S.sb("cc_t", [17, D], F32, es=sc)
                cc_b = S.sb("cc_b", [17, D], BF16, es=sc)
                cT = S.sb("cT", [128, 16, 17], BF16, es=sc)
                mtok = S.sb("mtok", [17, 2, 512], F32, nslots=2, es=sc)
                WA = Ring(S, "wa", [128, 16, 512], BF16, 3, es=sc)
                WS32 = Ring(S, "ws32", [128, 16, 512], F32, 2, es=sc)
                S.dma("sp", cc_t.ap, cc, writes=[cc_t])
                S.op("act", lambda e: e.activation(out=cc_b.ap, in_=cc_t.ap, func=AF.Silu), reads=[cc_t], writes=[cc_b])
                ckpt(20)
                for k in range(16):
                    if k == 2:
                        ckpt(202)
                    if k == 9:
                        ckpt(209)
                    sl = ps_next()
                    transpose_to(PS, sl, PS[:, sl, 0:17], cc_b[:, k * 128:(k + 1) * 128], CB[0:17, 0, 0:17],
                                 reads=[cc_b, (CB, 0)])
                    S.op("dve", lambda e, sl=sl, k=k: e.tensor_copy(out=cT[:, k, :], in_=PS[:, sl, 0:17]),
                         reads=[(PS, sl)], writes=[cT])
                ckpt(21)
                for blk in range(24):
                    if blk == 1:
                        ckpt(25)
                    if blk == 3:
                        ckpt(26)
                    if blk == 8:
                        ckpt(27)
                    if blk % 2 == 0:
                        ws = load_w(w_ada[:, blk * 512:(blk + 1) * 512], 16, 512, ring=WA)
                    else:
                        ws = WA.next()
                        s32 = WS32.next()
                        src = wsrc(w_ada[:, blk * 512:(blk + 1) * 512])
                        for k0 in range(0, 16, 4):
                            S.dma("sp", WS32.b[:, s32, k0:k0 + 4, :], src[:, k0:k0 + 4, :], writes=[(WS32.b, s32)])
                        S.op("dve", lambda e, ws=ws, s32=s32: e.tensor_copy(out=WA.b[:, ws, 0:8, :], in_=WS32.b[:, s32, 0:8, :]),
                             reads=[(WS32.b, s32)], writes=[(WA.b, ws)])
                        S.op("act", lambda e, ws=ws, s32=s32: e.copy(out=WA.b[:, ws, 8:16, :], in_=WS32.b[:, s32, 8:16, :]),
                             reads=[(WS32.b, s32)], writes=[(WA.b, ws)])
                    sl = ps_next()
                    mm_group(sl, PS[0:17, sl, :], [(cT[:, k, :], WA.b[:, ws, k, :]) for k in range(16)],
                             reads=[cT, (WA.b, ws)])
                    ms = blk % 2
                    ckpt(22)
                    S.op("act", lambda e, sl=sl, ms=ms: e.copy(out=mtok[:, ms, :], in_=PS[0:17, sl, :]),
                         reads=[(PS, sl)], writes=[(mtok, ms)])
                    ckpt(23)
                    sl2 = ps_next()

                    def tf(e, sl2=sl2, ms=ms):
                        ins = None
                        for j in range(4):
                            ins = e.matmul(PS[:, sl2, j * 32:j * 32 + 17], lhsT=mtok[:, ms, j * 128:(j + 1) * 128],
                                           rhs=C[0:17, IDF, 0:17], start=True, stop=True)
                        return ins
                    S.op("pe", tf, reads=[(mtok, ms), (C, IDF)], writes=[(PS, sl2)])
                    ckpt(24)
                    S.op("dve", lambda e, sl2=sl2, blk=blk: e.tensor_tensor(
                        out=modT[:, blk * 4:(blk + 1) * 4, :],
                        in0=PS[:, sl2, 0:128].rearrange("p (j c) -> p j c", c=32)[:, :, 0:17],
                        in1=b_adaT[:, blk * 4:(blk + 1) * 4].unsqueeze(2).broadcast_to([128, 4, 17]), op=ALU.add),
                        reads=[(PS, sl2), b_adaT], writes=[modT])
                ckpt(28)
                for i, (gT, pc) in enumerate(((g1T, 1), (g2T, 4))):
                    S.op("dve", lambda e, i=i, gT=gT, pc=pc: e.scalar_tensor_tensor(
                        out=GG[:, i], in0=modT[:, pc * 16:(pc + 1) * 16, :], scalar=1.0,
                        in1=gT.ap.unsqueeze(2).broadcast_to([128, 16, 17]), op0=ALU.add, op1=ALU.mult),
                        reads=[modT, gT], writes=[(GG, i)])
                S.op("dve", lambda e: e.tensor_scalar_mul(out=FV[:, 0, :], in0=GG[:, 0, :, 0], scalar1=flag_t[:, 0:1]),
                     reads=[(GG, 0), flag_t], writes=[(FV, 0)])
                S.op("dve", lambda e: e.tensor_scalar_mul(out=FV[:, 1, :], in0=modT[:, 0:16, 0], scalar1=flag_t[:, 0:1]),
                     reads=[modT, flag_t], writes=[(FV, 1)])
                ckpt(29)
                dbg_out("modT", modT, [128, 96, 17])
                S.barrier()

            ckpt(3)
            def make_u(xsrc, ntok, uT, col0, kind, xt, xb, st, xr):
                s_ = xr.next()
                S.dma("sp", xt[0:ntok, s_, :], xsrc, writes=[(xt, s_)])
                S.op("dve", lambda e: e.memset(st[:, s_, :], 0.0), writes=[(st, s_)])
                S.op("act", lambda e: e.activation(out=xb[0:ntok, s_, :], in_=xt[0:ntok, s_, :], func=AF.Square,
                                                   accum_out=st[0:ntok, s_, 0:1]), reads=[(xt, s_), (st, s_)],
                     writes=[(xb, s_), (st, s_)])
                S.op("dve", lambda e: e.tensor_scalar(out=st[0:ntok, s_, 1:2], in0=st[0:ntok, s_, 0:1], scalar1=1.0 / D,
                                                      scalar2=EPS, op0=ALU.mult, op1=ALU.add), reads=[(st, s_)],
                     writes=[(st, s_)])
                S.op("act", lambda e: e.activation(out=st[0:ntok, s_, 1:2], in_=st[0:ntok, s_, 1:2], func=AF.Ln),
                     reads=[(st, s_)], writes=[(st, s_)])
                S.op("act", lambda e: e.activation(out=st[0:ntok, s_, 1:2], in_=st[0:ntok, s_, 1:2], func=AF.Exp, scale=-0.5),
                     reads=[(st, s_)], writes=[(st, s_)])
                S.op("act", lambda e: e.activation(out=xb[0:ntok, s_, :], in_=xt[0:ntok, s_, :], func=AF.Copy,
                                                   scale=st[0:ntok, s_, 1:2]), reads=[(xt, s_), (st, s_)],
                     writes=[(xb, s_)])
                for k in range(16):
                    sl = ps_next()
                    transpose_to(PS, sl, PS[:, sl, 0:ntok], xb[0:ntok, s_, k * 128:(k + 1) * 128],
                                 CB[0:ntok, 0, 0:ntok], reads=[(xb, s_), (CB, 0)])
                    dst = uT[:, k, col0:col0 + ntok]
                    if kind == "own":
                        S.op("act", lambda e, sl=sl, k=k, dst=dst: e.activation(
                            out=dst, in_=PS[:, sl, 0:ntok], func=AF.Identity, scale=GG[:, 0, k, 0:1],
                            bias=modT[:, k, 0:1]), reads=[(PS, sl), (GG, 0), modT], writes=[(uT, k)])
                    elif kind == "pre":
                        S.op("act", lambda e, sl=sl, k=k, dst=dst: e.activation(
                            out=dst, in_=PS[:, sl, 0:ntok], func=AF.Identity, scale=FV[:, 0, k:k + 1],
                            bias=FV[:, 1, k:k + 1]), reads=[(PS, sl), FV], writes=[(uT, k)])
                    else:
                        dv = dst.rearrange("p (b t) -> p b t", t=4)
                        S.op("dve", lambda e, sl=sl, k=k, dv=dv: e.tensor_tensor(
                            out=dv, in0=PS[:, sl, 0:64].rearrange("p (b t) -> p b t", t=4),
                            in1=GG[:, 0, k, 1:17].unsqueeze(2).broadcast_to([128, 16, 4]), op=ALU.mult),
                            reads=[(PS, sl), (GG, 0)], writes=[(uT, k)])
                        S.op("dve", lambda e, k=k, dv=dv: e.tensor_tensor(
                            out=dv, in0=dv, in1=modT[:, k, 1:17].unsqueeze(2).broadcast_to([128, 16, 4]), op=ALU.add),
                            reads=[(uT, k), modT], writes=[(uT, k)])

            ogT = S.sb("ogT", [128, 8, 1088], BF16, nslots=8)
            ypsT = S.sb("ypsT", [128, 8, 1088], BF16, nslots=8)
            scD = ExitStack()
            scD.__enter__()
            Sst = S.sb("Sst", [128, NH, 128], F32, nslots=NH, es=scD)
            Sb = S.sb("Sb", [128, NH, 128], BF16, nslots=NH, es=scD)
            S.op("dve", lambda e: e.memset(Sst.ap, 0.0), writes=[Sst])
            S.op("dve", lambda e: e.memset(Sb.ap, 0.0), writes=[Sb])
            TF = TB = TN = SST = None
            WAB = S.sb("wab", [128, 16, 16], BF16, es=scD)
            S.dma("pool", WAB.ap, wsrc(w_in[:, AOFF:AOFF + 16]), writes=[WAB])

            def decay_prep(uT, tiles, sc):
                nt = len(tiles)
                DP = S.sb("dp", [128, 8, nt, 8], F32, nslots=8, es=sc)
                ab = S.sb("ab", [128, nt, 16], F32, es=sc)
                S.op("dve", lambda e: e.memset(DP.ap, 0.0), writes=[DP])
                S.op("dve", lambda e: e.memset(ab.ap, 0.0), writes=[ab])
                sl = ps_next()

                def fn(e):
                    ins = None
                    for ti, (c0, n, smp) in enumerate(tiles):
                        for k in range(16):
                            ins = e.matmul(PS[0:n, sl, ti * 16:(ti + 1) * 16], lhsT=uT[:, k, c0:c0 + n], rhs=WAB[:, k, :],
                                           start=(k == 0), stop=(k == 15))
                    return ins
                S.op("pe", fn, reads=[uT, WAB], writes=[(PS, sl)])
                for ti, (c0, n, smp) in enumerate(tiles):
                    S.op("act", lambda e, ti=ti, n=n: e.copy(out=ab[0:n, ti, :], in_=PS[0:n, sl, ti * 16:(ti + 1) * 16]),
                         reads=[(PS, sl)], writes=[ab])
                bc = lambda t: t.ap.unsqueeze(1).broadcast_to([128, nt, 8])
                S.op("dve", lambda e: e.tensor_tensor(out=DP[:, 0], in0=ab[:, :, 0:8], in1=bc(dtb_t), op=ALU.add),
                     reads=[ab, dtb_t], writes=[(DP, 0)])
                S.op("act", lambda e: e.activation(out=DP[:, 0], in_=DP[:, 0], func=AF.Exp), reads=[(DP, 0)], writes=[(DP, 0)])
                S.op("act", lambda e: e.activation(out=DP[:, 0], in_=DP[:, 0], func=AF.Ln, bias=1.0), reads=[(DP, 0)],
                     writes=[(DP, 0)])
                S.op("dve", lambda e: e.tensor_tensor(out=DP[:, 0], in0=DP[:, 0], in1=bc(nea_t), op=ALU.mult),
                     reads=[(DP, 0), nea_t], writes=[(DP, 0)])
                S.op("act", lambda e: e.activation(out=DP[:, 1], in_=ab[:, :, 8:16], func=AF.Sigmoid), reads=[ab],
                     writes=[(DP, 1)])
                S.op("dve", lambda e: e.tensor_scalar_mul(out=DP[:, 2], in0=DP[:, 1], scalar1=-1.0), reads=[(DP, 1)],
                     writes=[(DP, 2)])
                sl2 = ps_next()

                def fn2(e):
                    ins = None
                    for ti, (c0, n, smp) in enumerate(tiles):
                        e.matmul(PS[0:n, sl2, ti * 8:(ti + 1) * 8], lhsT=C[0:n, BTRI if smp else TRI, 0:n],
                                 rhs=DP[0:n, 0, ti, :], start=True, stop=True)
                        ins = e.matmul(PS[0:n, sl2, 256 + ti * 8:256 + (ti + 1) * 8], lhsT=C[0:n, BLK if smp else ONESF, 0:n],
                                       rhs=DP[0:n, 0, ti, :], start=True, stop=True)
                    return ins
                S.op("pe", fn2, reads=[(DP, 0), C], writes=[(PS, sl2)])
                for ti, (c0, n, smp) in enumerate(tiles):
                    S.op("act", lambda e, ti=ti, n=n: e.copy(out=DP[0:n, 3, ti, :], in_=PS[0:n, sl2, ti * 8:(ti + 1) * 8]),
                         reads=[(PS, sl2)], writes=[(DP, 3)])
                    S.op("act", lambda e, ti=ti, n=n: e.copy(out=DP[0:n, 4, ti, :],
                                                             in_=PS[0:n, sl2, 256 + ti * 8:256 + (ti + 1) * 8]),
                         reads=[(PS, sl2)], writes=[(DP, 4)])
                S.op("dve", lambda e: e.tensor_tensor(out=DP[:, 5], in0=DP[:, 4], in1=DP[:, 3], op=ALU.subtract),
                     reads=[(DP, 3), (DP, 4)], writes=[(DP, 5)])
                S.op("act", lambda e: e.activation(out=DP[:, 5], in_=DP[:, 5], func=AF.Exp), reads=[(DP, 5)], writes=[(DP, 5)])
                S.op("act", lambda e: e.activation(out=DP[:, 6], in_=DP[:, 4], func=AF.Exp), reads=[(DP, 4)], writes=[(DP, 6)])
                S.op("act", lambda e: e.activation(out=DP[:, 7], in_=DP[:, 3], func=AF.Exp), reads=[(DP, 3)], writes=[(DP, 7)])
                S.op("dve", lambda e: e.tensor_tensor(out=DP[:, 7], in0=DP[:, 7], in1=DP[:, 1], op=ALU.mult),
                     reads=[(DP, 7), (DP, 1)], writes=[(DP, 7)])
                return DP

            cur_mode = ['A']
            turn = [0]

            def unit(ui, DP, h, ti, n, smp, kT_c, qT_c, ktok_c, vtok_c, do_o, kreads, opost, sm):
                TRIc, STRc = (BTRI, BSTRICT) if smp else (TRI, STRICT)
                col = lambda j: DP[0:n, j, ti, h:h + 1]
                tf_ = lambda s_: TF.b[0:n, s_, 0:n]
                tb_ = lambda s_: TB.b[0:n, s_, 0:n]
                tnf_ = lambda s_: TN.b[0:n, s_, 0:n]
                tn_ = (lambda s_: TN.b[0:n, s_, 0:n].bitcast(F32R)) if NEUMANN_F32R else tnf_
                tr_ = tn_
                f1 = ui * 6
                S.op("dve", lambda e: e.tensor_scalar_mul(out=tf_(f1), in0=C[0:n, TRIc, 0:n], scalar1=col(0)),
                     reads=[C, (DP, 0)], writes=[(TF.b, f1)])
                sld = ps_next()
                mm_group(sld, PS[:, sld, 0:n], [(C[0:n, ONESF, :], tf_(f1))], reads=[C, (TF.b, f1)])
                f2, f3 = ui * 6 + 1, ui * 6 + 2
                S.op("dve", lambda e: e.tensor_scalar(out=tf_(f2), in0=PS[0:n, sld, 0:n], scalar1=col(3), scalar2=0.0,
                                                      op0=ALU.subtract, op1=ALU.max), reads=[(PS, sld), (DP, 3)],
                     writes=[(TF.b, f2)])
                S.op("dve", lambda e: e.tensor_scalar(out=tf_(f3), in0=PS[0:n, sld, 0:n], scalar1=col(3), scalar2=0.0,
                                                      op0=ALU.subtract, op1=ALU.min), reads=[(PS, sld), (DP, 3)],
                     writes=[(TF.b, f3)])
                if debug is not None and 'trace' in debug and cur_mode[0] == 'B' and h == 0:
                    print("UNIT", ti, "sld", sld, "f1,f2,f3", f1, f2, f3, "cnt", dict(S.cnt), flush=True)
                fed = None
                if do_o:
                    fed = ui * 6 + 3
                    S.op("dve", lambda e: e.tensor_copy(out=TF.b[:, fed, 0:n], in_=PS[:, sld, 0:n]),
                         reads=[(PS, sld)], writes=[(TF.b, fed)])
                    S.op("act", lambda e: e.activation(out=TF.b[:, fed, 0:n], in_=TF.b[:, fed, 0:n], func=AF.Exp),
                         reads=[(TF.b, fed)], writes=[(TF.b, fed)])
                yield
                if smp:
                    S.op("dve", lambda e: e.tensor_copy(
                        out=sm["cds"].ap, in_=TF.b[:, fed, 0:64].rearrange("p (b t) -> p b t", t=4)[:, :, 3]),
                        reads=[(TF.b, fed)], writes=[sm["cds"]])
                S.op("act", lambda e: e.activation(out=tf_(f2), in_=tf_(f2), func=AF.Exp, scale=-1.0), reads=[(TF.b, f2)],
                     writes=[(TF.b, f2)])
                S.op("act", lambda e: e.activation(out=tf_(f3), in_=tf_(f3), func=AF.Exp), reads=[(TF.b, f3)],
                     writes=[(TF.b, f3)])
                S.op("dve", lambda e: e.tensor_tensor(out=tf_(f2), in0=tf_(f2), in1=C[0:n, STRc, 0:n], op=ALU.mult),
                     reads=[(TF.b, f2), C], writes=[(TF.b, f2)])
                S.op("dve", lambda e: e.tensor_tensor(out=tf_(f3), in0=tf_(f3), in1=C[0:n, TRIc, 0:n], op=ALU.mult),
                     reads=[(TF.b, f3), C], writes=[(TF.b, f3)])
                yield
                slg = ps_next()
                mm_group(slg, PS[0:n, slg, 0:n], [(kT_c, kT_c)], reads=kreads)
                b1 = ui * 10
                S.op("dve", lambda e: e.scalar_tensor_tensor(out=tn_(b1), in0=PS[0:n, slg, 0:n], scalar=col(2), in1=tf_(f2),
                                                             op0=ALU.mult, op1=ALU.mult),
                     reads=[(PS, slg), (DP, 2), (TF.b, f2)], writes=[(TN.b, b1)])
                pb = ps_next()
                transpose_to(PS, pb, PS[0:n, pb, 0:n], tnf_(b1), C[0:n, IDF, 0:n], reads=[(TN.b, b1), C])
                b2 = ui * 10 + 1
                S.op("act", lambda e: e.copy(out=tn_(b2), in_=PS[0:n, pb, 0:n]), reads=[(PS, pb)], writes=[(TN.b, b2)])
                bx, by = ui * 10 + 8, ui * 10 + 9
                S.op("dve", lambda e: e.tensor_tensor(out=tn_(bx), in0=tn_(b2), in1=C[0:n, IDF, 0:n], op=ALU.add),
                     reads=[(TN.b, b2), C], writes=[(TN.b, bx)])
                S.op("dve", lambda e: e.tensor_tensor(out=tn_(by), in0=tn_(b1), in1=C[0:n, IDF, 0:n], op=ALU.add),
                     reads=[(TN.b, b1), C], writes=[(TN.b, by)])
                UD = debug is not None and 'udump' in debug and cur_mode[0] == 'A' and h == 0 and ti == 0
                if UD:
                    dbg_out("u_N", TN.b, [128, 128], F32, ap=TN.b[:, b1, :])
                    dbg_out("u_Lm", TF.b, [128, 128], F32, ap=TF.b[:, f2, :])
                    dbg_out("u_DP", DP, [128, 8 * DP.ap.shape[2] * 8], F32, ap=DP.ap.rearrange("p a b c -> p (a b c)"))
                yield
                curN, curP = b1, b2
                nsq = 1 if smp else 6
                for lv in range(nsq):
                    lastlv = lv == nsq - 1
                    slp = ps_next()
                    mm_group(slp, PS[0:n, slp, 0:n], [(tr_(curN), tr_(curP))], reads=[(TN.b, curN), (TN.b, curP)])
                    lset = ui * 10 + (2 if lv % 2 == 0 else 6)
                    bp2 = lset
                    S.op("act", lambda e, slp=slp, bp2=bp2: e.copy(out=tn_(bp2), in_=PS[0:n, slp, 0:n]), reads=[(PS, slp)],
                         writes=[(TN.b, bp2)])
                    bn2 = None
                    if not lastlv:
                        sln = ps_next()
                        mm_group(sln, PS[0:n, sln, 0:n], [(tr_(curP), tr_(curN))], reads=[(TN.b, curN), (TN.b, curP)])
                        bn2 = lset + 1
                        S.op("act", lambda e, sln=sln, bn2=bn2: e.copy(out=tn_(bn2), in_=PS[0:n, sln, 0:n]),
                             reads=[(PS, sln)], writes=[(TN.b, bn2)])
                    yield
                    slx = ps_next()
                    mm_group(slx, PS[0:n, slx, 0:n], [(tr_(by), tr_(bp2))], reads=[(TN.b, by), (TN.b, bp2)])
                    bx2 = lset + 2
                    S.op("dve", lambda e, slx=slx, bx2=bx2, bx=bx: e.tensor_tensor(out=tn_(bx2), in0=PS[0:n, slx, 0:n],
                                                                                  in1=tn_(bx), op=ALU.add),
                         reads=[(PS, slx), (TN.b, bx)], writes=[(TN.b, bx2)])
                    if not lastlv:
                        sly = ps_next()
                        mm_group(sly, PS[0:n, sly, 0:n], [(tr_(bx), tr_(bn2))], reads=[(TN.b, bx), (TN.b, bn2)])
                        by2 = lset + 3
                        S.op("dve", lambda e, sly=sly, by2=by2, by=by: e.tensor_tensor(out=tn_(by2), in0=PS[0:n, sly, 0:n],
                                                                                      in1=tn_(by), op=ALU.add),
                             reads=[(PS, sly), (TN.b, by)], writes=[(TN.b, by2)])
                        by, curN = by2, bn2
                    bx, curP = bx2, bp2
                    yield
                if UD:
                    dbg_out("u_X", TN.b, [128, 128], F32, ap=TN.b[:, bx, :])
                yield
                bx16 = ui * 8
                S.op("act", lambda e: e.copy(out=tb_(bx16), in_=tn_(bx)), reads=[(TN.b, bx)], writes=[(TB.b, bx16)])
                bx = bx16
                bvb, bkb, bkt = ui * 8 + 1, ui * 8 + 2, ui * 8 + 3
                S.op("dve", lambda e: e.tensor_scalar_mul(out=TB.b[0:n, bvb, :], in0=vtok_c, scalar1=col(1)),
                     reads=kreads + [(DP, 1)], writes=[(TB.b, bvb)])
                S.op("dve", lambda e: e.tensor_scalar_mul(out=TB.b[0:n, bkb, :], in0=ktok_c, scalar1=col(7)),
                     reads=kreads + [(DP, 7)], writes=[(TB.b, bkb)])
                S.op("dve", lambda e: e.tensor_scalar_mul(out=TB.b[0:n, bkt, :], in0=ktok_c, scalar1=col(5)),
                     reads=kreads + [(DP, 5)], writes=[(TB.b, bkt)])
                slu = ps_next()
                mm_group(slu, PS[0:n, slu, 0:128], [(tb_(bx), TB.b[0:n, bvb, :])], reads=[(TB.b, bx), (TB.b, bvb)])
                fub = ui * 6 + 4
                S.op("act", lambda e: e.copy(out=TF.b[0:n, fub, :], in_=PS[0:n, slu, 0:128]), reads=[(PS, slu)],
                     writes=[(TF.b, fub)])
                slw = ps_next()
                mm_group(slw, PS[:, slw, 0:n], [(TB.b[0:n, bkb, :], tb_(bx))], reads=[(TB.b, bx), (TB.b, bkb)])
                bwd = ui * 8 + 4
                S.op("act", lambda e: e.copy(out=TB.b[:, bwd, 0:n], in_=PS[:, slw, 0:n]), reads=[(PS, slw)],
                     writes=[(TB.b, bwd)])
                bqd = bqk = None
                if do_o:
                    bqd, bqk = ui * 8 + 5, ui * 8 + 6
                    S.op("dve", lambda e: e.tensor_tensor(out=TB.b[:, bqd, 0:n], in0=qT_c, in1=TF.b[:, fed, 0:n], op=ALU.mult),
                         reads=kreads + [(TF.b, fed)], writes=[(TB.b, bqd)])
                    slq = ps_next()
                    mm_group(slq, PS[0:n, slq, 0:n], [(kT_c, qT_c)], reads=kreads)
                    S.op("dve", lambda e: e.tensor_tensor(out=tb_(bqk), in0=PS[0:n, slq, 0:n], in1=tf_(f3), op=ALU.mult),
                         reads=[(PS, slq), (TF.b, f3)], writes=[(TB.b, bqk)])
                if UD:
                    dbg_out("u_ub", TF.b, [128, 128], F32, ap=TF.b[:, fub, :])
                    dbg_out("u_wd", TB.b, [128, 128], BF16, ap=TB.b[:, bwd, :])
                    dbg_out("u_kt", TB.b, [128, 128], BF16, ap=TB.b[:, bkt, :])
                yield
                bu = ui * 8 + 7
                if not smp:
                    while turn[0] != ti:
                        yield
                    sl = ps_next()
                    mm_group(sl, PS[0:n, sl, 0:128], [(TB.b[:, bwd, 0:n], Sb[:, h, :])], reads=[(TB.b, bwd), (Sb, h)])
                    S.op("dve", lambda e: e.tensor_tensor(out=TB.b[0:n, bu, :], in0=TF.b[0:n, fub, :], in1=PS[0:n, sl, 0:128],
                                                          op=ALU.subtract), reads=[(PS, sl), (TF.b, fub)], writes=[(TB.b, bu)])
                    yield
                    if do_o:
                        slo = ps_next()
                        mm_group(slo, PS[0:n, slo, 0:128], [(TB.b[:, bqd, 0:n], Sb[:, h, :]), (tb_(bqk), TB.b[0:n, bu, :])],
                                 reads=[(TB.b, bqd), (Sb, h), (TB.b, bqk), (TB.b, bu)])
                        opost(slo)
                    yield
                    slk = ps_next()
                    mm_group(slk, PS[:, slk, 0:128], [(TB.b[0:n, bkt, :], TB.b[0:n, bu, :])], reads=[(TB.b, bkt), (TB.b, bu)])
                    S.op("dve", lambda e: e.scalar_tensor_tensor(out=Sb[:, h, :], in0=Sst[:, h, :], scalar=DP[:, 6, ti, h:h + 1],
                                                                 in1=PS[:, slk, 0:128], op0=ALU.mult, op1=ALU.add),
                         reads=[(Sst, h), (DP, 6), (PS, slk)], writes=[(Sb, h)])
                    S.op("dve", lambda e: e.scalar_tensor_tensor(out=Sst[:, h, :], in0=Sst[:, h, :], scalar=DP[:, 6, ti, h:h + 1],
                                                                 in1=PS[:, slk, 0:128], op0=ALU.mult, op1=ALU.add),
                         reads=[(Sst, h), (DP, 6), (PS, slk)], writes=[(Sst, h)])
                    turn[0] += 1
                    if UD:
                        dbg_out("u_S", Sst, [128, 128], F32, ap=Sst[:, 0, :])
                        dbg_out("u_u", TB.b, [128, 128], BF16, ap=TB.b[:, bu, :])
                else:
                    Ss, Ssb, wdm, qdm, ktm, cds = sm["Ss"], sm["Ssb"], sm["wdm"], sm["qdm"], sm["ktm"], sm["cds"]
                    S.op("dve", lambda e: e.tensor_tensor(out=wdm.ap, in0=TB.b[:, bwd, 0:64].unsqueeze(1).broadcast_to([128, 16, 64]),
                                                          in1=CM.ap, op=ALU.mult), reads=[(TB.b, bwd), CM], writes=[wdm])
                    S.op("dve", lambda e: e.tensor_tensor(out=qdm.ap, in0=TB.b[:, bqd, 0:64].unsqueeze(1).broadcast_to([128, 16, 64]),
                                                          in1=CM.ap, op=ALU.mult), reads=[(TB.b, bqd), CM], writes=[qdm])
                    S.op("dve", lambda e: e.tensor_tensor(out=ktm[0:64], in0=TB.b[0:64, bkt, :].unsqueeze(1).broadcast_to([64, 16, 128]),
                                                          in1=RM[0:64, :].unsqueeze(2).broadcast_to([64, 16, 128]), op=ALU.mult),
                         reads=[(TB.b, bkt), RM], writes=[ktm])
                    yield
                    sl = ps_next()
                    mm_group(sl, PS[0:64, sl, 0:128], [(wdm[:, b_, :], Ssb[:, b_, :]) for b_ in range(16)], reads=[wdm, Ssb])
                    S.op("dve", lambda e: e.tensor_tensor(out=TB.b[0:64, bu, :], in0=TF.b[0:64, fub, :], in1=PS[0:64, sl, 0:128],
                                                          op=ALU.subtract), reads=[(PS, sl), (TF.b, fub)], writes=[(TB.b, bu)])
                    yield
                    slo = ps_next()
                    mm_group(slo, PS[0:64, slo, 0:128],
                             [(qdm[:, b_, :], Ssb[:, b_, :]) for b_ in range(16)] + [(tb_(bqk), TB.b[0:64, bu, :])],
                             reads=[qdm, Ssb, (TB.b, bqk), (TB.b, bu)])
                    opost(slo)
                    for g4 in range(4):
                        yield
                        slk = ps_next()

                        def fk(e, slk=slk, g4=g4):
                            ins = None
                            for j in range(4):
                                ins = e.matmul(PS[:, slk, j * 128:(j + 1) * 128], lhsT=ktm[0:64, 4 * g4 + j, :],
                                               rhs=TB.b[0:64, bu, :], start=True, stop=True)
                            return ins
                        S.op("pe", fk, reads=[ktm, (TB.b, bu)], writes=[(PS, slk)])
                        sv = Ss[:, 4 * g4:4 * g4 + 4, :]
                        S.op("dve", lambda e, sv=sv, g4=g4: e.tensor_tensor(
                            out=sv, in0=sv, in1=cds[:, 4 * g4:4 * g4 + 4].unsqueeze(2).broadcast_to([128, 4, 128]), op=ALU.mult),
                            reads=[Ss, cds], writes=[Ss])
                        S.op("dve", lambda e, sv=sv, slk=slk: e.tensor_tensor(
                            out=sv, in0=sv, in1=PS[:, slk, :].rearrange("p (j v) -> p j v", v=128), op=ALU.add),
                            reads=[Ss, (PS, slk)], writes=[Ss])
                    S.dma("sp", o_delta_s[:, h].rearrange("b k v -> k b v"), Ss.ap, reads=[Ss], is_out=True)

            sconvT = S.sb("sconvT", [128, 24, 48], F32, es=scD)
            nconvT = S.sb("nconvT", [128, 24, 51], F32, nslots=24, es=scD)

            def head_pass(mode, uT, DP, sc):
                nonlocal TF, TB, TN, SST
                cur_mode[0] = mode
                KU_ = KUNITS if mode == "B" else KUNITS_A
                TF = Ring(S, "tf", [128, 128], F32, 6 * KU_, es=sc)
                TB = Ring(S, "tb", [128, 128], BF16, 8 * KU_, es=sc)
                TN = Ring(S, "tn", [128, 128], F32, 10 * KU_, es=sc)
                SST = Ring(S, "sst", [128, 4], F32, KU_, es=sc)
                UDH = debug is not None and 'udump' in debug and mode == 'A'
                B_ = mode == "B"
                L = 1152 if B_ else 1024
                o0 = 128 if B_ else 0
                W_ = L + (112 if B_ else 0)
                RW = 3 + W_
                stg = S.sb("stg" + mode, [128, 1, RW], F32, nslots=1, es=sc)
                yrow = S.sb("yrow" + mode, [128, W_], F32, es=sc)
                crow = S.sb("crow" + mode, [128, 2, W_], BF16, nslots=2, es=sc)
                cmap = {0: 0, 1: 0, 2: 1}
                nrmP = [S.sb("nrm" + mode, [128, 2, W_], BF16, nslots=2, es=sc) for _ in range(2)]
                ktokP = [S.sb("ktok" + mode, [128, 9, 128], BF16, nslots=9, es=sc) for _ in range(2)]
                vtokP = [S.sb("vtok" + mode, [128, 9, 128], BF16, nslots=9, es=sc) for _ in range(2)]
                cmpP = [None, None]
                wzs = {}
                sm = None
                if B_:
                    cmpP = [S.sb("cmp", [128, 3, 64], BF16, nslots=3, es=sc) for _ in range(2)]
                    gz = S.sb("gz", [128, KUNITS, 128], F32, nslots=KUNITS, es=sc)
                    ogtok = S.sb("ogtok", [128, KUNITS, 128], BF16, nslots=KUNITS, es=sc)
                    sm = {"Ss": S.sb("Ss", [128, 16, 128], F32, es=sc), "Ssb": S.sb("Ssb", [128, 16, 128], BF16, es=sc),
                          "wdm": S.sb("wdm", [128, 16, 64], BF16, es=sc), "qdm": S.sb("qdm", [128, 16, 64], BF16, es=sc),
                          "ktm": S.sb("ktm", [128, 16, 128], BF16, es=sc), "cds": S.sb("cds", [128, 16], F32, es=sc)}
                WZ = S.sb("WZ", [128, 16, 128], BF16, es=sc) if B_ else None
                S.op("dve", lambda e: e.memset(stg.ap, 0.0), writes=[stg])
                comps = [("q", 0, QOFF), ("k", 1, KOFF), ("v", 2, VOFF)] if B_ else [("k", 1, KOFF), ("v", 2, VOFF)]
                ptiles = tok_tiles(L)
                ext = lambda ci: stg[:, 0, 3 + L:3 + L + 112].rearrange("p (b j) -> p b j", j=7)
                def prep(h):
                    nrm, ktok, vtok, cmp_ = nrmP[h % 2], ktokP[h % 2], vtokP[h % 2], cmpP[h % 2]
                    for (nm, ci, off) in comps:
                        chn = ci * 8 + h
                        ws = load_w(w_in[:, off + h * 128:off + (h + 1) * 128], 16)
                        for (a, b) in ptiles:
                            sl = ps_next()
                            mm_group(sl, PS[:, sl, 0:b - a], [(WR.b[:, ws, k, :], uT[:, k, a:b]) for k in range(16)],
                                     reads=[(WR.b, ws), uT])
                            S.op("act", lambda e, sl=sl, a=a, b=b, ci=ci: e.copy(out=stg[:, 0, 3 + a:3 + b], in_=PS[:, sl, 0:b - a]),
                                 reads=[(PS, sl)], writes=[(stg, 0)])
                            yield
                        if B_:
                            sl = ps_next()
                            mm_group(sl, PS[:, sl, 0:64], [(WR.b[:, ws, k, :], uT[:, k, 1152:1216]) for k in range(16)],
                                     reads=[(WR.b, ws), uT])
                            S.op("act", lambda e, sl=sl, ci=ci: e.copy(out=ext(ci)[:, :, 3:7],
                                                                       in_=PS[:, sl, 0:64].rearrange("p (b t) -> p b t", t=4)),
                                 reads=[(PS, sl)], writes=[(stg, 0)])
                            S.op("dve", lambda e, ci=ci, chn=chn: e.tensor_copy(
                                out=ext(ci)[:, :, 0:3], in_=sconvT[:, chn, :].rearrange("p (b j) -> p b j", j=3)),
                                reads=[sconvT], writes=[(stg, 0)])
                            S.op("dve", lambda e, ci=ci, chn=chn: e.tensor_copy(
                                out=nconvT[:, chn, 0:48].rearrange("p (b j) -> p b j", j=3), in_=ext(ci)[:, :, 4:7]),
                                reads=[(stg, 0)], writes=[(nconvT, chn)])
                            S.op("dve", lambda e, ci=ci, chn=chn: e.tensor_copy(out=nconvT[:, chn, 48:51], in_=stg[:, 0, L:L + 3]),
                                 reads=[(stg, 0)], writes=[(nconvT, chn)])
                        S.op("dve", lambda e, ci=ci, chn=chn: e.tensor_scalar_mul(out=yrow.ap, in0=stg[:, 0, 0:W_],
                                                                                 scalar1=cwT[:, chn:chn + 1]),
                             reads=[(stg, 0), cwT], writes=[yrow])
                        for j in range(1, 4):
                            S.op("dve", lambda e, ci=ci, chn=chn, j=j: e.scalar_tensor_tensor(
                                out=yrow.ap, in0=stg[:, 0, j:j + W_], scalar=cwT[:, j * 24 + chn:j * 24 + chn + 1], in1=yrow.ap,
                                op0=ALU.mult, op1=ALU.add), reads=[(stg, 0), cwT, yrow], writes=[yrow])
                            yield
                        S.op("act", lambda e, ci=ci: e.activation(out=crow[:, cmap[ci], :], in_=yrow.ap, func=AF.Silu), reads=[yrow],
                             writes=[(crow, cmap[ci])])
                        yield
                        if nm in ("q", "k"):
                            S.op("act", lambda e, ci=ci: e.activation(out=nrm[:, ci, :], in_=crow[:, cmap[ci], :], func=AF.Square),
                                 reads=[(crow, cmap[ci])], writes=[(nrm, ci)])
                            for (a, b) in tok_tiles(W_):
                                sl = ps_next()
                                mm_group(sl, PS[:, sl, 0:b - a], [(CB[:, 1, :], nrm[:, ci, a:b])], reads=[CB, (nrm, ci)])
                                S.op("act", lambda e, sl=sl, a=a, b=b: e.activation(
                                    out=yrow[:, a:b], in_=PS[:, sl, 0:b - a], func=AF.Ln, bias=EPS),
                                    reads=[(PS, sl)], writes=[yrow])
                                S.op("act", lambda e, a=a, b=b: e.activation(
                                    out=yrow[:, a:b], in_=yrow[:, a:b], func=AF.Exp, scale=-0.5),
                                    reads=[yrow], writes=[yrow])
                                yield
                            S.op("dve", lambda e, ci=ci, nm=nm: e.scalar_tensor_tensor(
                                out=nrm[:, ci, :], in0=crow[:, cmap[ci], :], scalar=(128.0 ** -0.5 if nm == "q" else 1.0), in1=yrow.ap,
                                op0=ALU.mult, op1=ALU.mult), reads=[(crow, cmap[ci]), yrow], writes=[(nrm, ci)])
                    ckpt(mode + '61')
                    for c in range(8):
                        kc = o0 + 128 * c
                        for (src, dstb) in ((nrm[:, 1, kc:kc + 128], ktok), (crow[:, 1, kc:kc + 128], vtok)):
                            pb = ps_next()
                            transpose_to(PS, pb, PS[:, pb, 0:128], src, CB[:, 0, :], reads=[(nrm, 1), (crow, 1), CB])
                            S.op("act", lambda e, pb=pb, dstb=dstb, c=c: e.copy(out=dstb[:, c, :], in_=PS[:, pb, 0:128]),
                                 reads=[(PS, pb)], writes=[(dstb, c)])
                        yield
                    if B_:
                        for i, src in enumerate((nrm[:, 0, :], nrm[:, 1, :], crow[:, 1, :])):
                            S.op("dve", lambda e, i=i, src=src: e.tensor_copy(
                                out=cmp_[:, i, :].rearrange("p (b t) -> p b t", t=4),
                                in_=src[:, L:L + 112].rearrange("p (b j) -> p b j", j=7)[:, :, 3:7]),
                                reads=[(nrm, 0), (nrm, 1), (crow, 1)], writes=[(cmp_, i)])
                        for (i, dstb) in ((1, ktok), (2, vtok)):
                            pb = ps_next()
                            transpose_to(PS, pb, PS[0:64, pb, 0:128], cmp_[:, i, :], CB[:, 0, :], reads=[(cmp_, i), CB])
                            S.op("act", lambda e, pb=pb, dstb=dstb: e.copy(out=dstb[0:64, 8, :], in_=PS[0:64, pb, 0:128]),
                                 reads=[(PS, pb)], writes=[(dstb, 8)])

                    yield

                for _g in prep(0):
                    pass
                for h in range(NH):
                    nrm, ktok, vtok, cmp_ = nrmP[h % 2], ktokP[h % 2], vtokP[h % 2], cmpP[h % 2]
                    if B_:
                        zsrc = wsrc(w_in[:, ZOFF + h * 128:ZOFF + (h + 1) * 128])
                        for k0 in range(0, 16, 4):
                            S.dma("pool", WZ[:, k0:k0 + 4, :], zsrc[:, k0:k0 + 4, :], writes=[WZ])
                    nxt = prep(h + 1) if h + 1 < NH else None

                    def mk_opost(ui, n, ucol, ogcol, h=h):
                        def opost(slo):
                            gs = ui
                            zsl = ps_next()
                            mm_group(zsl, PS[0:n, zsl, 0:128], [(uT[:, k, ucol:ucol + n], WZ[:, k, :]) for k in range(16)],
                                     reads=[uT, WZ])
                            S.op("act", lambda e: e.activation(out=gz[0:n, gs, :], in_=PS[0:n, zsl, 0:128], func=AF.Silu),
                                 reads=[(PS, zsl)], writes=[(gz, gs)])
                            S.op("dve", lambda e: e.tensor_tensor(out=gz[0:n, gs, :], in0=gz[0:n, gs, :], in1=ong_t[0:n, :], op=ALU.mult),
                                 reads=[(gz, gs), ong_t], writes=[(gz, gs)])
                            ss = ui
                            fj = ui * 6 + 5
                            S.op("dve", lambda e: e.memset(SST.b[:, ss, :], 0.0), writes=[(SST.b, ss)])
                            S.op("act", lambda e: e.activation(out=TF.b[0:n, fj, :], in_=PS[0:n, slo, 0:128], func=AF.Square,
                                                               accum_out=SST.b[0:n, ss, 0:1]),
                                 reads=[(PS, slo), (SST.b, ss)], writes=[(TF.b, fj), (SST.b, ss)])
                            S.op("dve", lambda e: e.tensor_scalar(out=SST.b[0:n, ss, 1:2], in0=SST.b[0:n, ss, 0:1], scalar1=1.0 / 128,
                                                                  scalar2=EPS, op0=ALU.mult, op1=ALU.add), reads=[(SST.b, ss)],
                                 writes=[(SST.b, ss)])
                            S.op("act", lambda e: e.activation(out=SST.b[0:n, ss, 1:2], in_=SST.b[0:n, ss, 1:2], func=AF.Ln),
                                 reads=[(SST.b, ss)], writes=[(SST.b, ss)])
                            S.op("act", lambda e: e.activation(out=SST.b[0:n, ss, 1:2], in_=SST.b[0:n, ss, 1:2], func=AF.Exp,
                                                               scale=-0.5), reads=[(SST.b, ss)], writes=[(SST.b, ss)])
                            S.op("dve", lambda e: e.scalar_tensor_tensor(out=ogtok[0:n, gs, :], in0=PS[0:n, slo, 0:128],
                                                                         scalar=SST.b[0:n, ss, 1:2], in1=gz[0:n, gs, :],
                                                                         op0=ALU.mult, op1=ALU.mult),
                                 reads=[(PS, slo), (SST.b, ss), (gz, gs)], writes=[(ogtok, gs)])
                            pb = ps_next()
                            transpose_to(PS, pb, PS[:, pb, 0:n], ogtok[0:n, gs, :], CB[0:n, 0, 0:n], reads=[(ogtok, gs), CB])
                            S.op("act", lambda e: e.copy(out=ogT[:, h, ogcol:ogcol + n], in_=PS[:, pb, 0:n]), reads=[(PS, pb)],
                                 writes=[(ogT, h)])
                        return opost
                    kreads = [(nrm, 0), (nrm, 1), ktok, vtok]
                    ckpt(mode + '62')
                    pending = []
                    for c in range(8):
                        kc = o0 + 128 * c
                        pending.append(dict(c=c, n=128, smp=False, kT=nrm[:, 1, kc:kc + 128], qT=nrm[:, 0, kc:kc + 128] if B_ else None,
                                            ktok=ktok[:, c, :], vtok=vtok[:, c, :], ucol=kc, ogcol=128 * c, kr=kreads))
                    if B_:
                        pending.append(dict(c=8, n=64, smp=True, kT=cmp_[:, 1, :], qT=cmp_[:, 0, :], ktok=ktok[0:64, 8, :],
                                            vtok=vtok[0:64, 8, :], ucol=1152, ogcol=1024, kr=kreads + [cmp_]))
                    gens = []
                    turn[0] = 0
                    free_ids = list(range(KU_))
                    while pending or gens:
                        if pending and free_ids:
                            a_ = pending.pop(0)
                            ui = free_ids.pop(0)
                            if a_["smp"]:
                                S.dma("sp", sm["Ss"].ap, sdelta[:, h].rearrange("b k v -> k b v"), writes=[sm["Ss"]])
                                S.op("act", lambda e: e.copy(out=sm["Ssb"].ap, in_=sm["Ss"].ap), reads=[sm["Ss"]], writes=[sm["Ssb"]])
                            g_ = unit(ui, DP, h, a_["c"], a_["n"], a_["smp"], a_["kT"], a_["qT"], a_["ktok"], a_["vtok"], B_, a_["kr"],
                                      mk_opost(ui, a_["n"], a_["ucol"], a_["ogcol"]) if B_ else None, sm if a_["smp"] else None)
                            gens.append((g_, ui))
                        for (g_, ui) in list(gens):
                            try:
                                next(g_)
                            except StopIteration:
                                gens.remove((g_, ui))
                                free_ids.append(ui)
                        if nxt is not None:
                            try:
                                next(nxt)
                            except StopIteration:
                                nxt = None
                    if nxt is not None:
                        for _g in nxt:
                            pass

            ckpt(4)
            with ExitStack() as sc:
                uTA = S.sb("uTA", [128, 16, 1024], BF16, nslots=16, es=sc)
                with ExitStack() as scu:
                    xt = S.sb("xtA", [128, 2, D], F32, nslots=2, es=scu)
                    xb = S.sb("xbA", [128, 2, D], BF16, nslots=2, es=scu)
                    st = S.sb("stA", [128, 2, 2], F32, nslots=2, es=scu)
                    xr = Ring.__new__(Ring)
                    xr.n, xr.i = 2, 0
                    for t in range(8):
                        make_u(xpre[t * 128:(t + 1) * 128, :], 128, uTA, t * 128, "pre", xt, xb, st, xr)
                    S.barrier()
                dbg_out("uTA", uTA, [128, 16 * 1024], BF16, ap=uTA.ap.rearrange("p k t -> p (k t)"))
                ckpt(5)
                DPA = decay_prep(uTA, [(128 * c, 128, False) for c in range(8)], sc)
                ckpt(6)
                head_pass("A", uTA, DPA, sc)
                dbg_out("Smid", Sst, [128, 1024], ap=Sst.ap.rearrange("p h v -> p (h v)"))
                S.barrier()
            ckpt(7)
            with ExitStack() as sc:
                uTB = S.sb("uTB", [128, 16, 1216], BF16, nslots=16, es=sc)
                with ExitStack() as sc2:
                    xt = S.sb("xtB", [128, 2, D], F32, nslots=2, es=sc2)
                    xb = S.sb("xbB", [128, 2, D], BF16, nslots=2, es=sc2)
                    st = S.sb("stB", [128, 2, 2], F32, nslots=2, es=sc2)
                    xr = Ring.__new__(Ring)
                    xr.n, xr.i = 2, 0
                    make_u(xpre[896:1024, :], 128, uTB, 0, "pre", xt, xb, st, xr)
                    for t in range(8):
                        make_u(xown[t * 128:(t + 1) * 128, :], 128, uTB, 128 + t * 128, "own", xt, xb, st, xr)
                    make_u(xsm[:, :], 64, uTB, 1152, "smp", xt, xb, st, xr)
                    sct = S.sb("sct", [48, 3072], F32, es=sc2)
                    S.dma("sp", sct.ap, sconv, writes=[sct])
                    for chn in range(24):
                        sl = ps_next()
                        transpose_to(PS, sl, PS[:, sl, 0:48], sct[:, chn * 128:(chn + 1) * 128], C[0:48, IDF, 0:48],
                                     reads=[sct, C])
                        S.op("act", lambda e, sl=sl, chn=chn: e.copy(out=sconvT[:, chn, :], in_=PS[:, sl, 0:48]),
                             reads=[(PS, sl)], writes=[sconvT])
                    S.barrier()
                ckpt(72)
                DPB = decay_prep(uTB, [(128 + 128 * c, 128, False) for c in range(8)] + [(1152, 64, True)], sc)
                ckpt(73)
                if debug is not None and 'shift' in debug:
                    for _ in range(3):
                        S.op("act", lambda e: e.copy(out=flag_t.ap, in_=flag_t.ap), reads=[flag_t], writes=[flag_t])
                with ExitStack() as scH:
                    head_pass("B", uTB, DPB, scH)
                    S.barrier()
                ckpt(74)

                with ExitStack() as scP:
                    PW = 16 + 1152 + 304
                    xrow = S.sb("xrow", [128, 4, PW], F32, nslots=4, es=scP)
                    hist = S.sb("hist", [128, 2, 1024], F32, nslots=2, es=scP)
                    histT = S.sb("histT", [128, 240], F32, es=scP)
                    pwt = S.sb("pwt", [128, 4, 2, 256], BF16, es=scP)
                    ypl = S.sb("ypl", [128, 2, 1088], BF16, nslots=2, es=scP)
                    npoolT = S.sb("npoolT", [128, 8, 256], F32, nslots=8, es=scP)
                    invc = S.sb("invc", [128, 4, 16], F32, es=scP)
                    io_i = S.sb("io_i", [128, 16], I32, es=scP)
                    t16 = S.sb("t16", [128, 16], F32, es=scP)
                    otok = hist
                    S.dma("sp", hist[:, 0, :], spool[0:128, :], writes=[(hist, 0)])
                    S.dma("sp", hist[0:112, 1, :], spool[128:240, :], writes=[(hist, 1)])
                    for g in range(4):
                        S.dma("pool", pwt[:, g], pool_w[g].rearrange("(c p) d -> p c d", p=128), writes=[pwt])
                    S.op("dve", lambda e: e.memset(xrow.ap, 0.0), writes=[xrow])
                    S.op("dve", lambda e: e.memset(npoolT.ap, 0.0), writes=[npoolT])
                    S.op("pool", lambda e: e.iota(io_i.ap, pattern=[[1, 16]], base=1, channel_multiplier=0), writes=[io_i])
                    S.op("dve", lambda e: e.tensor_copy(out=invc[:, 0, :], in_=io_i.ap), reads=[io_i], writes=[invc])
                    S.op("dve", lambda e: e.tensor_scalar_add(out=invc[:, 0, :], in0=invc[:, 0, :], scalar1=pos0_t[:, 0:1]),
                         reads=[invc, pos0_t], writes=[invc])
                    for g in (3, 2, 1, 0):
                        S.op("dve", lambda e, g=g: e.tensor_scalar_min(out=invc[:, g, :], in0=invc[:, 0, :],
                                                                       scalar1=float((2, 4, 8, 16)[g])), reads=[invc], writes=[invc])
                    S.op("dve", lambda e: e.reciprocal(out=invc.ap, in_=invc.ap), reads=[invc], writes=[invc])
                    xe = lambda sl_: xrow[:, sl_, 16 + 1152:PW].rearrange("p (b j) -> p b j", j=19)
                    for g in range(4):
                        wwin = (2, 4, 8, 16)[g]
                        for c2 in range(2):
                            ch = 2 * g + c2
                            xs = 0 if ch % 2 == 0 else 3
                            ws = load_w(w_in[:, POFF + ch * 128:POFF + (ch + 1) * 128], 16)
                            for (a, b) in tok_tiles(1152):
                                sl = ps_next()
                                mm_group(sl, PS[:, sl, 0:b - a], [(WR.b[:, ws, k, :], uTB[:, k, a:b]) for k in range(16)],
                                         reads=[(WR.b, ws), uTB])
                                S.op("act", lambda e, sl=sl, a=a, b=b: e.copy(out=xrow[:, xs, 16 + a:16 + b], in_=PS[:, sl, 0:b - a]),
                                     reads=[(PS, sl)], writes=[(xrow, xs)])
                            sl = ps_next()
                            mm_group(sl, PS[:, sl, 0:64], [(WR.b[:, ws, k, :], uTB[:, k, 1152:1216]) for k in range(16)],
                                     reads=[(WR.b, ws), uTB])
                            S.op("act", lambda e, sl=sl: e.copy(out=xe(xs)[:, :, 15:19],
                                                               in_=PS[:, sl, 0:64].rearrange("p (b t) -> p b t", t=4)),
                                 reads=[(PS, sl)], writes=[(xrow, xs)])
                            for t2, rows in ((0, 128), (1, 112)):
                                sl = ps_next()
                                transpose_to(PS, sl, PS[:, sl, 0:rows], hist[0:rows, t2, ch * 128:(ch + 1) * 128],
                                             C[0:rows, IDF, 0:rows], reads=[(hist, t2), C])
                                S.op("act", lambda e, sl=sl, t2=t2, rows=rows: e.copy(out=histT[:, t2 * 128:t2 * 128 + rows],
                                                                                    in_=PS[:, sl, 0:rows]),
                                     reads=[(PS, sl)], writes=[histT])
                            S.op("dve", lambda e: e.tensor_copy(out=xe(xs)[:, :, 0:15],
                                                                in_=histT.ap.rearrange("p (b j) -> p b j", j=15)),
                                 reads=[histT], writes=[(xrow, xs)])
                            S.op("dve", lambda e, ch=ch: e.tensor_copy(out=npoolT[:, ch, 0:240].rearrange("p (b j) -> p b j", j=15),
                                                                      in_=xe(xs)[:, :, 4:19]), reads=[(xrow, xs)], writes=[(npoolT, ch)])
                            S.op("dve", lambda e, ch=ch: e.tensor_copy(out=npoolT[:, ch, 240:255], in_=xrow[:, xs, 16 + 1152 - 15:16 + 1152]),
                                 reads=[(xrow, xs)], writes=[(npoolT, ch)])
                            cur = xs
                            for lv in range(g + 1):
                                sh = 1 << lv
                                new = 1 + (lv % 2)
                                S.op("dve", lambda e, cur=cur, new=new, sh=sh: e.tensor_tensor(
                                    out=xrow[:, new, 16:PW], in0=xrow[:, cur, 16:PW], in1=xrow[:, cur, 16 - sh:PW - sh], op=ALU.add),
                                    reads=[(xrow, cur)], writes=[(xrow, new)])
                                cur = new
                            S.op("dve", lambda e, cur=cur, c2=c2: e.scalar_tensor_tensor(
                                out=ypl[:, c2, 0:1024], in0=xrow[:, cur, 144:1168], scalar=1.0 / wwin, in1=xrow[:, xs, 144:1168],
                                op0=ALU.mult, op1=ALU.subtract), reads=[(xrow, cur), (xrow, xs)], writes=[(ypl, c2)])
                            S.op("dve", lambda e, cur=cur, g=g: e.tensor_tensor(out=t16.ap, in0=xrow[:, cur, 144:160], in1=invc[:, g, :],
                                                                                 op=ALU.mult), reads=[(xrow, cur), invc], writes=[t16])
                            S.op("dve", lambda e, c2=c2: e.tensor_tensor(out=ypl[:, c2, 0:16], in0=t16.ap, in1=xrow[:, xs, 144:160],
                                                                         op=ALU.subtract), reads=[t16, (xrow, xs)], writes=[(ypl, c2)])
                            S.op("dve", lambda e, cur=cur, c2=c2: e.scalar_tensor_tensor(
                                out=ypl[:, c2, 1024:1088].rearrange("p (b t) -> p b t", t=4), in0=xe(cur)[:, :, 15:19],
                                scalar=1.0 / wwin, in1=xe(xs)[:, :, 15:19], op0=ALU.mult, op1=ALU.subtract),
                                reads=[(xrow, cur), (xrow, xs)], writes=[(ypl, c2)])
                        for dc in range(2):
                            for (a, b) in ((0, 512), (512, 1024), (1024, 1088)):
                                sl = ps_next()
                                mm_group(sl, PS[:, sl, 0:b - a],
                                         [(pwt[:, g, c2, dc * 128:(dc + 1) * 128], ypl[:, c2, a:b]) for c2 in range(2)],
                                         reads=[pwt, ypl])
                                S.op("act", lambda e, sl=sl, a=a, b=b, g=g, dc=dc: e.activation(
                                    out=ypsT[:, 2 * g + dc, a:b], in_=PS[:, sl, 0:b - a], func=AF.Copy,
                                    scale=pscT[:, 2 * g + dc:2 * g + dc + 1]), reads=[(PS, sl), pscT], writes=[(ypsT, 2 * g + dc)])
                    for ch in range(8):
                        for half in range(2):
                            sl = ps_next()
                            transpose_to(PS, sl, PS[:, sl, 0:128], npoolT[:, ch, half * 128:(half + 1) * 128], C[:, IDF, :],
                                         reads=[(npoolT, ch), C])
                            S.op("act", lambda e, sl=sl, ch=ch, half=half: e.copy(out=otok[:, half, ch * 128:(ch + 1) * 128],
                                                                                 in_=PS[:, sl, 0:128]),
                                 reads=[(PS, sl)], writes=[(otok, half)])
                    S.dma("sp", o_pool[0:128, :], otok[:, 0, :], reads=[(otok, 0)], is_out=True)
                    S.dma("sp", o_pool[128:255, :], otok[0:127, 1, :], reads=[(otok, 1)], is_out=True)
                    S.barrier()
                S.dma("sp", o_delta_p.rearrange("h k v -> k h v"), Sst.ap, reads=[Sst], is_out=True)
                octok = S.sb("octok", [51, 3072], F32, es=sc)
                for chn in range(24):
                    sl = ps_next()
                    transpose_to(PS, sl, PS[0:51, sl, 0:128], nconvT[:, chn, :], C[:, IDF, :], reads=[(nconvT, chn), C])
                    S.op("act", lambda e, sl=sl, chn=chn: e.copy(out=octok[:, chn * 128:(chn + 1) * 128], in_=PS[0:51, sl, 0:128]),
                         reads=[(PS, sl)], writes=[octok])
                S.dma("sp", o_conv, octok.ap, reads=[octok], is_out=True)
                S.barrier()
            scD.close()
            ckpt(8)
            T2 = [(0, 512), (512, 1024), (1024, 1088)]
            with ExitStack() as s4:
                MH = S.sb("MH", [128, 16, 1088], BF16, nslots=16, es=s4)
                with ExitStack() as sa:
                    uT2 = S.sb("uT2", [128, 16, 1088], BF16, nslots=16, es=sa)
                    xt = S.sb("xt4", [128, 1, D], F32, nslots=1, es=sa)
                    xb = S.sb("xb4", [128, 1, D], BF16, nslots=1, es=sa)
                    st = S.sb("st4", [128, 1, 2], F32, nslots=1, es=sa)
                    sg = S.sb("sg", [128, 3, 1088], F32, nslots=3, es=sa)
                    xr = Ring.__new__(Ring)
                    xr.n, xr.i = 1, 0
                    for t in range(8):
                        make_u(xown[t * 128:(t + 1) * 128, :], 128, uT2, t * 128, "own", xt, xb, st, xr)
                    make_u(xsm[:, :], 64, uT2, 1024, "smp", xt, xb, st, xr)
                    for n in range(16):
                        for (i, off) in ((0, GAOFF), (1, GBOFF)):
                            ws = load_w(w_in[:, off + n * 128:off + (n + 1) * 128], 16)
                            for (a, b) in T2:
                                sl = ps_next()
                                mm_group(sl, PS[:, sl, 0:b - a], [(WR.b[:, ws, k, :], uT2[:, k, a:b]) for k in range(16)],
                                         reads=[(WR.b, ws), uT2])
                                S.op("act", lambda e, sl=sl, a=a, b=b, i=i: e.activation(out=sg[:, i, a:b], in_=PS[:, sl, 0:b - a],
                                                                                        func=AF.Sigmoid),
                                     reads=[(PS, sl)], writes=[(sg, i)])
                        ws = load_w(w_proj_a[:, n * 128:(n + 1) * 128], 8)
                        for (a, b) in T2:
                            sl = ps_next()
                            mm_group(sl, PS[:, sl, 0:b - a], [(WR.b[:, ws, k, :], ogT[:, k, a:b]) for k in range(8)],
                                     reads=[(WR.b, ws), ogT])
                            S.op("dve", lambda e, sl=sl, a=a, b=b: e.tensor_tensor(out=sg[:, 2, a:b], in0=PS[:, sl, 0:b - a],
                                                                                  in1=sg[:, 0, a:b], op=ALU.mult),
                                 reads=[(PS, sl), (sg, 0)], writes=[(sg, 2)])
                        ws = load_w(w_proj_b[:, n * 128:(n + 1) * 128], 8)
                        for (a, b) in T2:
                            sl = ps_next()
                            mm_group(sl, PS[:, sl, 0:b - a], [(WR.b[:, ws, k, :], ypsT[:, k, a:b]) for k in range(8)],
                                     reads=[(WR.b, ws), ypsT])
                            S.op("dve", lambda e, sl=sl, a=a, b=b: e.tensor_tensor(out=sg[:, 1, a:b], in0=PS[:, sl, 0:b - a],
                                                                                  in1=sg[:, 1, a:b], op=ALU.mult),
                                 reads=[(PS, sl), (sg, 1)], writes=[(sg, 1)])
                            S.op("dve", lambda e, a=a, b=b, n=n: e.tensor_tensor(out=MH[:, n, a:b], in0=sg[:, 1, a:b],
                                                                                in1=sg[:, 2, a:b], op=ALU.add),
                                 reads=[(sg, 1), (sg, 2)], writes=[(MH, n)])
                    S.barrier()
                ckpt(81)
                xT = S.sb("xT", [128, 16, 1088], F32, nslots=16, es=s4)
                rsb = S.sb("rsb", [128, 1088], F32, es=s4)
                tmpf = S.sb("tmpf", [128, 2, 1088], F32, nslots=2, es=s4)

                def mod_add(n, sl, a, b, chunk0):
                    if b <= 1024:
                        S.op("dve", lambda e: e.scalar_tensor_tensor(out=xT[:, n, a:b], in0=PS[:, sl, 0:b - a],
                                                                     scalar=modT[:, chunk0 + n, 0:1], in1=xT[:, n, a:b],
                                                                     op0=ALU.mult, op1=ALU.add),
                             reads=[(PS, sl), modT, (xT, n)], writes=[(xT, n)])
                    else:
                        v4 = lambda ap_: ap_.rearrange("p (b t) -> p b t", t=4)
                        S.op("dve", lambda e: e.tensor_tensor(out=v4(tmpf[:, 0, 0:64]), in0=v4(PS[:, sl, 0:64]),
                                                              in1=modT[:, chunk0 + n, 1:17].unsqueeze(2).broadcast_to([128, 16, 4]),
                                                              op=ALU.mult), reads=[(PS, sl), modT], writes=[(tmpf, 0)])
                        S.op("dve", lambda e: e.tensor_tensor(out=xT[:, n, a:b], in0=xT[:, n, a:b], in1=tmpf[:, 0, 0:64], op=ALU.add),
                             reads=[(tmpf, 0), (xT, n)], writes=[(xT, n)])

                with ExitStack() as sb_:
                    xl = S.sb("xl", [128, D], F32, es=sb_)
                    for t in range(9):
                        ntok = 128 if t < 8 else 64
                        src = xown[t * 128:(t + 1) * 128, :] if t < 8 else xsm[:, :]
                        S.dma("sp", xl[0:ntok, :], src, writes=[xl])
                        for k in range(16):
                            sl = ps_next()
                            transpose_to(PS, sl, PS[:, sl, 0:ntok], xl[0:ntok, k * 128:(k + 1) * 128], C[0:ntok, IDF, 0:ntok],
                                         reads=[xl, C])
                            S.op("act", lambda e, sl=sl, k=k, t=t, ntok=ntok: e.copy(out=xT[:, k, t * 128:t * 128 + ntok],
                                                                                    in_=PS[:, sl, 0:ntok]),
                                 reads=[(PS, sl)], writes=[(xT, k)])
                    for n in range(16):
                        ws = load_w(w_out[:, n * 128:(n + 1) * 128], 16)
                        for (a, b) in T2:
                            sl = ps_next()
                            mm_group(sl, PS[:, sl, 0:b - a], [(WR.b[:, ws, k, :], MH[:, k, a:b]) for k in range(16)],
                                     reads=[(WR.b, ws), MH])
                            mod_add(n, sl, a, b, 32)
                    S.barrier()
                ckpt(82)

                def rstd_bc():
                    sls = [ps_next() for _ in T2]
                    for n in range(16):
                        S.op("act", lambda e, n=n: e.activation(out=tmpf[:, n % 2, :], in_=xT[:, n, :], func=AF.Square),
                             reads=[(xT, n)], writes=[(tmpf, n % 2)])

                        def fn(e, n=n):
                            ins = None
                            for sl_, (a, b) in zip(sls, T2):
                                ins = e.matmul(PS[:, sl_, 0:b - a], lhsT=C[:, ONESF, :], rhs=tmpf[:, n % 2, a:b], start=(n == 0),
                                               stop=(n == 15))
                            return ins
                        S.op("pe", fn, reads=[(tmpf, n % 2), C], writes=[(PS, s_) for s_ in sls])
                    for sl_, (a, b) in zip(sls, T2):
                        S.op("act", lambda e, sl_=sl_, a=a, b=b: e.activation(out=rsb[:, a:b], in_=PS[:, sl_, 0:b - a], func=AF.Ln,
                                                                             scale=1.0 / D, bias=EPS), reads=[(PS, sl_)], writes=[rsb])
                    S.op("act", lambda e: e.activation(out=rsb.ap, in_=rsb.ap, func=AF.Exp, scale=-0.5), reads=[rsb], writes=[rsb])

                rstd_bc()
                for n in range(16):
                    S.op("dve", lambda e, n=n: e.tensor_tensor(out=tmpf[:, n % 2, :], in0=xT[:, n, :], in1=rsb.ap, op=ALU.mult),
                         reads=[(xT, n), rsb], writes=[(tmpf, n % 2)])
                    S.op("act", lambda e, n=n: e.activation(out=MH[:, n, 0:1024], in_=tmpf[:, n % 2, 0:1024], func=AF.Identity,
                                                            scale=GG[:, 1, n, 0:1], bias=modT[:, 48 + n, 0:1]),
                         reads=[(tmpf, n % 2), (GG, 1), modT], writes=[(MH, n)])
                    v4 = lambda ap_: ap_.rearrange("p (b t) -> p b t", t=4)
                    S.op("dve", lambda e, n=n: e.tensor_tensor(out=v4(tmpf[:, n % 2, 1024:1088]), in0=v4(tmpf[:, n % 2, 1024:1088]),
                                                               in1=GG[:, 1, n, 1:17].unsqueeze(2).broadcast_to([128, 16, 4]),
                                                               op=ALU.mult), reads=[(tmpf, n % 2), (GG, 1)], writes=[(tmpf, n % 2)])
                    S.op("dve", lambda e, n=n: e.tensor_tensor(out=v4(MH[:, n, 1024:1088]), in0=v4(tmpf[:, n % 2, 1024:1088]),
                                                               in1=modT[:, 48 + n, 1:17].unsqueeze(2).broadcast_to([128, 16, 4]),
                                                               op=ALU.add), reads=[(tmpf, n % 2), modT], writes=[(MH, n)])
                ckpt(83)
                actq = S.sb("actq", [128, 4, 1088], BF16, nslots=4, es=s4)
                sgf = S.sb("sgf", [128, 1088], F32, nslots=3, es=s4)
                for qi in range(11):
                    for j in range(4):
                        f_ = qi * 4 + j
                        wg = load_w(w_gate_up[:, f_ * 128:(f_ + 1) * 128], 16)
                        for ti_, (a, b) in enumerate(T2):
                            sl = ps_next()
                            mm_group(sl, PS[:, sl, 0:b - a], [(WR.b[:, wg, k, :], MH[:, k, a:b]) for k in range(16)],
                                     reads=[(WR.b, wg), MH])
                            S.op("act", lambda e, sl=sl, a=a, b=b: e.activation(out=sgf[:, a:b], in_=PS[:, sl, 0:b - a], func=AF.Silu),
                                 reads=[(PS, sl)], writes=[(sgf, ti_)])
                        wu = load_w(w_gate_up[:, DFF + f_ * 128:DFF + (f_ + 1) * 128], 16)
                        for ti_, (a, b) in enumerate(T2):
                            sl = ps_next()
                            mm_group(sl, PS[:, sl, 0:b - a], [(WR.b[:, wu, k, :], MH[:, k, a:b]) for k in range(16)],
                                     reads=[(WR.b, wu), MH])
                            S.op("dve", lambda e, sl=sl, a=a, b=b, j=j: e.tensor_tensor(out=actq[:, j, a:b], in0=PS[:, sl, 0:b - a],
                                                                                       in1=sgf[:, a:b], op=ALU.mult),
                                 reads=[(PS, sl), (sgf, ti_)], writes=[(actq, j)])
                    for n in range(16):
                        wd = load_w(w_down[qi * 512:(qi + 1) * 512, n * 128:(n + 1) * 128], 4)
                        for (a, b) in T2:
                            sl = ps_next()
                            mm_group(sl, PS[:, sl, 0:b - a], [(WR.b[:, wd, j, :], actq[:, j, a:b]) for j in range(4)],
                                     reads=[(WR.b, wd), actq])
                            mod_add(n, sl, a, b, 80)
                ckpt(84)
                rstd_bc()
                for n in range(16):
                    S.op("dve", lambda e, n=n: e.scalar_tensor_tensor(out=xT[:, n, :], in0=xT[:, n, :], scalar=fgT[:, n:n + 1],
                                                                      in1=rsb.ap, op0=ALU.mult, op1=ALU.mult),
                         reads=[(xT, n), fgT, rsb], writes=[(xT, n)])
                ytok = S.sb("ytok", [128, 1, D], F32, nslots=1, es=s4)
                for t in range(9):
                    ntok = 128 if t < 8 else 64
                    ys = 0
                    for k4 in range(4):
                        sl = ps_next()

                        def ft(e, sl=sl, k4=k4, t=t, ntok=ntok):
                            ins = None
                            for j in range(4):
                                ins = e.matmul(PS[0:ntok, sl, j * 128:(j + 1) * 128], lhsT=xT[:, k4 * 4 + j, t * 128:t * 128 + ntok],
                                               rhs=C[:, IDF, :], start=True, stop=True)
                            return ins
                        S.op("pe", ft, reads=[xT, C], writes=[(PS, sl)])
                        S.op("act", lambda e, sl=sl, k4=k4, ys=ys, ntok=ntok: e.copy(out=ytok[0:ntok, ys, k4 * 512:(k4 + 1) * 512],
                                                                                    in_=PS[0:ntok, sl, :]),
                             reads=[(PS, sl)], writes=[(ytok, ys)])
                    dst = y_own[t * 128:(t + 1) * 128, :] if t < 8 else y_sm[:, :]
                    S.dma("sp", dst, ytok[0:ntok, ys, :], reads=[(ytok, ys)], is_out=True)
                S.barrier()

        try:
            body()
        except _Stop:
            pass
        S.finish()
    return nc, dbg


def prep_inputs(inp):
    f = lambda a: np.ascontiguousarray(a, dtype=np.float32)
    shared = {
        "w_ada": f(inp["w_ada"][0]), "b_ada": f(inp["b_ada"][0].reshape(96, 128)),
        "norm1_g": f(inp["norm1_g"][0].reshape(16, 128)), "w_in": f(inp["w_in"][0]),
        "conv_w": f(inp["conv_w"][0].reshape(96, 128)), "a_log": f(inp["a_log"][0].reshape(1, 8)),
        "dt_bias": f(inp["dt_bias"][0].reshape(1, 8)), "o_norm_g": f(inp["o_norm_g"][0].reshape(1, 128)),
        "pool_w": f(inp["pool_w"][0]), "pool_scale": f(inp["pool_scale"][0].reshape(8, 128)),
        "w_proj_a": f(inp["w_proj_a"][0]), "w_proj_b": f(inp["w_proj_b"][0]), "w_out": f(inp["w_out"][0]),
        "norm2_g": f(inp["norm2_g"][0].reshape(16, 128)), "w_gate_up": f(inp["w_gate_up"][0]),
        "w_down": f(inp["w_down"][0]), "final_g": f(inp["final_g"].reshape(16, 128)),
    }
    maps = []
    for c in range(8):
        b, hh = c // 2, c % 2
        sb = slice(16 * c, 16 * c + 16)
        m = dict(shared)
        xp = inp["x_prompt"][b]
        m["xpre"] = f(xp[0:1024]) if hh == 1 else np.zeros((1024, D), np.float32)
        m["xown"] = f(xp[1024 * hh:1024 * hh + 1024])
        m["xsm"] = f(inp["x_sample"][sb].reshape(64, D))
        m["cc"] = f(np.concatenate([inp["c_prompt"][b:b + 1], inp["c_sample"][sb]], axis=0))
        m["sdelta"] = f(inp["state_delta"][0, sb])
        m["sconv"] = f(inp["state_conv"][0, sb].reshape(48, 3072))
        m["spool"] = f(inp["state_pool"][0, sb].reshape(240, 1024))
        m["flag"] = np.full((128, 1), float(hh), np.float32)
        m["pos0"] = np.full((128, 1), float(1024 * hh), np.float32)
        maps.append(m)
    return maps


_NC_CACHE = {}


def kernel(**inputs):
    if "nc" not in _NC_CACHE:
        _NC_CACHE["nc"] = build_program()[0]
    nc = _NC_CACHE["nc"]
    maps = prep_inputs(inputs)
    res = run_bass_kernel_spmd(nc, maps, core_ids=list(range(8))).results
    y_prompt = np.zeros((4, 2048, D), np.float32)
    y_sample = np.zeros((128, 4, D), np.float32)
    ndp = np.zeros((1, 4, NH, 128, 128), np.float32)
    ncp = np.zeros((1, 4, 3, 3072), np.float32)
    npp = np.zeros((1, 4, 15, 1024), np.float32)
    nds = np.zeros((1, 128, NH, 128, 128), np.float32)
    ncs = np.zeros((1, 128, 3, 3072), np.float32)
    nps = np.zeros((1, 128, 15, 1024), np.float32)
    for c in range(8):
        b, hh = c // 2, c % 2
        r = res[c]
        sb = slice(16 * c, 16 * c + 16)
        y_prompt[b, 1024 * hh:1024 * hh + 1024] = r["y_own"]
        y_sample[sb] = r["y_sm"].reshape(16, 4, D)
        nds[0, sb] = r["o_delta_s"]
        ncs[0, sb] = r["o_conv"][0:48].reshape(16, 3, 3072)
        nps[0, sb] = r["o_pool"][0:240].reshape(16, 15, 1024)
        if hh == 1:
            ndp[0, b] = r["o_delta_p"]
            ncp[0, b] = r["o_conv"][48:51]
            npp[0, b] = r["o_pool"][240:255]
    return (y_prompt, y_sample, ndp, ncp, npp, nds, ncs, nps)
```

```python
import numpy as np
from contextlib import ExitStack
import concourse.bass as bass
import concourse.mybir as mybir
from concourse.bass_utils import run_bass_kernel_spmd

F32 = mybir.dt.float32
BF16 = mybir.dt.bfloat16
I32 = mybir.dt.int32
F32R = mybir.dt.float32r
NEUMANN_F32R = False
AF = mybir.ActivationFunctionType
ALU = mybir.AluOpType
AX = mybir.AxisListType

D = 2048
NH = 8
DFF = 5632
QOFF, KOFF, VOFF, ZOFF, AOFF, BOFF, POFF, GAOFF, GBOFF = 0, 1024, 2048, 3072, 4096, 4104, 4112, 5136, 7184
INW = 9232
EPS = 1e-6
NDS = 24
SAME_ENGINE_SYNC = True
KUNITS = 2
KUNITS_A = 5
RAW_ONLY_SELF = False


class Buf:
    _n = 0

    def __init__(self, ap, nslots=1, name=""):
        self.ap = ap if type(ap).__name__ == 'AP' else ap[:]
        self.n = nslots
        self.name = name
        Buf._n += 1
        self.id = Buf._n

    def __getitem__(self, k):
        return self.ap[k]


class Sched:
    def __init__(self, nc, es):
        self.nc = nc
        self.es = es
        self.E = {"pe": nc.tensor, "act": nc.scalar, "dve": nc.vector, "pool": nc.gpsimd, "sp": nc.sync}
        self.sem = {e: es.enter_context(nc.semaphore("S_" + e)) for e in ["pe", "act", "dve", "pool"]}
        self.cnt = {e: 0 for e in self.sem}
        self.dsem = [es.enter_context(nc.semaphore(f"D{i}")) for i in range(NDS)]
        self.dval = [0] * NDS
        self.dnext = 0
        self.dq = {}
        self.seen = {e: {} for e in self.E}
        self.lastw = {}
        self.readers = {}
        self.out_toks = []
        self.nops = 0
        self.off = False

    def sb(self, name, shape, dt, nslots=1, es=None):
        self._uid = getattr(self, "_uid", 0) + 1
        name = f"{name}_{self._uid}"
        return Buf((es or self.es).enter_context(self.nc.sbuf_tensor(name, shape, dt)), nslots, name)

    def barrier(self):
        if self.off:
            return
        for e in self.E:
            for j in range(NDS):
                if self.dval[j] > 0:
                    self._wait(e, ("dma", j, self.dval[j]))
            for e2 in self.sem:
                if self.cnt[e2] > 0 and e2 != e:
                    self._wait(e, ("eng", e2, self.cnt[e2]))

    def _keys(self, acc):
        ks = []
        for a in acc:
            if isinstance(a, Buf):
                a = (a, None)
            b, s = a
            if s is None:
                ks.extend((b.id, i) for i in range(b.n))
            elif isinstance(s, (list, tuple, range)):
                ks.extend((b.id, i) for i in s)
            else:
                ks.append((b.id, s))
        return ks

    def _wait(self, e, tok):
        kind, key, val = tok
        if self.seen[e].get((kind, key), 0) >= val:
            return
        sem = self.sem[key] if kind == "eng" else self.dsem[key]
        self.E[e].wait_ge(sem, val)
        self.seen[e][(kind, key)] = val

    def _deps(self, e, reads, writes):
        rk, wk = self._keys(reads), self._keys(writes)
        deps = set()
        deps2 = set()
        for k in rk:
            if k in self.lastw:
                deps.add(self.lastw[k])
        for k in wk:
            if k in self.lastw:
                deps2.add(self.lastw[k])
            for t in self.readers.get(k, ()):
                deps2.add(t)
        for t in sorted(deps | deps2, key=lambda t: (t[0], str(t[1]), t[2])):
            if t[0] == "eng" and t[1] == e:
                if e == "pe" or not SAME_ENGINE_SYNC or (RAW_ONLY_SELF and t not in deps):
                    continue
            self._wait(e, t)
        return rk, wk

    def _commit(self, tok, rk, wk):
        for k in rk:
            self.readers.setdefault(k, []).append(tok)
        for k in wk:
            self.lastw[k] = tok
            self.readers[k] = []

    def op(self, e, fn, reads=(), writes=()):
        if self.off:
            return
        rk, wk = self._deps(e, reads, writes)
        inst = fn(self.E[e])
        self.cnt[e] += 1
        inst.then_inc(self.sem[e], 1)
        self._commit(("eng", e, self.cnt[e]), rk, wk)
        self.nops += 1

    def dma(self, q, out, in_, reads=(), writes=(), is_out=False, **kw):
        if self.off:
            return
        rk, wk = self._deps(q, reads, writes)
        half = NDS // 2
        base = 0 if q == "pool" else half
        cur = self.dq.get(q, 0)
        j = base + cur
        self.dq[q] = (cur + 1) % half
        if self.dval[j] > 0:
            self._wait(q, ("dma", j, self.dval[j]))
        inst = self.E[q].dma_start(out=out, in_=in_, **kw)
        self.dval[j] += 16
        inst.then_inc(self.dsem[j], 16)
        tok = ("dma", j, self.dval[j])
        self._commit(tok, rk, wk)
        if is_out:
            self.out_toks.append(tok)
        self.nops += 1

    def finish(self):
        for j in range(NDS):
            if self.dval[j] > 0:
                self._wait("sp", ("dma", j, self.dval[j]))
        for e in self.sem:
            if self.cnt[e] > 0:
                self._wait("sp", ("eng", e, self.cnt[e]))


class Ring:
    def __init__(self, sch, name, shape, dt, n, es=None):
        self.b = sch.sb(name, [shape[0], n] + list(shape[1:]), dt, nslots=n, es=es)
        self.n = n
        self.i = 0

    def next(self):
        s = self.i % self.n
        self.i += 1
        return s


def tok_tiles(n, step=512):
    return [(a, min(a + step, n)) for a in range(0, n, step)]


def build_program(debug=None):
    nc = bass.Bass("TRN2", target_bir_lowering=False)
    dbg = {}

    def din(name, shape, dt=F32):
        return nc.dram_tensor(name, list(shape), dt, kind="ExternalInput").ap()

    def dout(name, shape, dt=F32):
        return nc.dram_tensor(name, list(shape), dt, kind="ExternalOutput").ap()

    xpre = din("xpre", [1024, D])
    xown = din("xown", [1024, D])
    xsm = din("xsm", [64, D])
    cc = din("cc", [17, D])
    sdelta = din("sdelta", [16, NH, 128, 128])
    sconv = din("sconv", [48, 3072])
    spool = din("spool", [240, 1024])
    flag = din("flag", [128, 1])
    pos0 = din("pos0", [128, 1])
    w_ada = din("w_ada", [D, 6 * D])
    b_ada = din("b_ada", [96, 128])
    norm1_g = din("norm1_g", [16, 128])
    w_in = din("w_in", [D, INW])
    conv_w = din("conv_w", [96, 128])
    a_log = din("a_log", [1, 8])
    dt_bias = din("dt_bias", [1, 8])
    o_norm_g = din("o_norm_g", [1, 128])
    pool_w = din("pool_w", [4, 256, 256])
    pool_scale = din("pool_scale", [8, 128])
    w_proj_a = din("w_proj_a", [1024, D])
    w_proj_b = din("w_proj_b", [1024, D])
    w_out = din("w_out", [D, D])
    norm2_g = din("norm2_g", [16, 128])
    w_gate_up = din("w_gate_up", [D, 2 * DFF])
    w_down = din("w_down", [DFF, D])
    final_g = din("final_g", [16, 128])

    y_own = dout("y_own", [1024, D])
    y_sm = dout("y_sm", [64, D])
    o_delta_p = dout("o_delta_p", [NH, 128, 128])
    o_conv = dout("o_conv", [51, 3072])
    o_pool = dout("o_pool", [255, 1024])
    o_delta_s = dout("o_delta_s", [16, NH, 128, 128])

    with ExitStack() as es:
        S = Sched(nc, es)
        E = S.E

        def dbg_out(name, buf, shape, dt=F32, ap=None):
            if debug is None or name not in debug:
                return
            t = dout("dbg_" + name, shape, dt)
            S.dma("sp", t, ap if ap is not None else buf.ap, reads=[buf], is_out=True)
            dbg[name] = shape

        PS = Buf(es.enter_context(nc.psum_tensor("PS", [128, 8, 512], F32)), 8, "PS")
        ps_i = [0]
        psb_i = [0]

        def ps_next():
            s = ps_i[0] % 8
            ps_i[0] += 1
            return s

        def ps_next():
            s = psb_i[0] % 8
            psb_i[0] += 1
            return s

        C = S.sb("consts_f", [128, 8, 128], F32, nslots=8)
        CB = S.sb("consts_b", [128, 2, 128], BF16, nslots=2)
        IDF, ONESF, TRI, STRICT, BTRI, BSTRICT, BLK = 0, 1, 2, 3, 5, 6, 7

        def mk_const(slot, fn):
            S.op("pool", fn, writes=[(C, slot)])

        mk_const(ONESF, lambda e: e.memset(C[:, ONESF, :], 1.0))
        for slot, cmp_, base in ((IDF, ALU.is_equal, 0), (TRI, ALU.is_ge, 0), (STRICT, ALU.is_gt, 0)):
            S.op("pool", lambda e, slot=slot: e.memset(C[:, slot, :], 1.0), writes=[(C, slot)])
            if slot == STRICT:
                S.op("pool", lambda e, slot=slot, cmp_=cmp_: e.affine_select(
                    out=C[:, slot, :], in_=C[:, slot, :], pattern=[[-1, 128]], compare_op=cmp_, fill=0.0,
                    base=0, channel_multiplier=1), reads=[(C, slot)], writes=[(C, slot)])
            else:
                S.op("pool", lambda e, slot=slot, cmp_=cmp_: e.affine_select(
                    out=C[:, slot, :], in_=C[:, slot, :], pattern=[[1, 128]], compare_op=cmp_, fill=0.0,
                    base=0, channel_multiplier=-1), reads=[(C, slot)], writes=[(C, slot)])
        blk_i = S.sb("blk_i", [128, 2, 128], I32, nslots=2)
        S.op("pool", lambda e: e.iota(blk_i[:, 0, :], pattern=[[1, 128]], base=0, channel_multiplier=0),
             writes=[(blk_i, 0)])
        S.op("pool", lambda e: e.iota(blk_i[:, 1, :], pattern=[[0, 128]], base=0, channel_multiplier=1),
             writes=[(blk_i, 1)])
        S.op("dve", lambda e: e.tensor_single_scalar(out=blk_i[:, 0, :], in_=blk_i[:, 0, :], scalar=2,
                                                      op=ALU.arith_shift_right), reads=[(blk_i, 0)], writes=[(blk_i, 0)])
        S.op("dve", lambda e: e.tensor_single_scalar(out=blk_i[:, 1, :], in_=blk_i[:, 1, :], scalar=2,
                                                      op=ALU.arith_shift_right), reads=[(blk_i, 1)], writes=[(blk_i, 1)])
        S.op("dve", lambda e: e.tensor_tensor(out=blk_i[:, 0, :], in0=blk_i[:, 0, :], in1=blk_i[:, 1, :],
                                               op=ALU.is_equal), reads=[blk_i], writes=[(blk_i, 0)])
        S.op("dve", lambda e: e.tensor_copy(out=C[:, BLK, :], in_=blk_i[:, 0, :]), reads=[(blk_i, 0)],
             writes=[(C, BLK)])
        S.op("pool", lambda e: e.tensor_tensor(out=C[:, BTRI, :], in0=C[:, TRI, :], in1=C[:, BLK, :], op=ALU.mult),
             reads=[(C, TRI), (C, BLK)], writes=[(C, BTRI)])
        S.op("pool", lambda e: e.tensor_tensor(out=C[:, BSTRICT, :], in0=C[:, STRICT, :], in1=C[:, BLK, :],
                                               op=ALU.mult), reads=[(C, STRICT), (C, BLK)], writes=[(C, BSTRICT)])
        S.op("pool", lambda e: e.tensor_copy(out=CB[:, 0, :], in_=C[:, IDF, :]), reads=[(C, IDF)], writes=[(CB, 0)])
        S.op("pool", lambda e: e.memset(CB[:, 1, :], 1.0), writes=[(CB, 1)])
        dbg_out("consts", C, [128, 8, 128])

        class _Stop(Exception):
            pass

        def ckpt(k):
            if debug is not None and f'stop{k}' in debug:
                S.off = True

        def body():
            def mm_group(slot, out_ap, terms, reads, psbuf=PS):
                def fn(e):
                    n_ = len(terms)
                    ins = None
                    for i, (l, r) in enumerate(terms):
                        ins = e.matmul(out_ap, lhsT=l, rhs=r, start=(i == 0), stop=(i == n_ - 1))
                    return ins
                S.op("pe", fn, reads=reads, writes=[(psbuf, slot)])

            def transpose_to(psbuf, slot, out_ap, in_ap, ident_ap, reads):
                S.op("pe", lambda e: e.matmul(out_ap, lhsT=in_ap, rhs=ident_ap, start=True, stop=True), reads=reads,
                     writes=[(psbuf, slot)])

            def load_vecT(name, dram, n):
                tmp = S.sb(name + "_tm", [n, 128], F32)
                dst = S.sb(name, [128, n], F32)
                S.dma("sp", tmp.ap, dram, writes=[tmp])
                sl = ps_next()
                transpose_to(PS, sl, PS[:, sl, 0:n], tmp.ap, C[0:n, IDF, 0:n], reads=[tmp, (C, IDF)])
                S.op("act", lambda e: e.copy(out=dst.ap, in_=PS[:, sl, 0:n]), reads=[(PS, sl)], writes=[dst])
                return dst

            def wsrc(dram_ap):
                return dram_ap.rearrange("(k p) c -> p k c", p=128)

            WR = Ring(S, "wr", [128, 16, 128], BF16, 4)

            def load_w(dram_ap, kc, ncols=128, ring=None):
                ring = ring or WR
                s_ = ring.next()
                src = wsrc(dram_ap)
                for k0 in range(0, kc, 4):
                    k1 = min(kc, k0 + 4)
                    S.dma("pool", ring.b[:, s_, k0:k1, 0:ncols], src[:, k0:k1, :], writes=[(ring.b, s_)])
                return s_

            b_adaT = load_vecT("b_adaT", b_ada, 96)
            g1T = load_vecT("g1T", norm1_g, 16)
            g2T = load_vecT("g2T", norm2_g, 16)
            fgT = load_vecT("fgT", final_g, 16)
            pscT = load_vecT("pscT", pool_scale, 8)
            cwT = load_vecT("cwT", conv_w, 96)
            flag_t = S.sb("flag_t", [128, 1], F32)
            S.dma("sp", flag_t.ap, flag, writes=[flag_t])
            pos0_t = S.sb("pos0_t", [128, 1], F32)
            S.dma("sp", pos0_t.ap, pos0, writes=[pos0_t])
            nea_t = S.sb("nea_t", [128, 8], F32)
            S.dma("sp", nea_t.ap, a_log[0:1, :].broadcast_to([128, 8]), writes=[nea_t])
            dtb_t = S.sb("dtb_t", [128, 8], F32)
            S.dma("sp", dtb_t.ap, dt_bias[0:1, :].broadcast_to([128, 8]), writes=[dtb_t])
            ong_t = S.sb("ong_t", [128, 128], F32)
            S.dma("sp", ong_t.ap, o_norm_g[0:1, :].broadcast_to([128, 128]), writes=[ong_t])
            S.op("act", lambda e: e.activation(out=nea_t.ap, in_=nea_t.ap, func=AF.Exp), reads=[nea_t], writes=[nea_t])
            S.op("dve", lambda e: e.tensor_scalar_mul(out=nea_t.ap, in0=nea_t.ap, scalar1=-1.0), reads=[nea_t], writes=[nea_t])

            ckpt(1)
            CM = S.sb("CM", [128, 16, 64], BF16)
            RM = S.sb("RM", [128, 16], BF16)
            scM = ExitStack()
            scM.__enter__()
            CMf = S.sb("CMf", [128, 16, 64], F32, es=scM)
            RMf = S.sb("RMf", [128, 16], F32, es=scM)
            S.op("pool", lambda e: e.memset(CMf.ap, 1.0), writes=[CMf])
            S.op("pool", lambda e: e.affine_select(out=CMf.ap, in_=CMf.ap, pattern=[[-4, 16], [1, 64]], compare_op=ALU.is_ge,
                                                   fill=0.0, base=0, channel_multiplier=0), reads=[CMf], writes=[CMf])
            S.op("pool", lambda e: e.affine_select(out=CMf.ap, in_=CMf.ap, pattern=[[4, 16], [-1, 64]], compare_op=ALU.is_ge,
                                                   fill=0.0, base=3, channel_multiplier=0), reads=[CMf], writes=[CMf])
            S.op("pool", lambda e: e.memset(RMf.ap, 1.0), writes=[RMf])
            S.op("pool", lambda e: e.affine_select(out=RMf.ap, in_=RMf.ap, pattern=[[-4, 16]], compare_op=ALU.is_ge,
                                                   fill=0.0, base=0, channel_multiplier=1), reads=[RMf], writes=[RMf])
            S.op("pool", lambda e: e.affine_select(out=RMf.ap, in_=RMf.ap, pattern=[[4, 16]], compare_op=ALU.is_ge,
                                                   fill=0.0, base=3, channel_multiplier=-1), reads=[RMf], writes=[RMf])
            S.op("pool", lambda e: e.tensor_copy(out=CM.ap, in_=CMf.ap), reads=[CMf], writes=[CM])
            S.op("pool", lambda e: e.tensor_copy(out=RM.ap, in_=RMf.ap), reads=[RMf], writes=[RM])
            S.barrier()
            scM.close()

            ckpt(2)
            modT = S.sb("modT", [128, 96, 17], F32)
            GG = S.sb("GG", [128, 2, 16, 17], F32, nslots=2)
            FV = S.sb("FV", [128, 2, 16], F32, nslots=2)
            with ExitStack() as sc:
                cc_t = S.sb("cc_t", [17, D], F32, es=sc)
                cc_b = S.sb("cc_b", [17, D], BF16, es=sc)
                cT = S.sb("cT", [128, 16, 17], BF16, es=sc)
                mtok = S.sb("mtok", [17, 2, 512], F32, nslots=2, es=sc)
                WA = Ring(S, "wa", [128, 16, 512], BF16, 3, es=sc)
                WS32 = Ring(S, "ws32", [128, 16, 512], F32, 2, es=sc)
                S.dma("sp", cc_t.ap, cc, writes=[cc_t])
                S.op("act", lambda e: e.activation(out=cc_b.ap, in_=cc_t.ap, func=AF.Silu), reads=[cc_t], writes=[cc_b])
                ckpt(20)
                for k in range(16):
                    if k == 2:
                        ckpt(202)
                    if k == 9:
                        ckpt(209)
                    sl = ps_next()
                    transpose_to(PS, sl, PS[:, sl, 0:17], cc_b[:, k * 128:(k + 1) * 128], CB[0:17, 0, 0:17],
                                 reads=[cc_b, (CB, 0)])
                    S.op("dve", lambda e, sl=sl, k=k: e.tensor_copy(out=cT[:, k, :], in_=PS[:, sl, 0:17]),
                         reads=[(PS, sl)], writes=[cT])
                ckpt(21)
                for blk in range(24):
                    if blk == 1:
                        ckpt(25)
                    if blk == 3:
                        ckpt(26)
                    if blk == 8:
                        ckpt(27)
                    if blk % 2 == 0:
                        ws = load_w(w_ada[:, blk * 512:(blk + 1) * 512], 16, 512, ring=WA)
                    else:
                        ws = WA.next()
                        s32 = WS32.next()
                        src = wsrc(w_ada[:, blk * 512:(blk + 1) * 512])
                        for k0 in range(0, 16, 4):
                            S.dma("sp", WS32.b[:, s32, k0:k0 + 4, :], src[:, k0:k0 + 4, :], writes=[(WS32.b, s32)])
                        S.op("dve", lambda e, ws=ws, s32=s32: e.tensor_copy(out=WA.b[:, ws, 0:8, :], in_=WS32.b[:, s32, 0:8, :]),
                             reads=[(WS32.b, s32)], writes=[(WA.b, ws)])
                        S.op("act", lambda e, ws=ws, s32=s32: e.copy(out=WA.b[:, ws, 8:16, :], in_=WS32.b[:, s32, 8:16, :]),
                             reads=[(WS32.b, s32)], writes=[(WA.b, ws)])
                    sl = ps_next()
                    mm_group(sl, PS[0:17, sl, :], [(cT[:, k, :], WA.b[:, ws, k, :]) for k in range(16)],
                             reads=[cT, (WA.b, ws)])
                    ms = blk % 2
                    ckpt(22)
                    S.op("act", lambda e, sl=sl, ms=ms: e.copy(out=mtok[:, ms, :], in_=PS[0:17, sl, :]),
                         reads=[(PS, sl)], writes=[(mtok, ms)])
                    ckpt(23)
                    sl2 = ps_next()

                    def tf(e, sl2=sl2, ms=ms):
                        ins = None
                        for j in range(4):
                            ins = e.matmul(PS[:, sl2, j * 32:j * 32 + 17], lhsT=mtok[:, ms, j * 128:(j + 1) * 128],
                                           rhs=C[0:17, IDF, 0:17], start=True, stop=True)
                        return ins
                    S.op("pe", tf, reads=[(mtok, ms), (C, IDF)], writes=[(PS, sl2)])
                    ckpt(24)
                    S.op("dve", lambda e, sl2=sl2, blk=blk: e.tensor_tensor(
                        out=modT[:, blk * 4:(blk + 1) * 4, :],
                        in0=PS[:, sl2, 0:128].rearrange("p (j c) -> p j c", c=32)[:, :, 0:17],
                        in1=b_adaT[:, blk * 4:(blk + 1) * 4].unsqueeze(2).broadcast_to([128, 4, 17]), op=ALU.add),
                        reads=[(PS, sl2), b_adaT], writes=[modT])
                ckpt(28)
                for i, (gT, pc) in enumerate(((g1T, 1), (g2T, 4))):
                    S.op("dve", lambda e, i=i, gT=gT, pc=pc: e.scalar_tensor_tensor(
                        out=GG[:, i], in0=modT[:, pc * 16:(pc + 1) * 16, :], scalar=1.0,
                        in1=gT.ap.unsqueeze(2).broadcast_to([128, 16, 17]), op0=ALU.add, op1=ALU.mult),
                        reads=[modT, gT], writes=[(GG, i)])
                S.op("dve", lambda e: e.tensor_scalar_mul(out=FV[:, 0, :], in0=GG[:, 0, :, 0], scalar1=flag_t[:, 0:1]),
                     reads=[(GG, 0), flag_t], writes=[(FV, 0)])
                S.op("dve", lambda e: e.tensor_scalar_mul(out=FV[:, 1, :], in0=modT[:, 0:16, 0], scalar1=flag_t[:, 0:1]),
                     reads=[modT, flag_t], writes=[(FV, 1)])
                ckpt(29)
                dbg_out("modT", modT, [128, 96, 17])
                S.barrier()

            ckpt(3)
            def make_u(xsrc, ntok, uT, col0, kind, xt, xb, st, xr):
                s_ = xr.next()
                S.dma("sp", xt[0:ntok, s_, :], xsrc, writes=[(xt, s_)])
                S.op("dve", lambda e: e.memset(st[:, s_, :], 0.0), writes=[(st, s_)])
                S.op("act", lambda e: e.activation(out=xb[0:ntok, s_, :], in_=xt[0:ntok, s_, :], func=AF.Square,
                                                   accum_out=st[0:ntok, s_, 0:1]), reads=[(xt, s_), (st, s_)],
                     writes=[(xb, s_), (st, s_)])
                S.op("dve", lambda e: e.tensor_scalar(out=st[0:ntok, s_, 1:2], in0=st[0:ntok, s_, 0:1], scalar1=1.0 / D,
                                                      scalar2=EPS, op0=ALU.mult, op1=ALU.add), reads=[(st, s_)],
                     writes=[(st, s_)])
                S.op("act", lambda e: e.activation(out=st[0:ntok, s_, 1:2], in_=st[0:ntok, s_, 1:2], func=AF.Ln),
                     reads=[(st, s_)], writes=[(st, s_)])
                S.op("act", lambda e: e.activation(out=st[0:ntok, s_, 1:2], in_=st[0:ntok, s_, 1:2], func=AF.Exp, scale=-0.5),
                     reads=[(st, s_)], writes=[(st, s_)])
                S.op("act", lambda e: e.activation(out=xb[0:ntok, s_, :], in_=xt[0:ntok, s_, :], func=AF.Copy,
                                                   scale=st[0:ntok, s_, 1:2]), reads=[(xt, s_), (st, s_)],
                     writes=[(xb, s_)])
                for k in range(16):
                    sl = ps_next()
                    transpose_to(PS, sl, PS[:, sl, 0:ntok], xb[0:ntok, s_, k * 128:(k + 1) * 128],
                                 CB[0:ntok, 0, 0:ntok], reads=[(xb, s_), (CB, 0)])
                    dst = uT[:, k, col0:col0 + ntok]
                    if kind == "own":
                        S.op("act", lambda e, sl=sl, k=k, dst=dst: e.activation(
                            out=dst, in_=PS[:, sl, 0:ntok], func=AF.Identity, scale=GG[:, 0, k, 0:1],
                            bias=modT[:, k, 0:1]), reads=[(PS, sl), (GG, 0), modT], writes=[(uT, k)])
                    elif kind == "pre":
                        S.op("act", lambda e, sl=sl, k=k, dst=dst: e.activation(
                            out=dst, in_=PS[:, sl, 0:ntok], func=AF.Identity, scale=FV[:, 0, k:k + 1],
                            bias=FV[:, 1, k:k + 1]), reads=[(PS, sl), FV], writes=[(uT, k)])
                    else:
                        dv = dst.rearrange("p (b t) -> p b t", t=4)
                        S.op("dve", lambda e, sl=sl, k=k, dv=dv: e.tensor_tensor(
                            out=dv, in0=PS[:, sl, 0:64].rearrange("p (b t) -> p b t", t=4),
                            in1=GG[:, 0, k, 1:17].unsqueeze(2).broadcast_to([128, 16, 4]), op=ALU.mult),
                            reads=[(PS, sl), (GG, 0)], writes=[(uT, k)])
                        S.op("dve", lambda e, k=k, dv=dv: e.tensor_tensor(
                            out=dv, in0=dv, in1=modT[:, k, 1:17].unsqueeze(2).broadcast_to([128, 16, 4]), op=ALU.add),
                            reads=[(uT, k), modT], writes=[(uT, k)])

            ogT = S.sb("ogT", [128, 8, 1088], BF16, nslots=8)
            ypsT = S.sb("ypsT", [128, 8, 1088], BF16, nslots=8)
            scD = ExitStack()
            scD.__enter__()
            Sst = S.sb("Sst", [128, NH, 128], F32, nslots=NH, es=scD)
            Sb = S.sb("Sb", [128, NH, 128], BF16, nslots=NH, es=scD)
            S.op("dve", lambda e: e.memset(Sst.ap, 0.0), writes=[Sst])
            S.op("dve", lambda e: e.memset(Sb.ap, 0.0), writes=[Sb])
            TF = TB = TN = SST = None
            WAB = S.sb("wab", [128, 16, 16], BF16, es=scD)
            S.dma("pool", WAB.ap, wsrc(w_in[:, AOFF:AOFF + 16]), writes=[WAB])

            def decay_prep(uT, tiles, sc):
                nt = len(tiles)
                DP = S.sb("dp", [128, 8, nt, 8], F32, nslots=8, es=sc)
                ab = S.sb("ab", [128, nt, 16], F32, es=sc)
                S.op("dve", lambda e: e.memset(DP.ap, 0.0), writes=[DP])
                S.op("dve", lambda e: e.memset(ab.ap, 0.0), writes=[ab])
                sl = ps_next()

                def fn(e):
                    ins = None
                    for ti, (c0, n, smp) in enumerate(tiles):
                        for k in range(16):
                            ins = e.matmul(PS[0:n, sl, ti * 16:(ti + 1) * 16], lhsT=uT[:, k, c0:c0 + n], rhs=WAB[:, k, :],
                                           start=(k == 0), stop=(k == 15))
                    return ins
                S.op("pe", fn, reads=[uT, WAB], writes=[(PS, sl)])
                for ti, (c0, n, smp) in enumerate(tiles):
                    S.op("act", lambda e, ti=ti, n=n: e.copy(out=ab[0:n, ti, :], in_=PS[0:n, sl, ti * 16:(ti + 1) * 16]),
                         reads=[(PS, sl)], writes=[ab])
                bc = lambda t: t.ap.unsqueeze(1).broadcast_to([128, nt, 8])
                S.op("dve", lambda e: e.tensor_tensor(out=DP[:, 0], in0=ab[:, :, 0:8], in1=bc(dtb_t), op=ALU.add),
                     reads=[ab, dtb_t], writes=[(DP, 0)])
                S.op("act", lambda e: e.activation(out=DP[:, 0], in_=DP[:, 0], func=AF.Exp), reads=[(DP, 0)], writes=[(DP, 0)])
                S.op("act", lambda e: e.activation(out=DP[:, 0], in_=DP[:, 0], func=AF.Ln, bias=1.0), reads=[(DP, 0)],
                     writes=[(DP, 0)])
                S.op("dve", lambda e: e.tensor_tensor(out=DP[:, 0], in0=DP[:, 0], in1=bc(nea_t), op=ALU.mult),
                     reads=[(DP, 0), nea_t], writes=[(DP, 0)])
                S.op("act", lambda e: e.activation(out=DP[:, 1], in_=ab[:, :, 8:16], func=AF.Sigmoid), reads=[ab],
                     writes=[(DP, 1)])
                S.op("dve", lambda e: e.tensor_scalar_mul(out=DP[:, 2], in0=DP[:, 1], scalar1=-1.0), reads=[(DP, 1)],
                     writes=[(DP, 2)])
                sl2 = ps_next()

                def fn2(e):
                    ins = None
                    for ti, (c0, n, smp) in enumerate(tiles):
                        e.matmul(PS[0:n, sl2, ti * 8:(ti + 1) * 8], lhsT=C[0:n, BTRI if smp else TRI, 0:n],
                                 rhs=DP[0:n, 0, ti, :], start=True, stop=True)
                        ins = e.matmul(PS[0:n, sl2, 256 + ti * 8:256 + (ti + 1) * 8], lhsT=C[0:n, BLK if smp else ONESF, 0:n],
                                       rhs=DP[0:n, 0, ti, :], start=True, stop=True)
                    return ins
                S.op("pe", fn2, reads=[(DP, 0), C], writes=[(PS, sl2)])
                for ti, (c0, n, smp) in enumerate(tiles):
                    S.op("act", lambda e, ti=ti, n=n: e.copy(out=DP[0:n, 3, ti, :], in_=PS[0:n, sl2, ti * 8:(ti + 1) * 8]),
                         reads=[(PS, sl2)], writes=[(DP, 3)])
                    S.op("act", lambda e, ti=ti, n=n: e.copy(out=DP[0:n, 4, ti, :],
                                                             in_=PS[0:n, sl2, 256 + ti * 8:256 + (ti + 1) * 8]),
                         reads=[(PS, sl2)], writes=[(DP, 4)])
                S.op("dve", lambda e: e.tensor_tensor(out=DP[:, 5], in0=DP[:, 4], in1=DP[:, 3], op=ALU.subtract),
                     reads=[(DP, 3), (DP, 4)], writes=[(DP, 5)])
                S.op("act", lambda e: e.activation(out=DP[:, 5], in_=DP[:, 5], func=AF.Exp), reads=[(DP, 5)], writes=[(DP, 5)])
                S.op("act", lambda e: e.activation(out=DP[:, 6], in_=DP[:, 4], func=AF.Exp), reads=[(DP, 4)], writes=[(DP, 6)])
                S.op("act", lambda e: e.activation(out=DP[:, 7], in_=DP[:, 3], func=AF.Exp), reads=[(DP, 3)], writes=[(DP, 7)])
                S.op("dve", lambda e: e.tensor_tensor(out=DP[:, 7], in0=DP[:, 7], in1=DP[:, 1], op=ALU.mult),
                     reads=[(DP, 7), (DP, 1)], writes=[(DP, 7)])
                return DP

            cur_mode = ['A']
            turn = [0]

            def unit(ui, DP, h, ti, n, smp, kT_c, qT_c, ktok_c, vtok_c, do_o, kreads, opost, sm):
                TRIc, STRc = (BTRI, BSTRICT) if smp else (TRI, STRICT)
                col = lambda j: DP[0:n, j, ti, h:h + 1]
                tf_ = lambda s_: TF.b[0:n, s_, 0:n]
                tb_ = lambda s_: TB.b[0:n, s_, 0:n]
                tnf_ = lambda s_: TN.b[0:n, s_, 0:n]
                tn_ = (lambda s_: TN.b[0:n, s_, 0:n].bitcast(F32R)) if NEUMANN_F32R else tnf_
                tr_ = tn_
                f1 = ui * 6
                S.op("dve", lambda e: e.tensor_scalar_mul(out=tf_(f1), in0=C[0:n, TRIc, 0:n], scalar1=col(0)),
                     reads=[C, (DP, 0)], writes=[(TF.b, f1)])
                sld = ps_next()
                mm_group(sld, PS[:, sld, 0:n], [(C[0:n, ONESF, :], tf_(f1))], reads=[C, (TF.b, f1)])
                f2, f3 = ui * 6 + 1, ui * 6 + 2
                S.op("dve", lambda e: e.tensor_scalar(out=tf_(f2), in0=PS[0:n, sld, 0:n], scalar1=col(3), scalar2=0.0,
                                                      op0=ALU.subtract, op1=ALU.max), reads=[(PS, sld), (DP, 3)],
                     writes=[(TF.b, f2)])
                S.op("dve", lambda e: e.tensor_scalar(out=tf_(f3), in0=PS[0:n, sld, 0:n], scalar1=col(3), scalar2=0.0,
                                                      op0=ALU.subtract, op1=ALU.min), reads=[(PS, sld), (DP, 3)],
                     writes=[(TF.b, f3)])
                if debug is not None and 'trace' in debug and cur_mode[0] == 'B' and h == 0:
                    print("UNIT", ti, "sld", sld, "f1,f2,f3", f1, f2, f3, "cnt", dict(S.cnt), flush=True)
                fed = None
                if do_o:
                    fed = ui * 6 + 3
                    S.op("dve", lambda e: e.tensor_copy(out=TF.b[:, fed, 0:n], in_=PS[:, sld, 0:n]),
                         reads=[(PS, sld)], writes=[(TF.b, fed)])
                    S.op("act", lambda e: e.activation(out=TF.b[:, fed, 0:n], in_=TF.b[:, fed, 0:n], func=AF.Exp),
                         reads=[(TF.b, fed)], writes=[(TF.b, fed)])
                yield
                if smp:
                    S.op("dve", lambda e: e.tensor_copy(
                        out=sm["cds"].ap, in_=TF.b[:, fed, 0:64].rearrange("p (b t) -> p b t", t=4)[:, :, 3]),
                        reads=[(TF.b, fed)], writes=[sm["cds"]])
                S.op("act", lambda e: e.activation(out=tf_(f2), in_=tf_(f2), func=AF.Exp, scale=-1.0), reads=[(TF.b, f2)],
                     writes=[(TF.b, f2)])
                S.op("act", lambda e: e.activation(out=tf_(f3), in_=tf_(f3), func=AF.Exp), reads=[(TF.b, f3)],
                     writes=[(TF.b, f3)])
                S.op("dve", lambda e: e.tensor_tensor(out=tf_(f2), in0=tf_(f2), in1=C[0:n, STRc, 0:n], op=ALU.mult),
                     reads=[(TF.b, f2), C], writes=[(TF.b, f2)])
                S.op("dve", lambda e: e.tensor_tensor(out=tf_(f3), in0=tf_(f3), in1=C[0:n, TRIc, 0:n], op=ALU.mult),
                     reads=[(TF.b, f3), C], writes=[(TF.b, f3)])
                yield
                slg = ps_next()
                mm_group(slg, PS[0:n, slg, 0:n], [(kT_c, kT_c)], reads=kreads)
                b1 = ui * 10
                S.op("dve", lambda e: e.scalar_tensor_tensor(out=tn_(b1), in0=PS[0:n, slg, 0:n], scalar=col(2), in1=tf_(f2),
                                                             op0=ALU.mult, op1=ALU.mult),
                     reads=[(PS, slg), (DP, 2), (TF.b, f2)], writes=[(TN.b, b1)])
                pb = ps_next()
                transpose_to(PS, pb, PS[0:n, pb, 0:n], tnf_(b1), C[0:n, IDF, 0:n], reads=[(TN.b, b1), C])
                b2 = ui * 10 + 1
                S.op("act", lambda e: e.copy(out=tn_(b2), in_=PS[0:n, pb, 0:n]), reads=[(PS, pb)], writes=[(TN.b, b2)])
                bx, by = ui * 10 + 8, ui * 10 + 9
                S.op("dve", lambda e: e.tensor_tensor(out=tn_(bx), in0=tn_(b2), in1=C[0:n, IDF, 0:n], op=ALU.add),
                     reads=[(TN.b, b2), C], writes=[(TN.b, bx)])
                S.op("dve", lambda e: e.tensor_tensor(out=tn_(by), in0=tn_(b1), in1=C[0:n, IDF, 0:n], op=ALU.add),
                     reads=[(TN.b, b1), C], writes=[(TN.b, by)])
                UD = debug is not None and 'udump' in debug and cur_mode[0] == 'A' and h == 0 and ti == 0
                if UD:
                    dbg_out("u_N", TN.b, [128, 128], F32, ap=TN.b[:, b1, :])
                    dbg_out("u_Lm", TF.b, [128, 128], F32, ap=TF.b[:, f2, :])
                    dbg_out("u_DP", DP, [128, 8 * DP.ap.shape[2] * 8], F32, ap=DP.ap.rearrange("p a b c -> p (a b c)"))
                yield
                curN, curP = b1, b2
                nsq = 1 if smp else 6
                for lv in range(nsq):
                    lastlv = lv == nsq - 1
                    slp = ps_next()
                    mm_group(slp, PS[0:n, slp, 0:n], [(tr_(curN), tr_(curP))], reads=[(TN.b, curN), (TN.b, curP)])
                    lset = ui * 10 + (2 if lv % 2 == 0 else 6)
                    bp2 = lset
                    S.op("act", lambda e, slp=slp, bp2=bp2: e.copy(out=tn_(bp2), in_=PS[0:n, slp, 0:n]), reads=[(PS, slp)],
                         writes=[(TN.b, bp2)])
                    bn2 = None
                    if not lastlv:
                        sln = ps_next()
                        mm_group(sln, PS[0:n, sln, 0:n], [(tr_(curP), tr_(curN))], reads=[(TN.b, curN), (TN.b, curP)])
                        bn2 = lset + 1
                        S.op("act", lambda e, sln=sln, bn2=bn2: e.copy(out=tn_(bn2), in_=PS[0:n, sln, 0:n]),
                             reads=[(PS, sln)], writes=[(TN.b, bn2)])
                    yield
                    slx = ps_next()
                    mm_group(slx, PS[0:n, slx, 0:n], [(tr_(by), tr_(bp2))], reads=[(TN.b, by), (TN.b, bp2)])
                    bx2 = lset + 2
                    S.op("dve", lambda e, slx=slx, bx2=bx2, bx=bx: e.tensor_tensor(out=tn_(bx2), in0=PS[0:n, slx, 0:n],
                                                                                  in1=tn_(bx), op=ALU.add),
                         reads=[(PS, slx), (TN.b, bx)], writes=[(TN.b, bx2)])
                    if not lastlv:
                        sly = ps_next()
                        mm_group(sly, PS[0:n, sly, 0:n], [(tr_(bx), tr_(bn2))], reads=[(TN.b, bx), (TN.b, bn2)])
                        by2 = lset + 3
                        S.op("dve", lambda e, sly=sly, by2=by2, by=by: e.tensor_tensor(out=tn_(by2), in0=PS[0:n, sly, 0:n],
                                                                                      in1=tn_(by), op=ALU.add),
                             reads=[(PS, sly), (TN.b, by)], writes=[(TN.b, by2)])
                        by, curN = by2, bn2
                    bx, curP = bx2, bp2
                    yield
                if UD:
                    dbg_out("u_X", TN.b, [128, 128], F32, ap=TN.b[:, bx, :])
                yield
                bx16 = ui * 8
                S.op("act", lambda e: e.copy(out=tb_(bx16), in_=tn_(bx)), reads=[(TN.b, bx)], writes=[(TB.b, bx16)])
                bx = bx16
                bvb, bkb, bkt = ui * 8 + 1, ui * 8 + 2, ui * 8 + 3
                S.op("dve", lambda e: e.tensor_scalar_mul(out=TB.b[0:n, bvb, :], in0=vtok_c, scalar1=col(1)),
                     reads=kreads + [(DP, 1)], writes=[(TB.b, bvb)])
                S.op("dve", lambda e: e.tensor_scalar_mul(out=TB.b[0:n, bkb, :], in0=ktok_c, scalar1=col(7)),
                     reads=kreads + [(DP, 7)], writes=[(TB.b, bkb)])
                S.op("dve", lambda e: e.tensor_scalar_mul(out=TB.b[0:n, bkt, :], in0=ktok_c, scalar1=col(5)),
                     reads=kreads + [(DP, 5)], writes=[(TB.b, bkt)])
                slu = ps_next()
                mm_group(slu, PS[0:n, slu, 0:128], [(tb_(bx), TB.b[0:n, bvb, :])], reads=[(TB.b, bx), (TB.b, bvb)])
                fub = ui * 6 + 4
                S.op("act", lambda e: e.copy(out=TF.b[0:n, fub, :], in_=PS[0:n, slu, 0:128]), reads=[(PS, slu)],
                     writes=[(TF.b, fub)])
                slw = ps_next()
                mm_group(slw, PS[:, slw, 0:n], [(TB.b[0:n, bkb, :], tb_(bx))], reads=[(TB.b, bx), (TB.b, bkb)])
                bwd = ui * 8 + 4
                S.op("act", lambda e: e.copy(out=TB.b[:, bwd, 0:n], in_=PS[:, slw, 0:n]), reads=[(PS, slw)],
                     writes=[(TB.b, bwd)])
                bqd = bqk = None
                if do_o:
                    bqd, bqk = ui * 8 + 5, ui * 8 + 6
                    S.op("dve", lambda e: e.tensor_tensor(out=TB.b[:, bqd, 0:n], in0=qT_c, in1=TF.b[:, fed, 0:n], op=ALU.mult),
                         reads=kreads + [(TF.b, fed)], writes=[(TB.b, bqd)])
                    slq = ps_next()
                    mm_group(slq, PS[0:n, slq, 0:n], [(kT_c, qT_c)], reads=kreads)
                    S.op("dve", lambda e: e.tensor_tensor(out=tb_(bqk), in0=PS[0:n, slq, 0:n], in1=tf_(f3), op=ALU.mult),
                         reads=[(PS, slq), (TF.b, f3)], writes=[(TB.b, bqk)])
                if UD:
                    dbg_out("u_ub", TF.b, [128, 128], F32, ap=TF.b[:, fub, :])
                    dbg_out("u_wd", TB.b, [128, 128], BF16, ap=TB.b[:, bwd, :])
                    dbg_out("u_kt", TB.b, [128, 128], BF16, ap=TB.b[:, bkt, :])
                yield
                bu = ui * 8 + 7
                if not smp:
                    while turn[0] != ti:
                        yield
                    sl = ps_next()
                    mm_group(sl, PS[0:n, sl, 0:128], [(TB.b[:, bwd, 0:n], Sb[:, h, :])], reads=[(TB.b, bwd), (Sb, h)])
                    S.op("dve", lambda e: e.tensor_tensor(out=TB.b[0:n, bu, :], in0=TF.b[0:n, fub, :], in1=PS[0:n, sl, 0:128],
                                                          op=ALU.subtract), reads=[(PS, sl), (TF.b, fub)], writes=[(TB.b, bu)])
                    yield
                    if do_o:
                        slo = ps_next()
                        mm_group(slo, PS[0:n, slo, 0:128], [(TB.b[:, bqd, 0:n], Sb[:, h, :]), (tb_(bqk), TB.b[0:n, bu, :])],
                                 reads=[(TB.b, bqd), (Sb, h), (TB.b, bqk), (TB.b, bu)])
                        opost(slo)
                    yield
                    slk = ps_next()
                    mm_group(slk, PS[:, slk, 0:128], [(TB.b[0:n, bkt, :], TB.b[0:n, bu, :])], reads=[(TB.b, bkt), (TB.b, bu)])
                    S.op("dve", lambda e: e.scalar_tensor_tensor(out=Sb[:, h, :], in0=Sst[:, h, :], scalar=DP[:, 6, ti, h:h + 1],
                                                                 in1=PS[:, slk, 0:128], op0=ALU.mult, op1=ALU.add),
                         reads=[(Sst, h), (DP, 6), (PS, slk)], writes=[(Sb, h)])
                    S.op("dve", lambda e: e.scalar_tensor_tensor(out=Sst[:, h, :], in0=Sst[:, h, :], scalar=DP[:, 6, ti, h:h + 1],
                                                                 in1=PS[:, slk, 0:128], op0=ALU.mult, op1=ALU.add),
                         reads=[(Sst, h), (DP, 6), (PS, slk)], writes=[(Sst, h)])
                    turn[0] += 1
                    if UD:
                        dbg_out("u_S", Sst, [128, 128], F32, ap=Sst[:, 0, :])
                        dbg_out("u_u", TB.b, [128, 128], BF16, ap=TB.b[:, bu, :])
                else:
                    Ss, Ssb, wdm, qdm, ktm, cds = sm["Ss"], sm["Ssb"], sm["wdm"], sm["qdm"], sm["ktm"], sm["cds"]
                    S.op("dve", lambda e: e.tensor_tensor(out=wdm.ap, in0=TB.b[:, bwd, 0:64].unsqueeze(1).broadcast_to([128, 16, 64]),
                                                          in1=CM.ap, op=ALU.mult), reads=[(TB.b, bwd), CM], writes=[wdm])
                    S.op("dve", lambda e: e.tensor_tensor(out=qdm.ap, in0=TB.b[:, bqd, 0:64].unsqueeze(1).broadcast_to([128, 16, 64]),
                                                          in1=CM.ap, op=ALU.mult), reads=[(TB.b, bqd), CM], writes=[qdm])
                    S.op("dve", lambda e: e.tensor_tensor(out=ktm[0:64], in0=TB.b[0:64, bkt, :].unsqueeze(1).broadcast_to([64, 16, 128]),
                                                          in1=RM[0:64, :].unsqueeze(2).broadcast_to([64, 16, 128]), op=ALU.mult),
                         reads=[(TB.b, bkt), RM], writes=[ktm])
                    yield
                    sl = ps_next()
                    mm_group(sl, PS[0:64, sl, 0:128], [(wdm[:, b_, :], Ssb[:, b_, :]) for b_ in range(16)], reads=[wdm, Ssb])
                    S.op("dve", lambda e: e.tensor_tensor(out=TB.b[0:64, bu, :], in0=TF.b[0:64, fub, :], in1=PS[0:64, sl, 0:128],
                                                          op=ALU.subtract), reads=[(PS, sl), (TF.b, fub)], writes=[(TB.b, bu)])
                    yield
                    slo = ps_next()
                    mm_group(slo, PS[0:64, slo, 0:128],
                             [(qdm[:, b_, :], Ssb[:, b_, :]) for b_ in range(16)] + [(tb_(bqk), TB.b[0:64, bu, :])],
                             reads=[qdm, Ssb, (TB.b, bqk), (TB.b, bu)])
                    opost(slo)
                    for g4 in range(4):
                        yield
                        slk = ps_next()

                        def fk(e, slk=slk, g4=g4):
                            ins = None
                            for j in range(4):
                                ins = e.matmul(PS[:, slk, j * 128:(j + 1) * 128], lhsT=ktm[0:64, 4 * g4 + j, :],
                                               rhs=TB.b[0:64, bu, :], start=True, stop=True)
                            return ins
                        S.op("pe", fk, reads=[ktm, (TB.b, bu)], writes=[(PS, slk)])
                        sv = Ss[:, 4 * g4:4 * g4 + 4, :]
                        S.op("dve", lambda e, sv=sv, g4=g4: e.tensor_tensor(
                            out=sv, in0=sv, in1=cds[:, 4 * g4:4 * g4 + 4].unsqueeze(2).broadcast_to([128, 4, 128]), op=ALU.mult),
                            reads=[Ss, cds], writes=[Ss])
                        S.op("dve", lambda e, sv=sv, slk=slk: e.tensor_tensor(
                            out=sv, in0=sv, in1=PS[:, slk, :].rearrange("p (j v) -> p j v", v=128), op=ALU.add),
                            reads=[Ss, (PS, slk)], writes=[Ss])
                    S.dma("sp", o_delta_s[:, h].rearrange("b k v -> k b v"), Ss.ap, reads=[Ss], is_out=True)

            sconvT = S.sb("sconvT", [128, 24, 48], F32, es=scD)
            nconvT = S.sb("nconvT", [128, 24, 51], F32, nslots=24, es=scD)

            def head_pass(mode, uT, DP, sc):
                nonlocal TF, TB, TN, SST
                cur_mode[0] = mode
                KU_ = KUNITS if mode == "B" else KUNITS_A
                TF = Ring(S, "tf", [128, 128], F32, 6 * KU_, es=sc)
                TB = Ring(S, "tb", [128, 128], BF16, 8 * KU_, es=sc)
                TN = Ring(S, "tn", [128, 128], F32, 10 * KU_, es=sc)
                SST = Ring(S, "sst", [128, 4], F32, KU_, es=sc)
                UDH = debug is not None and 'udump' in debug and mode == 'A'
                B_ = mode == "B"
                L = 1152 if B_ else 1024
                o0 = 128 if B_ else 0
                W_ = L + (112 if B_ else 0)
                RW = 3 + W_
                stg = S.sb("stg" + mode, [128, 1, RW], F32, nslots=1, es=sc)
                yrow = S.sb("yrow" + mode, [128, W_], F32, es=sc)
                crow = S.sb("crow" + mode, [128, 2, W_], BF16, nslots=2, es=sc)
                cmap = {0: 0, 1: 0, 2: 1}
                nrmP = [S.sb("nrm" + mode, [128, 2, W_], BF16, nslots=2, es=sc) for _ in range(2)]
                ktokP = [S.sb("ktok" + mode, [128, 9, 128], BF16, nslots=9, es=sc) for _ in range(2)]
                vtokP = [S.sb("vtok" + mode, [128, 9, 128], BF16, nslots=9, es=sc) for _ in range(2)]
                cmpP = [None, None]
                wzs = {}
                sm = None
                if B_:
                    cmpP = [S.sb("cmp", [128, 3, 64], BF16, nslots=3, es=sc) for _ in range(2)]
                    gz = S.sb("gz", [128, KUNITS, 128], F32, nslots=KUNITS, es=sc)
                    ogtok = S.sb("ogtok", [128, KUNITS, 128], BF16, nslots=KUNITS, es=sc)
                    sm = {"Ss": S.sb("Ss", [128, 16, 128], F32, es=sc), "Ssb": S.sb("Ssb", [128, 16, 128], BF16, es=sc),
                          "wdm": S.sb("wdm", [128, 16, 64], BF16, es=sc), "qdm": S.sb("qdm", [128, 16, 64], BF16, es=sc),
                          "ktm": S.sb("ktm", [128, 16, 128], BF16, es=sc), "cds": S.sb("cds", [128, 16], F32, es=sc)}
                WZ = S.sb("WZ", [128, 16, 128], BF16, es=sc) if B_ else None
                S.op("dve", lambda e: e.memset(stg.ap, 0.0), writes=[stg])
                comps = [("q", 0, QOFF), ("k", 1, KOFF), ("v", 2, VOFF)] if B_ else [("k", 1, KOFF), ("v", 2, VOFF)]
                ptiles = tok_tiles(L)
                ext = lambda ci: stg[:, 0, 3 + L:3 + L + 112].rearrange("p (b j) -> p b j", j=7)
                def prep(h):
                    nrm, ktok, vtok, cmp_ = nrmP[h % 2], ktokP[h % 2], vtokP[h % 2], cmpP[h % 2]
                    for (nm, ci, off) in comps:
                        chn = ci * 8 + h
                        ws = load_w(w_in[:, off + h * 128:off + (h + 1) * 128], 16)
                        for (a, b) in ptiles:
                            sl = ps_next()
                            mm_group(sl, PS[:, sl, 0:b - a], [(WR.b[:, ws, k, :], uT[:, k, a:b]) for k in range(16)],
                                     reads=[(WR.b, ws), uT])
                            S.op("act", lambda e, sl=sl, a=a, b=b, ci=ci: e.copy(out=stg[:, 0, 3 + a:3 + b], in_=PS[:, sl, 0:b - a]),
                                 reads=[(PS, sl)], writes=[(stg, 0)])
                            yield
                        if B_:
                            sl = ps_next()
                            mm_group(sl, PS[:, sl, 0:64], [(WR.b[:, ws, k, :], uT[:, k, 1152:1216]) for k in range(16)],
                                     reads=[(WR.b, ws), uT])
                            S.op("act", lambda e, sl=sl, ci=ci: e.copy(out=ext(ci)[:, :, 3:7],
                                                                       in_=PS[:, sl, 0:64].rearrange("p (b t) -> p b t", t=4)),
                                 reads=[(PS, sl)], writes=[(stg, 0)])
                            S.op("dve", lambda e, ci=ci, chn=chn: e.tensor_copy(
                                out=ext(ci)[:, :, 0:3], in_=sconvT[:, chn, :].rearrange("p (b j) -> p b j", j=3)),
                                reads=[sconvT], writes=[(stg, 0)])
                            S.op("dve", lambda e, ci=ci, chn=chn: e.tensor_copy(
                                out=nconvT[:, chn, 0:48].rearrange("p (b j) -> p b j", j=3), in_=ext(ci)[:, :, 4:7]),
                                reads=[(stg, 0)], writes=[(nconvT, chn)])
                            S.op("dve", lambda e, ci=ci, chn=chn: e.tensor_copy(out=nconvT[:, chn, 48:51], in_=stg[:, 0, L:L + 3]),
                                 reads=[(stg, 0)], writes=[(nconvT, chn)])
                        S.op("dve", lambda e, ci=ci, chn=chn: e.tensor_scalar_mul(out=yrow.ap, in0=stg[:, 0, 0:W_],
                                                                                 scalar1=cwT[:, chn:chn + 1]),
                             reads=[(stg, 0), cwT], writes=[yrow])
                        for j in range(1, 4):
                            S.op("dve", lambda e, ci=ci, chn=chn, j=j: e.scalar_tensor_tensor(
                                out=yrow.ap, in0=stg[:, 0, j:j + W_], scalar=cwT[:, j * 24 + chn:j * 24 + chn + 1], in1=yrow.ap,
                                op0=ALU.mult, op1=ALU.add), reads=[(stg, 0), cwT, yrow], writes=[yrow])
                            yield
                        S.op("act", lambda e, ci=ci: e.activation(out=crow[:, cmap[ci], :], in_=yrow.ap, func=AF.Silu), reads=[yrow],
                             writes=[(crow, cmap[ci])])
                        yield
                        if nm in ("q", "k"):
                            S.op("act", lambda e, ci=ci: e.activation(out=nrm[:, ci, :], in_=crow[:, cmap[ci], :], func=AF.Square),
                                 reads=[(crow, cmap[ci])], writes=[(nrm, ci)])
                            for (a, b) in tok_tiles(W_):
                                sl = ps_next()
                                mm_group(sl, PS[:, sl, 0:b - a], [(CB[:, 1, :], nrm[:, ci, a:b])], reads=[CB, (nrm, ci)])
                                S.op("act", lambda e, sl=sl, a=a, b=b: e.activation(
                                    out=yrow[:, a:b], in_=PS[:, sl, 0:b - a], func=AF.Ln, bias=EPS),
                                    reads=[(PS, sl)], writes=[yrow])
                                S.op("act", lambda e, a=a, b=b: e.activation(
                                    out=yrow[:, a:b], in_=yrow[:, a:b], func=AF.Exp, scale=-0.5),
                                    reads=[yrow], writes=[yrow])
                                yield
                            S.op("dve", lambda e, ci=ci, nm=nm: e.scalar_tensor_tensor(
                                out=nrm[:, ci, :], in0=crow[:, cmap[ci], :], scalar=(128.0 ** -0.5 if nm == "q" else 1.0), in1=yrow.ap,
                                op0=ALU.mult, op1=ALU.mult), reads=[(crow, cmap[ci]), yrow], writes=[(nrm, ci)])
                    ckpt(mode + '61')
                    for c in range(8):
                        kc = o0 + 128 * c
                        for (src, dstb) in ((nrm[:, 1, kc:kc + 128], ktok), (crow[:, 1, kc:kc + 128], vtok)):
                            pb = ps_next()
                            transpose_to(PS, pb, PS[:, pb, 0:128], src, CB[:, 0, :], reads=[(nrm, 1), (crow, 1), CB])
                            S.op("act", lambda e, pb=pb, dstb=dstb, c=c: e.copy(out=dstb[:, c, :], in_=PS[:, pb, 0:128]),
                                 reads=[(PS, pb)], writes=[(dstb, c)])
                        yield
                    if B_:
                        for i, src in enumerate((nrm[:, 0, :], nrm[:, 1, :], crow[:, 1, :])):
                            S.op("dve", lambda e, i=i, src=src: e.tensor_copy(
                                out=cmp_[:, i, :].rearrange("p (b t) -> p b t", t=4),
                                in_=src[:, L:L + 112].rearrange("p (b j) -> p b j", j=7)[:, :, 3:7]),
                                reads=[(nrm, 0), (nrm, 1), (crow, 1)], writes=[(cmp_, i)])
                        for (i, dstb) in ((1, ktok), (2, vtok)):
                            pb = ps_next()
                            transpose_to(PS, pb, PS[0:64, pb, 0:128], cmp_[:, i, :], CB[:, 0, :], reads=[(cmp_, i), CB])
                            S.op("act", lambda e, pb=pb, dstb=dstb: e.copy(out=dstb[0:64, 8, :], in_=PS[0:64, pb, 0:128]),
                                 reads=[(PS, pb)], writes=[(dstb, 8)])

                    yield

                for _g in prep(0):
                    pass
                for h in range(NH):
                    nrm, ktok, vtok, cmp_ = nrmP[h % 2], ktokP[h % 2], vtokP[h % 2], cmpP[h % 2]
                    if B_:
                        zsrc = wsrc(w_in[:, ZOFF + h * 128:ZOFF + (h + 1) * 128])
                        for k0 in range(0, 16, 4):
                            S.dma("pool", WZ[:, k0:k0 + 4, :], zsrc[:, k0:k0 + 4, :], writes=[WZ])
                    nxt = prep(h + 1) if h + 1 < NH else None

                    def mk_opost(ui, n, ucol, ogcol, h=h):
                        def opost(slo):
                            gs = ui
                            zsl = ps_next()
                            mm_group(zsl, PS[0:n, zsl, 0:128], [(uT[:, k, ucol:ucol + n], WZ[:, k, :]) for k in range(16)],
                                     reads=[uT, WZ])
                            S.op("act", lambda e: e.activation(out=gz[0:n, gs, :], in_=PS[0:n, zsl, 0:128], func=AF.Silu),
                                 reads=[(PS, zsl)], writes=[(gz, gs)])
                            S.op("dve", lambda e: e.tensor_tensor(out=gz[0:n, gs, :], in0=gz[0:n, gs, :], in1=ong_t[0:n, :], op=ALU.mult),
                                 reads=[(gz, gs), ong_t], writes=[(gz, gs)])
                            ss = ui
                            fj = ui * 6 + 5
                            S.op("dve", lambda e: e.memset(SST.b[:, ss, :], 0.0), writes=[(SST.b, ss)])
                            S.op("act", lambda e: e.activation(out=TF.b[0:n, fj, :], in_=PS[0:n, slo, 0:128], func=AF.Square,
                                                               accum_out=SST.b[0:n, ss, 0:1]),
                                 reads=[(PS, slo), (SST.b, ss)], writes=[(TF.b, fj), (SST.b, ss)])
                            S.op("dve", lambda e: e.tensor_scalar(out=SST.b[0:n, ss, 1:2], in0=SST.b[0:n, ss, 0:1], scalar1=1.0 / 128,
                                                                  scalar2=EPS, op0=ALU.mult, op1=ALU.add), reads=[(SST.b, ss)],
                                 writes=[(SST.b, ss)])
                            S.op("act", lambda e: e.activation(out=SST.b[0:n, ss, 1:2], in_=SST.b[0:n, ss, 1:2], func=AF.Ln),
                                 reads=[(SST.b, ss)], writes=[(SST.b, ss)])
                            S.op("act", lambda e: e.activation(out=SST.b[0:n, ss, 1:2], in_=SST.b[0:n, ss, 1:2], func=AF.Exp,
                                                               scale=-0.5), reads=[(SST.b, ss)], writes=[(SST.b, ss)])
                            S.op("dve", lambda e: e.scalar_tensor_tensor(out=ogtok[0:n, gs, :], in0=PS[0:n, slo, 0:128],
                                                                         scalar=SST.b[0:n, ss, 1:2], in1=gz[0:n, gs, :],
                                                                         op0=ALU.mult, op1=ALU.mult),
                                 reads=[(PS, slo), (SST.b, ss), (gz, gs)], writes=[(ogtok, gs)])
                            pb = ps_next()
                            transpose_to(PS, pb, PS[:, pb, 0:n], ogtok[0:n, gs, :], CB[0:n, 0, 0:n], reads=[(ogtok, gs), CB])
                            S.op("act", lambda e: e.copy(out=ogT[:, h, ogcol:ogcol + n], in_=PS[:, pb, 0:n]), reads=[(PS, pb)],
                                 writes=[(ogT, h)])
                        return opost
                    kreads = [(nrm, 0), (nrm, 1), ktok, vtok]
                    ckpt(mode + '62')
                    pending = []
                    for c in range(8):
                        kc = o0 + 128 * c
                        pending.append(dict(c=c, n=128, smp=False, kT=nrm[:, 1, kc:kc + 128], qT=nrm[:, 0, kc:kc + 128] if B_ else None,
                                            ktok=ktok[:, c, :], vtok=vtok[:, c, :], ucol=kc, ogcol=128 * c, kr=kreads))
                    if B_:
                        pending.append(dict(c=8, n=64, smp=True, kT=cmp_[:, 1, :], qT=cmp_[:, 0, :], ktok=ktok[0:64, 8, :],
                                            vtok=vtok[0:64, 8, :], ucol=1152, ogcol=1024, kr=kreads + [cmp_]))
                    gens = []
                    turn[0] = 0
                    free_ids = list(range(KU_))
                    while pending or gens:
                        if pending and free_ids:
                            a_ = pending.pop(0)
                            ui = free_ids.pop(0)
                            if a_["smp"]:
                                S.dma("sp", sm["Ss"].ap, sdelta[:, h].rearrange("b k v -> k b v"), writes=[sm["Ss"]])
                                S.op("act", lambda e: e.copy(out=sm["Ssb"].ap, in_=sm["Ss"].ap), reads=[sm["Ss"]], writes=[sm["Ssb"]])
                            g_ = unit(ui, DP, h, a_["c"], a_["n"], a_["smp"], a_["kT"], a_["qT"], a_["ktok"], a_["vtok"], B_, a_["kr"],
                                      mk_opost(ui, a_["n"], a_["ucol"], a_["ogcol"]) if B_ else None, sm if a_["smp"] else None)
                            gens.append((g_, ui))
                        for (g_, ui) in list(gens):
                            try:
                                next(g_)
                            except StopIteration:
                                gens.remove((g_, ui))
                                free_ids.append(ui)
                        if nxt is not None:
                            try:
                                next(nxt)
                            except StopIteration:
                                nxt = None
                    if nxt is not None:
                        for _g in nxt:
                            pass

            ckpt(4)
            with ExitStack() as sc:
                uTA = S.sb("uTA", [128, 16, 1024], BF16, nslots=16, es=sc)
                with ExitStack() as scu:
                    xt = S.sb("xtA", [128, 2, D], F32, nslots=2, es=scu)
                    xb = S.sb("xbA", [128, 2, D], BF16, nslots=2, es=scu)
                    st = S.sb("stA", [128, 2, 2], F32, nslots=2, es=scu)
                    xr = Ring.__new__(Ring)
                    xr.n, xr.i = 2, 0
                    for t in range(8):
                        make_u(xpre[t * 128:(t + 1) * 128, :], 128, uTA, t * 128, "pre", xt, xb, st, xr)
                    S.barrier()
                dbg_out("uTA", uTA, [128, 16 * 1024], BF16, ap=uTA.ap.rearrange("p k t -> p (k t)"))
                ckpt(5)
                DPA = decay_prep(uTA, [(128 * c, 128, False) for c in range(8)], sc)
                ckpt(6)
                head_pass("A", uTA, DPA, sc)
                dbg_out("Smid", Sst, [128, 1024], ap=Sst.ap.rearrange("p h v -> p (h v)"))
                S.barrier()
            ckpt(7)
            with ExitStack() as sc:
                uTB = S.sb("uTB", [128, 16, 1216], BF16, nslots=16, es=sc)
                with ExitStack() as sc2:
                    xt = S.sb("xtB", [128, 2, D], F32, nslots=2, es=sc2)
                    xb = S.sb("xbB", [128, 2, D], BF16, nslots=2, es=sc2)
                    st = S.sb("stB", [128, 2, 2], F32, nslots=2, es=sc2)
                    xr = Ring.__new__(Ring)
                    xr.n, xr.i = 2, 0
                    make_u(xpre[896:1024, :], 128, uTB, 0, "pre", xt, xb, st, xr)
                    for t in range(8):
                        make_u(xown[t * 128:(t + 1) * 128, :], 128, uTB, 128 + t * 128, "own", xt, xb, st, xr)
                    make_u(xsm[:, :], 64, uTB, 1152, "smp", xt, xb, st, xr)
                    sct = S.sb("sct", [48, 3072], F32, es=sc2)
                    S.dma("sp", sct.ap, sconv, writes=[sct])
                    for chn in range(24):
                        sl = ps_next()
                        transpose_to(PS, sl, PS[:, sl, 0:48], sct[:, chn * 128:(chn + 1) * 128], C[0:48, IDF, 0:48],
                                     reads=[sct, C])
                        S.op("act", lambda e, sl=sl, chn=chn: e.copy(out=sconvT[:, chn, :], in_=PS[:, sl, 0:48]),
                             reads=[(PS, sl)], writes=[sconvT])
                    S.barrier()
                ckpt(72)
                DPB = decay_prep(uTB, [(128 + 128 * c, 128, False) for c in range(8)] + [(1152, 64, True)], sc)
                ckpt(73)
                if debug is not None and 'shift' in debug:
                    for _ in range(3):
                        S.op("act", lambda e: e.copy(out=flag_t.ap, in_=flag_t.ap), reads=[flag_t], writes=[flag_t])
                with ExitStack() as scH:
                    head_pass("B", uTB, DPB, scH)
                    S.barrier()
                ckpt(74)

                with ExitStack() as scP:
                    PW = 16 + 1152 + 304
                    xrow = S.sb("xrow", [128, 4, PW], F32, nslots=4, es=scP)
                    hist = S.sb("hist", [128, 2, 1024], F32, nslots=2, es=scP)
                    histT = S.sb("histT", [128, 240], F32, es=scP)
                    pwt = S.sb("pwt", [128, 4, 2, 256], BF16, es=scP)
                    ypl = S.sb("ypl", [128, 2, 1088], BF16, nslots=2, es=scP)
                    npoolT = S.sb("npoolT", [128, 8, 256], F32, nslots=8, es=scP)
                    invc = S.sb("invc", [128, 4, 16], F32, es=scP)
                    io_i = S.sb("io_i", [128, 16], I32, es=scP)
                    t16 = S.sb("t16", [128, 16], F32, es=scP)
                    otok = hist
                    S.dma("sp", hist[:, 0, :], spool[0:128, :], writes=[(hist, 0)])
                    S.dma("sp", hist[0:112, 1, :], spool[128:240, :], writes=[(hist, 1)])
                    for g in range(4):
                        S.dma("pool", pwt[:, g], pool_w[g].rearrange("(c p) d -> p c d", p=128), writes=[pwt])
                    S.op("dve", lambda e: e.memset(xrow.ap, 0.0), writes=[xrow])
                    S.op("dve", lambda e: e.memset(npoolT.ap, 0.0), writes=[npoolT])
                    S.op("pool", lambda e: e.iota(io_i.ap, pattern=[[1, 16]], base=1, channel_multiplier=0), writes=[io_i])
                    S.op("dve", lambda e: e.tensor_copy(out=invc[:, 0, :], in_=io_i.ap), reads=[io_i], writes=[invc])
                    S.op("dve", lambda e: e.tensor_scalar_add(out=invc[:, 0, :], in0=invc[:, 0, :], scalar1=pos0_t[:, 0:1]),
                         reads=[invc, pos0_t], writes=[invc])
                    for g in (3, 2, 1, 0):
                        S.op("dve", lambda e, g=g: e.tensor_scalar_min(out=invc[:, g, :], in0=invc[:, 0, :],
                                                                       scalar1=float((2, 4, 8, 16)[g])), reads=[invc], writes=[invc])
                    S.op("dve", lambda e: e.reciprocal(out=invc.ap, in_=invc.ap), reads=[invc], writes=[invc])
                    xe = lambda sl_: xrow[:, sl_, 16 + 1152:PW].rearrange("p (b j) -> p b j", j=19)
                    for g in range(4):
                        wwin = (2, 4, 8, 16)[g]
                        for c2 in range(2):
                            ch = 2 * g + c2
                            xs = 0 if ch % 2 == 0 else 3
                            ws = load_w(w_in[:, POFF + ch * 128:POFF + (ch + 1) * 128], 16)
                            for (a, b) in tok_tiles(1152):
                                sl = ps_next()
                                mm_group(sl, PS[:, sl, 0:b - a], [(WR.b[:, ws, k, :], uTB[:, k, a:b]) for k in range(16)],
                                         reads=[(WR.b, ws), uTB])
                                S.op("act", lambda e, sl=sl, a=a, b=b: e.copy(out=xrow[:, xs, 16 + a:16 + b], in_=PS[:, sl, 0:b - a]),
                                     reads=[(PS, sl)], writes=[(xrow, xs)])
                            sl = ps_next()
                            mm_group(sl, PS[:, sl, 0:64], [(WR.b[:, ws, k, :], uTB[:, k, 1152:1216]) for k in range(16)],
                                     reads=[(WR.b, ws), uTB])
                            S.op("act", lambda e, sl=sl: e.copy(out=xe(xs)[:, :, 15:19],
                                                               in_=PS[:, sl, 0:64].rearrange("p (b t) -> p b t", t=4)),
                                 reads=[(PS, sl)], writes=[(xrow, xs)])
                            for t2, rows in ((0, 128), (1, 112)):
                                sl = ps_next()
                                transpose_to(PS, sl, PS[:, sl, 0:rows], hist[0:rows, t2, ch * 128:(ch + 1) * 128],
                                             C[0:rows, IDF, 0:rows], reads=[(hist, t2), C])
                                S.op("act", lambda e, sl=sl, t2=t2, rows=rows: e.copy(out=histT[:, t2 * 128:t2 * 128 + rows],
                                                                                    in_=PS[:, sl, 0:rows]),
                                     reads=[(PS, sl)], writes=[histT])
                            S.op("dve", lambda e: e.tensor_copy(out=xe(xs)[:, :, 0:15],
                                                                in_=histT.ap.rearrange("p (b j) -> p b j", j=15)),
                                 reads=[histT], writes=[(xrow, xs)])
                            S.op("dve", lambda e, ch=ch: e.tensor_copy(out=npoolT[:, ch, 0:240].rearrange("p (b j) -> p b j", j=15),
                                                                      in_=xe(xs)[:, :, 4:19]), reads=[(xrow, xs)], writes=[(npoolT, ch)])
                            S.op("dve", lambda e, ch=ch: e.tensor_copy(out=npoolT[:, ch, 240:255], in_=xrow[:, xs, 16 + 1152 - 15:16 + 1152]),
                                 reads=[(xrow, xs)], writes=[(npoolT, ch)])
                            cur = xs
                            for lv in range(g + 1):
                                sh = 1 << lv
                                new = 1 + (lv % 2)
                                S.op("dve", lambda e, cur=cur, new=new, sh=sh: e.tensor_tensor(
                                    out=xrow[:, new, 16:PW], in0=xrow[:, cur, 16:PW], in1=xrow[:, cur, 16 - sh:PW - sh], op=ALU.add),
                                    reads=[(xrow, cur)], writes=[(xrow, new)])
                                cur = new
                            S.op("dve", lambda e, cur=cur, c2=c2: e.scalar_tensor_tensor(
                                out=ypl[:, c2, 0:1024], in0=xrow[:, cur, 144:1168], scalar=1.0 / wwin, in1=xrow[:, xs, 144:1168],
                                op0=ALU.mult, op1=ALU.subtract), reads=[(xrow, cur), (xrow, xs)], writes=[(ypl, c2)])
                            S.op("dve", lambda e, cur=cur, g=g: e.tensor_tensor(out=t16.ap, in0=xrow[:, cur, 144:160], in1=invc[:, g, :],
                                                                                 op=ALU.mult), reads=[(xrow, cur), invc], writes=[t16])
                            S.op("dve", lambda e, c2=c2: e.tensor_tensor(out=ypl[:, c2, 0:16], in0=t16.ap, in1=xrow[:, xs, 144:160],
                                                                         op=ALU.subtract), reads=[t16, (xrow, xs)], writes=[(ypl, c2)])
                            S.op("dve", lambda e, cur=cur, c2=c2: e.scalar_tensor_tensor(
                                out=ypl[:, c2, 1024:1088].rearrange("p (b t) -> p b t", t=4), in0=xe(cur)[:, :, 15:19],
                                scalar=1.0 / wwin, in1=xe(xs)[:, :, 15:19], op0=ALU.mult, op1=ALU.subtract),
                                reads=[(xrow, cur), (xrow, xs)], writes=[(ypl, c2)])
                        for dc in range(2):
                            for (a, b) in ((0, 512), (512, 1024), (1024, 1088)):
                                sl = ps_next()
                                mm_group(sl, PS[:, sl, 0:b - a],
                                         [(pwt[:, g, c2, dc * 128:(dc + 1) * 128], ypl[:, c2, a:b]) for c2 in range(2)],
                                         reads=[pwt, ypl])
                                S.op("act", lambda e, sl=sl, a=a, b=b, g=g, dc=dc: e.activation(
                                    out=ypsT[:, 2 * g + dc, a:b], in_=PS[:, sl, 0:b - a], func=AF.Copy,
                                    scale=pscT[:, 2 * g + dc:2 * g + dc + 1]), reads=[(PS, sl), pscT], writes=[(ypsT, 2 * g + dc)])
                    for ch in range(8):
                        for half in range(2):
                            sl = ps_next()
                            transpose_to(PS, sl, PS[:, sl, 0:128], npoolT[:, ch, half * 128:(half + 1) * 128], C[:, IDF, :],
                                         reads=[(npoolT, ch), C])
                            S.op("act", lambda e, sl=sl, ch=ch, half=half: e.copy(out=otok[:, half, ch * 128:(ch + 1) * 128],
                                                                                 in_=PS[:, sl, 0:128]),
                                 reads=[(PS, sl)], writes=[(otok, half)])
                    S.dma("sp", o_pool[0:128, :], otok[:, 0, :], reads=[(otok, 0)], is_out=True)
                    S.dma("sp", o_pool[128:255, :], otok[0:127, 1, :], reads=[(otok, 1)], is_out=True)
                    S.barrier()
                S.dma("sp", o_delta_p.rearrange("h k v -> k h v"), Sst.ap, reads=[Sst], is_out=True)
                octok = S.sb("octok", [51, 3072], F32, es=sc)
                for chn in range(24):
                    sl = ps_next()
                    transpose_to(PS, sl, PS[0:51, sl, 0:128], nconvT[:, chn, :], C[:, IDF, :], reads=[(nconvT, chn), C])
                    S.op("act", lambda e, sl=sl, chn=chn: e.copy(out=octok[:, chn * 128:(chn + 1) * 128], in_=PS[0:51, sl, 0:128]),
                         reads=[(PS, sl)], writes=[octok])
                S.dma("sp", o_conv, octok.ap, reads=[octok], is_out=True)
                S.barrier()
            scD.close()
            ckpt(8)
            T2 = [(0, 384), (384, 768), (768, 1088)]
            with ExitStack() as s4:
                MH = S.sb("MH", [128, 16, 1088], BF16, nslots=16, es=s4)
                with ExitStack() as sa:
                    uT2 = S.sb("uT2", [128, 16, 1088], BF16, nslots=16, es=sa)
                    xt = S.sb("xt4", [128, 1, D], F32, nslots=1, es=sa)
                    xb = S.sb("xb4", [128, 1, D], BF16, nslots=1, es=sa)
                    st = S.sb("st4", [128, 1, 2], F32, nslots=1, es=sa)
                    sg = S.sb("sg", [128, 3, 1088], F32, nslots=3, es=sa)
                    xr = Ring.__new__(Ring)
                    xr.n, xr.i = 1, 0
                    for t in range(8):
                        make_u(xown[t * 128:(t + 1) * 128, :], 128, uT2, t * 128, "own", xt, xb, st, xr)
                    make_u(xsm[:, :], 64, uT2, 1024, "smp", xt, xb, st, xr)
                    for n in range(16):
                        for (i, off) in ((0, GAOFF), (1, GBOFF)):
                            ws = load_w(w_in[:, off + n * 128:off + (n + 1) * 128], 16)
                            for (a, b) in T2:
                                sl = ps_next()
                                mm_group(sl, PS[:, sl, 0:b - a], [(WR.b[:, ws, k, :], uT2[:, k, a:b]) for k in range(16)],
                                         reads=[(WR.b, ws), uT2])
                                S.op("act", lambda e, sl=sl, a=a, b=b, i=i: e.activation(out=sg[:, i, a:b], in_=PS[:, sl, 0:b - a],
                                                                                        func=AF.Sigmoid),
                                     reads=[(PS, sl)], writes=[(sg, i)])
                        ws = load_w(w_proj_a[:, n * 128:(n + 1) * 128], 8)
                        for (a, b) in T2:
                            sl = ps_next()
                            mm_group(sl, PS[:, sl, 0:b - a], [(WR.b[:, ws, k, :], ogT[:, k, a:b]) for k in range(8)],
                                     reads=[(WR.b, ws), ogT])
                            S.op("dve", lambda e, sl=sl, a=a, b=b: e.tensor_tensor(out=sg[:, 2, a:b], in0=PS[:, sl, 0:b - a],
                                                                                  in1=sg[:, 0, a:b], op=ALU.mult),
                                 reads=[(PS, sl), (sg, 0)], writes=[(sg, 2)])
                        ws = load_w(w_proj_b[:, n * 128:(n + 1) * 128], 8)
                        for (a, b) in T2:
                            sl = ps_next()
                            mm_group(sl, PS[:, sl, 0:b - a], [(WR.b[:, ws, k, :], ypsT[:, k, a:b]) for k in range(8)],
                                     reads=[(WR.b, ws), ypsT])
                            S.op("dve", lambda e, sl=sl, a=a, b=b: e.tensor_tensor(out=sg[:, 1, a:b], in0=PS[:, sl, 0:b - a],
                                                                                  in1=sg[:, 1, a:b], op=ALU.mult),
                                 reads=[(PS, sl), (sg, 1)], writes=[(sg, 1)])
                            S.op("dve", lambda e, a=a, b=b, n=n: e.tensor_tensor(out=MH[:, n, a:b], in0=sg[:, 1, a:b],
                                                                                in1=sg[:, 2, a:b], op=ALU.add),
                                 reads=[(sg, 1), (sg, 2)], writes=[(MH, n)])
                    S.barrier()
                ckpt(81)
                xT = S.sb("xT", [128, 16, 1088], F32, nslots=16, es=s4)
                rsb = S.sb("rsb", [128, 1088], F32, es=s4)
                tmpf = S.sb("tmpf", [128, 2, 1088], F32, nslots=2, es=s4)

                def mod_add(n, sl, a, b, chunk0):
                    pb_ = min(b, 1024)
                    if a < pb_:
                        S.op("dve", lambda e: e.scalar_tensor_tensor(out=xT[:, n, a:pb_], in0=PS[:, sl, 0:pb_ - a],
                                                                     scalar=modT[:, chunk0 + n, 0:1], in1=xT[:, n, a:pb_],
                                                                     op0=ALU.mult, op1=ALU.add),
                             reads=[(PS, sl), modT, (xT, n)], writes=[(xT, n)])
                    if b > 1024:
                        o_ = 1024 - a
                        v4 = lambda ap_: ap_.rearrange("p (b t) -> p b t", t=4)
                        S.op("dve", lambda e: e.tensor_tensor(out=v4(tmpf[:, 0, 0:64]), in0=v4(PS[:, sl, o_:o_ + 64]),
                                                              in1=modT[:, chunk0 + n, 1:17].unsqueeze(2).broadcast_to([128, 16, 4]),
                                                              op=ALU.mult), reads=[(PS, sl), modT], writes=[(tmpf, 0)])
                        S.op("dve", lambda e: e.tensor_tensor(out=xT[:, n, 1024:1088], in0=xT[:, n, 1024:1088], in1=tmpf[:, 0, 0:64],
                                                              op=ALU.add), reads=[(tmpf, 0), (xT, n)], writes=[(xT, n)])

                with ExitStack() as sb_:
                    xl = S.sb("xl", [128, D], F32, es=sb_)
                    for t in range(9):
                        ntok = 128 if t < 8 else 64
                        src = xown[t * 128:(t + 1) * 128, :] if t < 8 else xsm[:, :]
                        S.dma("sp", xl[0:ntok, :], src, writes=[xl])
                        for k in range(16):
                            sl = ps_next()
                            transpose_to(PS, sl, PS[:, sl, 0:ntok], xl[0:ntok, k * 128:(k + 1) * 128], C[0:ntok, IDF, 0:ntok],
                                         reads=[xl, C])
                            S.op("act", lambda e, sl=sl, k=k, t=t, ntok=ntok: e.copy(out=xT[:, k, t * 128:t * 128 + ntok],
                                                                                    in_=PS[:, sl, 0:ntok]),
                                 reads=[(PS, sl)], writes=[(xT, k)])
                    for n in range(16):
                        ws = load_w(w_out[:, n * 128:(n + 1) * 128], 16)
                        for (a, b) in T2:
                            sl = ps_next()
                            mm_group(sl, PS[:, sl, 0:b - a], [(WR.b[:, ws, k, :], MH[:, k, a:b]) for k in range(16)],
                                     reads=[(WR.b, ws), MH])
                            mod_add(n, sl, a, b, 32)
                    S.barrier()
                ckpt(82)

                def rstd_bc():
                    sls = [ps_next() for _ in T2]
                    for n in range(16):
                        S.op("act", lambda e, n=n: e.activation(out=tmpf[:, n % 2, :], in_=xT[:, n, :], func=AF.Square),
                             reads=[(xT, n)], writes=[(tmpf, n % 2)])

                        def fn(e, n=n):
                            ins = None
                            for sl_, (a, b) in zip(sls, T2):
                                ins = e.matmul(PS[:, sl_, 0:b - a], lhsT=C[:, ONESF, :], rhs=tmpf[:, n % 2, a:b], start=(n == 0),
                                               stop=(n == 15))
                            return ins
                        S.op("pe", fn, reads=[(tmpf, n % 2), C], writes=[(PS, s_) for s_ in sls])
                    for sl_, (a, b) in zip(sls, T2):
                        S.op("act", lambda e, sl_=sl_, a=a, b=b: e.activation(out=rsb[:, a:b], in_=PS[:, sl_, 0:b - a], func=AF.Ln,
                                                                             scale=1.0 / D, bias=EPS), reads=[(PS, sl_)], writes=[rsb])
                    S.op("act", lambda e: e.activation(out=rsb.ap, in_=rsb.ap, func=AF.Exp, scale=-0.5), reads=[rsb], writes=[rsb])

                rstd_bc()
                for n in range(16):
                    S.op("dve", lambda e, n=n: e.tensor_tensor(out=tmpf[:, n % 2, :], in0=xT[:, n, :], in1=rsb.ap, op=ALU.mult),
                         reads=[(xT, n), rsb], writes=[(tmpf, n % 2)])
                    S.op("act", lambda e, n=n: e.activation(out=MH[:, n, 0:1024], in_=tmpf[:, n % 2, 0:1024], func=AF.Identity,
                                                            scale=GG[:, 1, n, 0:1], bias=modT[:, 48 + n, 0:1]),
                         reads=[(tmpf, n % 2), (GG, 1), modT], writes=[(MH, n)])
                    v4 = lambda ap_: ap_.rearrange("p (b t) -> p b t", t=4)
                    S.op("dve", lambda e, n=n: e.tensor_tensor(out=v4(tmpf[:, n % 2, 1024:1088]), in0=v4(tmpf[:, n % 2, 1024:1088]),
                                                               in1=GG[:, 1, n, 1:17].unsqueeze(2).broadcast_to([128, 16, 4]),
                                                               op=ALU.mult), reads=[(tmpf, n % 2), (GG, 1)], writes=[(tmpf, n % 2)])
                    S.op("dve", lambda e, n=n: e.tensor_tensor(out=v4(MH[:, n, 1024:1088]), in0=v4(tmpf[:, n % 2, 1024:1088]),
                                                               in1=modT[:, 48 + n, 1:17].unsqueeze(2).broadcast_to([128, 16, 4]),
                                                               op=ALU.add), reads=[(tmpf, n % 2), modT], writes=[(MH, n)])
                ckpt(83)
                actq = S.sb("actq", [128, 4, 1088], BF16, nslots=4, es=s4)
                sgf = S.sb("sgf", [128, 1088], F32, es=s4)
                for qi in range(11):
                    for j in range(4):
                        f_ = qi * 4 + j
                        wg = load_w(w_gate_up[:, f_ * 128:(f_ + 1) * 128], 16)
                        for (a, b) in T2:
                            sl = ps_next()
                            mm_group(sl, PS[:, sl, 0:b - a], [(WR.b[:, wg, k, :], MH[:, k, a:b]) for k in range(16)],
                                     reads=[(WR.b, wg), MH])
                            S.op("act", lambda e, sl=sl, a=a, b=b: e.activation(out=sgf[:, a:b], in_=PS[:, sl, 0:b - a], func=AF.Silu),
                                 reads=[(PS, sl)], writes=[sgf])
                        wu = load_w(w_gate_up[:, DFF + f_ * 128:DFF + (f_ + 1) * 128], 16)
                        for (a, b) in T2:
                            sl = ps_next()
                            mm_group(sl, PS[:, sl, 0:b - a], [(WR.b[:, wu, k, :], MH[:, k, a:b]) for k in range(16)],
                                     reads=[(WR.b, wu), MH])
                            S.op("dve", lambda e, sl=sl, a=a, b=b, j=j: e.tensor_tensor(out=actq[:, j, a:b], in0=PS[:, sl, 0:b - a],
                                                                                       in1=sgf[:, a:b], op=ALU.mult),
                                 reads=[(PS, sl), sgf], writes=[(actq, j)])
                    for n in range(16):
                        wd = load_w(w_down[qi * 512:(qi + 1) * 512, n * 128:(n + 1) * 128], 4)
                        for (a, b) in T2:
                            sl = ps_next()
                            mm_group(sl, PS[:, sl, 0:b - a], [(WR.b[:, wd, j, :], actq[:, j, a:b]) for j in range(4)],
                                     reads=[(WR.b, wd), actq])
                            mod_add(n, sl, a, b, 80)
                ckpt(84)
                rstd_bc()
                for n in range(16):
                    S.op("dve", lambda e, n=n: e.scalar_tensor_tensor(out=xT[:, n, :], in0=xT[:, n, :], scalar=fgT[:, n:n + 1],
                                                                      in1=rsb.ap, op0=ALU.mult, op1=ALU.mult),
                         reads=[(xT, n), fgT, rsb], writes=[(xT, n)])
                ytok = S.sb("ytok", [128, 1, D], F32, nslots=1, es=s4)
                for t in range(9):
                    ntok = 128 if t < 8 else 64
                    ys = 0
                    for k4 in range(4):
                        sl = ps_next()

                        def ft(e, sl=sl, k4=k4, t=t, ntok=ntok):
                            ins = None
                            for j in range(4):
                                ins = e.matmul(PS[0:ntok, sl, j * 128:(j + 1) * 128], lhsT=xT[:, k4 * 4 + j, t * 128:t * 128 + ntok],
                                               rhs=C[:, IDF, :], start=True, stop=True)
                            return ins
                        S.op("pe", ft, reads=[xT, C], writes=[(PS, sl)])
                        S.op("act", lambda e, sl=sl, k4=k4, ys=ys, ntok=ntok: e.copy(out=ytok[0:ntok, ys, k4 * 512:(k4 + 1) * 512],
                                                                                    in_=PS[0:ntok, sl, :]),
                             reads=[(PS, sl)], writes=[(ytok, ys)])
                    dst = y_own[t * 128:(t + 1) * 128, :] if t < 8 else y_sm[:, :]
                    S.dma("sp", dst, ytok[0:ntok, ys, :], reads=[(ytok, ys)], is_out=True)
                S.barrier()

        try:
            body()
        except _Stop:
            pass
        S.finish()
    return nc, dbg


def prep_inputs(inp):
    f = lambda a: np.ascontiguousarray(a, dtype=np.float32)
    shared = {
        "w_ada": f(inp["w_ada"][0]), "b_ada": f(inp["b_ada"][0].reshape(96, 128)),
        "norm1_g": f(inp["norm1_g"][0].reshape(16, 128)), "w_in": f(inp["w_in"][0]),
        "conv_w": f(inp["conv_w"][0].reshape(96, 128)), "a_log": f(inp["a_log"][0].reshape(1, 8)),
        "dt_bias": f(inp["dt_bias"][0].reshape(1, 8)), "o_norm_g": f(inp["o_norm_g"][0].reshape(1, 128)),
        "pool_w": f(inp["pool_w"][0]), "pool_scale": f(inp["pool_scale"][0].reshape(8, 128)),
        "w_proj_a": f(inp["w_proj_a"][0]), "w_proj_b": f(inp["w_proj_b"][0]), "w_out": f(inp["w_out"][0]),
        "norm2_g": f(inp["norm2_g"][0].reshape(16, 128)), "w_gate_up": f(inp["w_gate_up"][0]),
        "w_down": f(inp["w_down"][0]), "final_g": f(inp["final_g"].reshape(16, 128)),
    }
    maps = []
    for c in range(8):
        b, hh = c // 2, c % 2
        sb = slice(16 * c, 16 * c + 16)
        m = dict(shared)
        xp = inp["x_prompt"][b]
        m["xpre"] = f(xp[0:1024]) if hh == 1 else np.zeros((1024, D), np.float32)
        m["xown"] = f(xp[1024 * hh:1024 * hh + 1024])
        m["xsm"] = f(inp["x_sample"][sb].reshape(64, D))
        m["cc"] = f(np.concatenate([inp["c_prompt"][b:b + 1], inp["c_sample"][sb]], axis=0))
        m["sdelta"] = f(inp["state_delta"][0, sb])
        m["sconv"] = f(inp["state_conv"][0, sb].reshape(48, 3072))
        m["spool"] = f(inp["state_pool"][0, sb].reshape(240, 1024))
        m["flag"] = np.full((128, 1), float(hh), np.float32)
        m["pos0"] = np.full((128, 1), float(1024 * hh), np.float32)
        maps.append(m)
    return maps


_NC_CACHE = {}


def kernel(**inputs):
    if "nc" not in _NC_CACHE:
        _NC_CACHE["nc"] = build_program()[0]
    nc = _NC_CACHE["nc"]
    maps = prep_inputs(inputs)
    res = run_bass_kernel_spmd(nc, maps, core_ids=list(range(8))).results
    y_prompt = np.zeros((4, 2048, D), np.float32)
    y_sample = np.zeros((128, 4, D), np.float32)
    ndp = np.zeros((1, 4, NH, 128, 128), np.float32)
    ncp = np.zeros((1, 4, 3, 3072), np.float32)
    npp = np.zeros((1, 4, 15, 1024), np.float32)
    nds = np.zeros((1, 128, NH, 128, 128), np.float32)
    ncs = np.zeros((1, 128, 3, 3072), np.float32)
    nps = np.zeros((1, 128, 15, 1024), np.float32)
    for c in range(8):
        b, hh = c // 2, c % 2
        r = res[c]
        sb = slice(16 * c, 16 * c + 16)
        y_prompt[b, 1024 * hh:1024 * hh + 1024] = r["y_own"]
        y_sample[sb] = r["y_sm"].reshape(16, 4, D)
        nds[0, sb] = r["o_delta_s"]
        ncs[0, sb] = r["o_conv"][0:48].reshape(16, 3, 3072)
        nps[0, sb] = r["o_pool"][0:240].reshape(16, 15, 1024)
        if hh == 1:
            ndp[0, b] = r["o_delta_p"]
            ncp[0, b] = r["o_conv"][48:51]
            npp[0, b] = r["o_pool"][240:255]
    return (y_prompt, y_sample, ndp, ncp, npp, nds, ncs, nps)
```

```python
import numpy as np
from contextlib import ExitStack
import concourse.bass as bass
import concourse.mybir as mybir
from concourse.bass_utils import run_bass_kernel_spmd

F32 = mybir.dt.float32
BF16 = mybir.dt.bfloat16
I32 = mybir.dt.int32
F32R = mybir.dt.float32r
NEUMANN_F32R = False
AF = mybir.ActivationFunctionType
ALU = mybir.AluOpType
AX = mybir.AxisListType

D = 2048
NH = 8
DFF = 5632
QOFF, KOFF, VOFF, ZOFF, AOFF, BOFF, POFF, GAOFF, GBOFF = 0, 1024, 2048, 3072, 4096, 4104, 4112, 5136, 7184
INW = 9232
EPS = 1e-6
NDS = 24
SAME_ENGINE_SYNC = True
KUNITS = 2
KUNITS_A = 5
RAW_ONLY_SELF = False


class Buf:
    _n = 0

    def __init__(self, ap, nslots=1, name=""):
        self.ap = ap if type(ap).__name__ == 'AP' else ap[:]
        self.n = nslots
        self.name = name
        Buf._n += 1
        self.id = Buf._n

    def __getitem__(self, k):
        return self.ap[k]


class Sched:
    def __init__(self, nc, es):
        self.nc = nc
        self.es = es
        self.E = {"pe": nc.tensor, "act": nc.scalar, "dve": nc.vector, "pool": nc.gpsimd, "sp": nc.sync}
        self.sem = {e: es.enter_context(nc.semaphore("S_" + e)) for e in ["pe", "act", "dve", "pool"]}
        self.cnt = {e: 0 for e in self.sem}
        self.dsem = [es.enter_context(nc.semaphore(f"D{i}")) for i in range(NDS)]
        self.dval = [0] * NDS
        self.dnext = 0
        self.dq = {}
        self.seen = {e: {} for e in self.E}
        self.lastw = {}
        self.readers = {}
        self.out_toks = []
        self.nops = 0
        self.off = False

    def sb(self, name, shape, dt, nslots=1, es=None):
        self._uid = getattr(self, "_uid", 0) + 1
        name = f"{name}_{self._uid}"
        return Buf((es or self.es).enter_context(self.nc.sbuf_tensor(name, shape, dt)), nslots, name)

    def barrier(self):
        if self.off:
            return
        for e in self.E:
            for j in range(NDS):
                if self.dval[j] > 0:
                    self._wait(e, ("dma", j, self.dval[j]))
            for e2 in self.sem:
                if self.cnt[e2] > 0 and e2 != e:
                    self._wait(e, ("eng", e2, self.cnt[e2]))

    def _keys(self, acc):
        ks = []
        for a in acc:
            if isinstance(a, Buf):
                a = (a, None)
            b, s = a
            if s is None:
                ks.extend((b.id, i) for i in range(b.n))
            elif isinstance(s, (list, tuple, range)):
                ks.extend((b.id, i) for i in s)
            else:
                ks.append((b.id, s))
        return ks

    def _wait(self, e, tok):
        kind, key, val = tok
        if self.seen[e].get((kind, key), 0) >= val:
            return
        sem = self.sem[key] if kind == "eng" else self.dsem[key]
        self.E[e].wait_ge(sem, val)
        self.seen[e][(kind, key)] = val

    def _deps(self, e, reads, writes):
        rk, wk = self._keys(reads), self._keys(writes)
        deps = set()
        deps2 = set()
        for k in rk:
            if k in self.lastw:
                deps.add(self.lastw[k])
        for k in wk:
            if k in self.lastw:
                deps2.add(self.lastw[k])
            for t in self.readers.get(k, ()):
                deps2.add(t)
        for t in sorted(deps | deps2, key=lambda t: (t[0], str(t[1]), t[2])):
            if t[0] == "eng" and t[1] == e:
                if e == "pe" or not SAME_ENGINE_SYNC or (RAW_ONLY_SELF and t not in deps):
                    continue
            self._wait(e, t)
        return rk, wk

    def _commit(self, tok, rk, wk):
        for k in rk:
            self.readers.setdefault(k, []).append(tok)
        for k in wk:
            self.lastw[k] = tok
            self.readers[k] = []

    def op(self, e, fn, reads=(), writes=()):
        if self.off:
            return
        rk, wk = self._deps(e, reads, writes)
        inst = fn(self.E[e])
        self.cnt[e] += 1
        inst.then_inc(self.sem[e], 1)
        self._commit(("eng", e, self.cnt[e]), rk, wk)
        self.nops += 1

    def dma(self, q, out, in_, reads=(), writes=(), is_out=False, **kw):
        if self.off:
            return
        rk, wk = self._deps(q, reads, writes)
        half = NDS // 2
        base = 0 if q == "pool" else half
        cur = self.dq.get(q, 0)
        j = base + cur
        self.dq[q] = (cur + 1) % half
        if self.dval[j] > 0:
            self._wait(q, ("dma", j, self.dval[j]))
        inst = self.E[q].dma_start(out=out, in_=in_, **kw)
        self.dval[j] += 16
        inst.then_inc(self.dsem[j], 16)
        tok = ("dma", j, self.dval[j])
        self._commit(tok, rk, wk)
        if is_out:
            self.out_toks.append(tok)
        self.nops += 1

    def finish(self):
        for j in range(NDS):
            if self.dval[j] > 0:
                self._wait("sp", ("dma", j, self.dval[j]))
        for e in self.sem:
            if self.cnt[e] > 0:
                self._wait("sp", ("eng", e, self.cnt[e]))


class Ring:
    def __init__(self, sch, name, shape, dt, n, es=None):
        self.b = sch.sb(name, [shape[0], n] + list(shape[1:]), dt, nslots=n, es=es)
        self.n = n
        self.i = 0

    def next(self):
        s = self.i % self.n
        self.i += 1
        return s


def tok_tiles(n, step=512):
    return [(a, min(a + step, n)) for a in range(0, n, step)]


def build_program(debug=None):
    nc = bass.Bass("TRN2", target_bir_lowering=False)
    dbg = {}

    def din(name, shape, dt=F32):
        return nc.dram_tensor(name, list(shape), dt, kind="ExternalInput").ap()

    def dout(name, shape, dt=F32):
        return nc.dram_tensor(name, list(shape), dt, kind="ExternalOutput").ap()

    xpre = din("xpre", [1024, D])
    xown = din("xown", [1024, D])
    xsm = din("xsm", [64, D])
    cc = din("cc", [17, D])
    sdelta = din("sdelta", [16, NH, 128, 128])
    sconv = din("sconv", [48, 3072])
    spool = din("spool", [240, 1024])
    flag = din("flag", [128, 1])
    pos0 = din("pos0", [128, 1])
    w_ada = din("w_ada", [D, 6 * D])
    b_ada = din("b_ada", [96, 128])
    norm1_g = din("norm1_g", [16, 128])
    w_in = din("w_in", [D, INW])
    conv_w = din("conv_w", [96, 128])
    a_log = din("a_log", [1, 8])
    dt_bias = din("dt_bias", [1, 8])
    o_norm_g = din("o_norm_g", [1, 128])
    pool_w = din("pool_w", [4, 256, 256])
    pool_scale = din("pool_scale", [8, 128])
    w_proj_a = din("w_proj_a", [1024, D])
    w_proj_b = din("w_proj_b", [1024, D])
    w_out = din("w_out", [D, D])
    norm2_g = din("norm2_g", [16, 128])
    w_gate_up = din("w_gate_up", [D, 2 * DFF])
    w_down = din("w_down", [DFF, D])
    final_g = din("final_g", [16, 128])

    y_own = dout("y_own", [1024, D])
    y_sm = dout("y_sm", [64, D])
    o_delta_p = dout("o_delta_p", [NH, 128, 128])
    o_conv = dout("o_conv", [51, 3072])
    o_pool = dout("o_pool", [255, 1024])
    o_delta_s = dout("o_delta_s", [16, NH, 128, 128])

    with ExitStack() as es:
        S = Sched(nc, es)
        E = S.E

        def dbg_out(name, buf, shape, dt=F32, ap=None):
            if debug is None or name not in debug:
                return
            t = dout("dbg_" + name, shape, dt)
            S.dma("sp", t, ap if ap is not None else buf.ap, reads=[buf], is_out=True)
            dbg[name] = shape

        PS = Buf(es.enter_context(nc.psum_tensor("PS", [128, 8, 512], F32)), 8, "PS")
        ps_i = [0]
        psb_i = [0]

        def ps_next():
            s = ps_i[0] % 8
            ps_i[0] += 1
            return s

        def ps_next():
            s = psb_i[0] % 8
            psb_i[0] += 1
            return s

        C = S.sb("consts_f", [128, 8, 128], F32, nslots=8)
        CB = S.sb("consts_b", [128, 2, 128], BF16, nslots=2)
        IDF, ONESF, TRI, STRICT, BTRI, BSTRICT, BLK = 0, 1, 2, 3, 5, 6, 7

        def mk_const(slot, fn):
            S.op("pool", fn, writes=[(C, slot)])

        mk_const(ONESF, lambda e: e.memset(C[:, ONESF, :], 1.0))
        for slot, cmp_, base in ((IDF, ALU.is_equal, 0), (TRI, ALU.is_ge, 0), (STRICT, ALU.is_gt, 0)):
            S.op("pool", lambda e, slot=slot: e.memset(C[:, slot, :], 1.0), writes=[(C, slot)])
            if slot == STRICT:
                S.op("pool", lambda e, slot=slot, cmp_=cmp_: e.affine_select(
                    out=C[:, slot, :], in_=C[:, slot, :], pattern=[[-1, 128]], compare_op=cmp_, fill=0.0,
                    base=0, channel_multiplier=1), reads=[(C, slot)], writes=[(C, slot)])
            else:
                S.op("pool", lambda e, slot=slot, cmp_=cmp_: e.affine_select(
                    out=C[:, slot, :], in_=C[:, slot, :], pattern=[[1, 128]], compare_op=cmp_, fill=0.0,
                    base=0, channel_multiplier=-1), reads=[(C, slot)], writes=[(C, slot)])
        blk_i = S.sb("blk_i", [128, 2, 128], I32, nslots=2)
        S.op("pool", lambda e: e.iota(blk_i[:, 0, :], pattern=[[1, 128]], base=0, channel_multiplier=0),
             writes=[(blk_i, 0)])
        S.op("pool", lambda e: e.iota(blk_i[:, 1, :], pattern=[[0, 128]], base=0, channel_multiplier=1),
             writes=[(blk_i, 1)])
        S.op("dve", lambda e: e.tensor_single_scalar(out=blk_i[:, 0, :], in_=blk_i[:, 0, :], scalar=2,
                                                      op=ALU.arith_shift_right), reads=[(blk_i, 0)], writes=[(blk_i, 0)])
        S.op("dve", lambda e: e.tensor_single_scalar(out=blk_i[:, 1, :], in_=blk_i[:, 1, :], scalar=2,
                                                      op=ALU.arith_shift_right), reads=[(blk_i, 1)], writes=[(blk_i, 1)])
        S.op("dve", lambda e: e.tensor_tensor(out=blk_i[:, 0, :], in0=blk_i[:, 0, :], in1=blk_i[:, 1, :],
                                               op=ALU.is_equal), reads=[blk_i], writes=[(blk_i, 0)])
        S.op("dve", lambda e: e.tensor_copy(out=C[:, BLK, :], in_=blk_i[:, 0, :]), reads=[(blk_i, 0)],
             writes=[(C, BLK)])
        S.op("pool", lambda e: e.tensor_tensor(out=C[:, BTRI, :], in0=C[:, TRI, :], in1=C[:, BLK, :], op=ALU.mult),
             reads=[(C, TRI), (C, BLK)], writes=[(C, BTRI)])
        S.op("pool", lambda e: e.tensor_tensor(out=C[:, BSTRICT, :], in0=C[:, STRICT, :], in1=C[:, BLK, :],
                                               op=ALU.mult), reads=[(C, STRICT), (C, BLK)], writes=[(C, BSTRICT)])
        S.op("pool", lambda e: e.tensor_copy(out=CB[:, 0, :], in_=C[:, IDF, :]), reads=[(C, IDF)], writes=[(CB, 0)])
        S.op("pool", lambda e: e.memset(CB[:, 1, :], 1.0), writes=[(CB, 1)])
        dbg_out("consts", C, [128, 8, 128])

        class _Stop(Exception):
            pass

        def ckpt(k):
            if debug is not None and f'stop{k}' in debug:
                S.off = True

        def body():
            def mm_group(slot, out_ap, terms, reads, psbuf=PS):
                def fn(e):
                    n_ = len(terms)
                    ins = None
                    for i, (l, r) in enumerate(terms):
                        ins = e.matmul(out_ap, lhsT=l, rhs=r, start=(i == 0), stop=(i == n_ - 1))
                    return ins
                S.op("pe", fn, reads=reads, writes=[(psbuf, slot)])

            def transpose_to(psbuf, slot, out_ap, in_ap, ident_ap, reads):
                S.op("pe", lambda e: e.matmul(out_ap, lhsT=in_ap, rhs=ident_ap, start=True, stop=True), reads=reads,
                     writes=[(psbuf, slot)])

            def load_vecT(name, dram, n):
                tmp = S.sb(name + "_tm", [n, 128], F32)
                dst = S.sb(name, [128, n], F32)
                S.dma("sp", tmp.ap, dram, writes=[tmp])
                sl = ps_next()
                transpose_to(PS, sl, PS[:, sl, 0:n], tmp.ap, C[0:n, IDF, 0:n], reads=[tmp, (C, IDF)])
                S.op("act", lambda e: e.copy(out=dst.ap, in_=PS[:, sl, 0:n]), reads=[(PS, sl)], writes=[dst])
                return dst

            def wsrc(dram_ap):
                return dram_ap.rearrange("(k p) c -> p k c", p=128)

            WR = Ring(S, "wr", [128, 16, 128], BF16, 4)

            def load_w(dram_ap, kc, ncols=128, ring=None):
                ring = ring or WR
                s_ = ring.next()
                src = wsrc(dram_ap)
                step = 4 if ncols > 128 else 8
                for k0 in range(0, kc, step):
                    k1 = min(kc, k0 + step)
                    S.dma("pool", ring.b[:, s_, k0:k1, 0:ncols], src[:, k0:k1, :], writes=[(ring.b, s_)])
                return s_

            b_adaT = load_vecT("b_adaT", b_ada, 96)
            g1T = load_vecT("g1T", norm1_g, 16)
            g2T = load_vecT("g2T", norm2_g, 16)
            fgT = load_vecT("fgT", final_g, 16)
            pscT = load_vecT("pscT", pool_scale, 8)
            cwT = load_vecT("cwT", conv_w, 96)
            flag_t = S.sb("flag_t", [128, 1], F32)
            S.dma("sp", flag_t.ap, flag, writes=[flag_t])
            pos0_t = S.sb("pos0_t", [128, 1], F32)
            S.dma("sp", pos0_t.ap, pos0, writes=[pos0_t])
            nea_t = S.sb("nea_t", [128, 8], F32)
            S.dma("sp", nea_t.ap, a_log[0:1, :].broadcast_to([128, 8]), writes=[nea_t])
            dtb_t = S.sb("dtb_t", [128, 8], F32)
            S.dma("sp", dtb_t.ap, dt_bias[0:1, :].broadcast_to([128, 8]), writes=[dtb_t])
            ong_t = S.sb("ong_t", [128, 128], F32)
            S.dma("sp", ong_t.ap, o_norm_g[0:1, :].broadcast_to([128, 128]), writes=[ong_t])
            S.op("act", lambda e: e.activation(out=nea_t.ap, in_=nea_t.ap, func=AF.Exp), reads=[nea_t], writes=[nea_t])
            S.op("dve", lambda e: e.tensor_scalar_mul(out=nea_t.ap, in0=nea_t.ap, scalar1=-1.0), reads=[nea_t], writes=[nea_t])

            ckpt(1)
            CM = S.sb("CM", [128, 16, 64], BF16)
            RM = S.sb("RM", [128, 16], BF16)
            scM = ExitStack()
            scM.__enter__()
            CMf = S.sb("CMf", [128, 16, 64], F32, es=scM)
            RMf = S.sb("RMf", [128, 16], F32, es=scM)
            S.op("pool", lambda e: e.memset(CMf.ap, 1.0), writes=[CMf])
            S.op("pool", lambda e: e.affine_select(out=CMf.ap, in_=CMf.ap, pattern=[[-4, 16], [1, 64]], compare_op=ALU.is_ge,
                                                   fill=0.0, base=0, channel_multiplier=0), reads=[CMf], writes=[CMf])
            S.op("pool", lambda e: e.affine_select(out=CMf.ap, in_=CMf.ap, pattern=[[4, 16], [-1, 64]], compare_op=ALU.is_ge,
                                                   fill=0.0, base=3, channel_multiplier=0), reads=[CMf], writes=[CMf])
            S.op("pool", lambda e: e.memset(RMf.ap, 1.0), writes=[RMf])
            S.op("pool", lambda e: e.affine_select(out=RMf.ap, in_=RMf.ap, pattern=[[-4, 16]], compare_op=ALU.is_ge,
                                                   fill=0.0, base=0, channel_multiplier=1), reads=[RMf], writes=[RMf])
            S.op("pool", lambda e: e.affine_select(out=RMf.ap, in_=RMf.ap, pattern=[[4, 16]], compare_op=ALU.is_ge,
                                                   fill=0.0, base=3, channel_multiplier=-1), reads=[RMf], writes=[RMf])
            S.op("pool", lambda e: e.tensor_copy(out=CM.ap, in_=CMf.ap), reads=[CMf], writes=[CM])
            S.op("pool", lambda e: e.tensor_copy(out=RM.ap, in_=RMf.ap), reads=[RMf], writes=[RM])
            S.barrier()
            scM.close()

            ckpt(2)
            modT = S.sb("modT", [128, 96, 17], F32)
            GG = S.sb("GG", [128, 2, 16, 17], F32, nslots=2)
            FV = S.sb("FV", [128, 2, 16], F32, nslots=2)
            with ExitStack() as sc:
                cc_t = S.sb("cc_t", [17, D], F32, es=sc)
                cc_b = S.sb("cc_b", [17, D], BF16, es=sc)
                cT = S.sb("cT", [128, 16, 17], BF16, es=sc)
                mtok = S.sb("mtok", [17, 2, 512], F32, nslots=2, es=sc)
                WA = Ring(S, "wa", [128, 16, 512], BF16, 3, es=sc)
                WS32 = Ring(S, "ws32", [128, 16, 512], F32, 2, es=sc)
                S.dma("sp", cc_t.ap, cc, writes=[cc_t])
                S.op("act", lambda e: e.activation(out=cc_b.ap, in_=cc_t.ap, func=AF.Silu), reads=[cc_t], writes=[cc_b])
                ckpt(20)
                for k in range(16):
                    if k == 2:
                        ckpt(202)
                    if k == 9:
                        ckpt(209)
                    sl = ps_next()
                    transpose_to(PS, sl, PS[:, sl, 0:17], cc_b[:, k * 128:(k + 1) * 128], CB[0:17, 0, 0:17],
                                 reads=[cc_b, (CB, 0)])
                    S.op("dve", lambda e, sl=sl, k=k: e.tensor_copy(out=cT[:, k, :], in_=PS[:, sl, 0:17]),
                         reads=[(PS, sl)], writes=[cT])
                ckpt(21)
                for blk in range(24):
                    if blk == 1:
                        ckpt(25)
                    if blk == 3:
                        ckpt(26)
                    if blk == 8:
                        ckpt(27)
                    if blk % 2 == 0:
                        ws = load_w(w_ada[:, blk * 512:(blk + 1) * 512], 16, 512, ring=WA)
                    else:
                        ws = WA.next()
                        s32 = WS32.next()
                        src = wsrc(w_ada[:, blk * 512:(blk + 1) * 512])
                        for k0 in range(0, 16, 4):
                            S.dma("sp", WS32.b[:, s32, k0:k0 + 4, :], src[:, k0:k0 + 4, :], writes=[(WS32.b, s32)])
                        S.op("dve", lambda e, ws=ws, s32=s32: e.tensor_copy(out=WA.b[:, ws, 0:8, :], in_=WS32.b[:, s32, 0:8, :]),
                             reads=[(WS32.b, s32)], writes=[(WA.b, ws)])
                        S.op("act", lambda e, ws=ws, s32=s32: e.copy(out=WA.b[:, ws, 8:16, :], in_=WS32.b[:, s32, 8:16, :]),
                             reads=[(WS32.b, s32)], writes=[(WA.b, ws)])
                    sl = ps_next()
                    mm_group(sl, PS[0:17, sl, :], [(cT[:, k, :], WA.b[:, ws, k, :]) for k in range(16)],
                             reads=[cT, (WA.b, ws)])
                    ms = blk % 2
                    ckpt(22)
                    S.op("act", lambda e, sl=sl, ms=ms: e.copy(out=mtok[:, ms, :], in_=PS[0:17, sl, :]),
                         reads=[(PS, sl)], writes=[(mtok, ms)])
                    ckpt(23)
                    sl2 = ps_next()

                    def tf(e, sl2=sl2, ms=ms):
                        ins = None
                        for j in range(4):
                            ins = e.matmul(PS[:, sl2, j * 32:j * 32 + 17], lhsT=mtok[:, ms, j * 128:(j + 1) * 128],
                                           rhs=C[0:17, IDF, 0:17], start=True, stop=True)
                        return ins
                    S.op("pe", tf, reads=[(mtok, ms), (C, IDF)], writes=[(PS, sl2)])
                    ckpt(24)
                    S.op("dve", lambda e, sl2=sl2, blk=blk: e.tensor_tensor(
                        out=modT[:, blk * 4:(blk + 1) * 4, :],
                        in0=PS[:, sl2, 0:128].rearrange("p (j c) -> p j c", c=32)[:, :, 0:17],
                        in1=b_adaT[:, blk * 4:(blk + 1) * 4].unsqueeze(2).broadcast_to([128, 4, 17]), op=ALU.add),
                        reads=[(PS, sl2), b_adaT], writes=[modT])
                ckpt(28)
                for i, (gT, pc) in enumerate(((g1T, 1), (g2T, 4))):
                    S.op("dve", lambda e, i=i, gT=gT, pc=pc: e.scalar_tensor_tensor(
                        out=GG[:, i], in0=modT[:, pc * 16:(pc + 1) * 16, :], scalar=1.0,
                        in1=gT.ap.unsqueeze(2).broadcast_to([128, 16, 17]), op0=ALU.add, op1=ALU.mult),
                        reads=[modT, gT], writes=[(GG, i)])
                S.op("dve", lambda e: e.tensor_scalar_mul(out=FV[:, 0, :], in0=GG[:, 0, :, 0], scalar1=flag_t[:, 0:1]),
                     reads=[(GG, 0), flag_t], writes=[(FV, 0)])
                S.op("dve", lambda e: e.tensor_scalar_mul(out=FV[:, 1, :], in0=modT[:, 0:16, 0], scalar1=flag_t[:, 0:1]),
                     reads=[modT, flag_t], writes=[(FV, 1)])
                ckpt(29)
                dbg_out("modT", modT, [128, 96, 17])
                S.barrier()

            ckpt(3)
            def make_u(xsrc, ntok, uT, col0, kind, xt, xb, st, xr):
                s_ = xr.next()
                S.dma("sp", xt[0:ntok, s_, :], xsrc, writes=[(xt, s_)])
                S.op("dve", lambda e: e.memset(st[:, s_, :], 0.0), writes=[(st, s_)])
                S.op("act", lambda e: e.activation(out=xb[0:ntok, s_, :], in_=xt[0:ntok, s_, :], func=AF.Square,
                                                   accum_out=st[0:ntok, s_, 0:1]), reads=[(xt, s_), (st, s_)],
                     writes=[(xb, s_), (st, s_)])
                S.op("dve", lambda e: e.tensor_scalar(out=st[0:ntok, s_, 1:2], in0=st[0:ntok, s_, 0:1], scalar1=1.0 / D,
                                                      scalar2=EPS, op0=ALU.mult, op1=ALU.add), reads=[(st, s_)],
                     writes=[(st, s_)])
                S.op("act", lambda e: e.activation(out=st[0:ntok, s_, 1:2], in_=st[0:ntok, s_, 1:2], func=AF.Ln),
                     reads=[(st, s_)], writes=[(st, s_)])
                S.op("act", lambda e: e.activation(out=st[0:ntok, s_, 1:2], in_=st[0:ntok, s_, 1:2], func=AF.Exp, scale=-0.5),
                     reads=[(st, s_)], writes=[(st, s_)])
                S.op("act", lambda e: e.activation(out=xb[0:ntok, s_, :], in_=xt[0:ntok, s_, :], func=AF.Copy,
                                                   scale=st[0:ntok, s_, 1:2]), reads=[(xt, s_), (st, s_)],
                     writes=[(xb, s_)])
                for k in range(16):
                    sl = ps_next()
                    transpose_to(PS, sl, PS[:, sl, 0:ntok], xb[0:ntok, s_, k * 128:(k + 1) * 128],
                                 CB[0:ntok, 0, 0:ntok], reads=[(xb, s_), (CB, 0)])
                    dst = uT[:, k, col0:col0 + ntok]
                    if kind == "own":
                        S.op("act", lambda e, sl=sl, k=k, dst=dst: e.activation(
                            out=dst, in_=PS[:, sl, 0:ntok], func=AF.Identity, scale=GG[:, 0, k, 0:1],
                            bias=modT[:, k, 0:1]), reads=[(PS, sl), (GG, 0), modT], writes=[(uT, k)])
                    elif kind == "pre":
                        S.op("act", lambda e, sl=sl, k=k, dst=dst: e.activation(
                            out=dst, in_=PS[:, sl, 0:ntok], func=AF.Identity, scale=FV[:, 0, k:k + 1],
                            bias=FV[:, 1, k:k + 1]), reads=[(PS, sl), FV], writes=[(uT, k)])
                    else:
                        dv = dst.rearrange("p (b t) -> p b t", t=4)
                        S.op("dve", lambda e, sl=sl, k=k, dv=dv: e.tensor_tensor(
                            out=dv, in0=PS[:, sl, 0:64].rearrange("p (b t) -> p b t", t=4),
                            in1=GG[:, 0, k, 1:17].unsqueeze(2).broadcast_to([128, 16, 4]), op=ALU.mult),
                            reads=[(PS, sl), (GG, 0)], writes=[(uT, k)])
                        S.op("dve", lambda e, k=k, dv=dv: e.tensor_tensor(
                            out=dv, in0=dv, in1=modT[:, k, 1:17].unsqueeze(2).broadcast_to([128, 16, 4]), op=ALU.add),
                            reads=[(uT, k), modT], writes=[(uT, k)])

            ogT = S.sb("ogT", [128, 8, 1088], BF16, nslots=8)
            ypsT = S.sb("ypsT", [128, 8, 1088], BF16, nslots=8)
            scD = ExitStack()
            scD.__enter__()
            Sst = S.sb("Sst", [128, NH, 128], F32, nslots=NH, es=scD)
            Sb = S.sb("Sb", [128, NH, 128], BF16, nslots=NH, es=scD)
            S.op("dve", lambda e: e.memset(Sst.ap, 0.0), writes=[Sst])
            S.op("dve", lambda e: e.memset(Sb.ap, 0.0), writes=[Sb])
            TF = TB = TN = SST = None
            WAB = S.sb("wab", [128, 16, 16], BF16, es=scD)
            S.dma("pool", WAB.ap, wsrc(w_in[:, AOFF:AOFF + 16]), writes=[WAB])

            def decay_prep(uT, tiles, sc):
                nt = len(tiles)
                DP = S.sb("dp", [128, 8, nt, 8], F32, nslots=8, es=sc)
                ab = S.sb("ab", [128, nt, 16], F32, es=sc)
                S.op("dve", lambda e: e.memset(DP.ap, 0.0), writes=[DP])
                S.op("dve", lambda e: e.memset(ab.ap, 0.0), writes=[ab])
                sl = ps_next()

                def fn(e):
                    ins = None
                    for ti, (c0, n, smp) in enumerate(tiles):
                        for k in range(16):
                            ins = e.matmul(PS[0:n, sl, ti * 16:(ti + 1) * 16], lhsT=uT[:, k, c0:c0 + n], rhs=WAB[:, k, :],
                                           start=(k == 0), stop=(k == 15))
                    return ins
                S.op("pe", fn, reads=[uT, WAB], writes=[(PS, sl)])
                for ti, (c0, n, smp) in enumerate(tiles):
                    S.op("act", lambda e, ti=ti, n=n: e.copy(out=ab[0:n, ti, :], in_=PS[0:n, sl, ti * 16:(ti + 1) * 16]),
                         reads=[(PS, sl)], writes=[ab])
                bc = lambda t: t.ap.unsqueeze(1).broadcast_to([128, nt, 8])
                S.op("dve", lambda e: e.tensor_tensor(out=DP[:, 0], in0=ab[:, :, 0:8], in1=bc(dtb_t), op=ALU.add),
                     reads=[ab, dtb_t], writes=[(DP, 0)])
                S.op("act", lambda e: e.activation(out=DP[:, 0], in_=DP[:, 0], func=AF.Exp), reads=[(DP, 0)], writes=[(DP, 0)])
                S.op("act", lambda e: e.activation(out=DP[:, 0], in_=DP[:, 0], func=AF.Ln, bias=1.0), reads=[(DP, 0)],
                     writes=[(DP, 0)])
                S.op("dve", lambda e: e.tensor_tensor(out=DP[:, 0], in0=DP[:, 0], in1=bc(nea_t), op=ALU.mult),
                     reads=[(DP, 0), nea_t], writes=[(DP, 0)])
                S.op("act", lambda e: e.activation(out=DP[:, 1], in_=ab[:, :, 8:16], func=AF.Sigmoid), reads=[ab],
                     writes=[(DP, 1)])
                S.op("dve", lambda e: e.tensor_scalar_mul(out=DP[:, 2], in0=DP[:, 1], scalar1=-1.0), reads=[(DP, 1)],
                     writes=[(DP, 2)])
                sl2 = ps_next()

                def fn2(e):
                    ins = None
                    for ti, (c0, n, smp) in enumerate(tiles):
                        e.matmul(PS[0:n, sl2, ti * 8:(ti + 1) * 8], lhsT=C[0:n, BTRI if smp else TRI, 0:n],
                                 rhs=DP[0:n, 0, ti, :], start=True, stop=True)
                        ins = e.matmul(PS[0:n, sl2, 256 + ti * 8:256 + (ti + 1) * 8], lhsT=C[0:n, BLK if smp else ONESF, 0:n],
                                       rhs=DP[0:n, 0, ti, :], start=True, stop=True)
                    return ins
                S.op("pe", fn2, reads=[(DP, 0), C], writes=[(PS, sl2)])
                for ti, (c0, n, smp) in enumerate(tiles):
                    S.op("act", lambda e, ti=ti, n=n: e.copy(out=DP[0:n, 3, ti, :], in_=PS[0:n, sl2, ti * 8:(ti + 1) * 8]),
                         reads=[(PS, sl2)], writes=[(DP, 3)])
                    S.op("act", lambda e, ti=ti, n=n: e.copy(out=DP[0:n, 4, ti, :],
                                                             in_=PS[0:n, sl2, 256 + ti * 8:256 + (ti + 1) * 8]),
                         reads=[(PS, sl2)], writes=[(DP, 4)])
                S.op("dve", lambda e: e.tensor_tensor(out=DP[:, 5], in0=DP[:, 4], in1=DP[:, 3], op=ALU.subtract),
                     reads=[(DP, 3), (DP, 4)], writes=[(DP, 5)])
                S.op("act", lambda e: e.activation(out=DP[:, 5], in_=DP[:, 5], func=AF.Exp), reads=[(DP, 5)], writes=[(DP, 5)])
                S.op("act", lambda e: e.activation(out=DP[:, 6], in_=DP[:, 4], func=AF.Exp), reads=[(DP, 4)], writes=[(DP, 6)])
                S.op("act", lambda e: e.activation(out=DP[:, 7], in_=DP[:, 3], func=AF.Exp), reads=[(DP, 3)], writes=[(DP, 7)])
                S.op("dve", lambda e: e.tensor_tensor(out=DP[:, 7], in0=DP[:, 7], in1=DP[:, 1], op=ALU.mult),
                     reads=[(DP, 7), (DP, 1)], writes=[(DP, 7)])
                return DP

            cur_mode = ['A']
            turn = [0]

            def unit(ui, DP, h, ti, n, smp, kT_c, qT_c, ktok_c, vtok_c, do_o, kreads, opost, sm):
                TRIc, STRc = (BTRI, BSTRICT) if smp else (TRI, STRICT)
                col = lambda j: DP[0:n, j, ti, h:h + 1]
                tf_ = lambda s_: TF.b[0:n, s_, 0:n]
                tb_ = lambda s_: TB.b[0:n, s_, 0:n]
                tnf_ = lambda s_: TN.b[0:n, s_, 0:n]
                tn_ = (lambda s_: TN.b[0:n, s_, 0:n].bitcast(F32R)) if NEUMANN_F32R else tnf_
                tr_ = tn_
                f1 = ui * 6
                S.op("dve", lambda e: e.tensor_scalar_mul(out=tf_(f1), in0=C[0:n, TRIc, 0:n], scalar1=col(0)),
                     reads=[C, (DP, 0)], writes=[(TF.b, f1)])
                sld = ps_next()
                mm_group(sld, PS[:, sld, 0:n], [(C[0:n, ONESF, :], tf_(f1))], reads=[C, (TF.b, f1)])
                f2, f3 = ui * 6 + 1, ui * 6 + 2
                S.op("dve", lambda e: e.tensor_scalar(out=tf_(f2), in0=PS[0:n, sld, 0:n], scalar1=col(3), scalar2=0.0,
                                                      op0=ALU.subtract, op1=ALU.max), reads=[(PS, sld), (DP, 3)],
                     writes=[(TF.b, f2)])
                S.op("dve", lambda e: e.tensor_scalar(out=tf_(f3), in0=PS[0:n, sld, 0:n], scalar1=col(3), scalar2=0.0,
                                                      op0=ALU.subtract, op1=ALU.min), reads=[(PS, sld), (DP, 3)],
                     writes=[(TF.b, f3)])
                if debug is not None and 'trace' in debug and cur_mode[0] == 'B' and h == 0:
                    print("UNIT", ti, "sld", sld, "f1,f2,f3", f1, f2, f3, "cnt", dict(S.cnt), flush=True)
                fed = None
                if do_o:
                    fed = ui * 6 + 3
                    S.op("dve", lambda e: e.tensor_copy(out=TF.b[:, fed, 0:n], in_=PS[:, sld, 0:n]),
                         reads=[(PS, sld)], writes=[(TF.b, fed)])
                    S.op("act", lambda e: e.activation(out=TF.b[:, fed, 0:n], in_=TF.b[:, fed, 0:n], func=AF.Exp),
                         reads=[(TF.b, fed)], writes=[(TF.b, fed)])
                yield
                if smp:
                    S.op("dve", lambda e: e.tensor_copy(
                        out=sm["cds"].ap, in_=TF.b[:, fed, 0:64].rearrange("p (b t) -> p b t", t=4)[:, :, 3]),
                        reads=[(TF.b, fed)], writes=[sm["cds"]])
                S.op("act", lambda e: e.activation(out=tf_(f2), in_=tf_(f2), func=AF.Exp, scale=-1.0), reads=[(TF.b, f2)],
                     writes=[(TF.b, f2)])
                S.op("act", lambda e: e.activation(out=tf_(f3), in_=tf_(f3), func=AF.Exp), reads=[(TF.b, f3)],
                     writes=[(TF.b, f3)])
                S.op("dve", lambda e: e.tensor_tensor(out=tf_(f2), in0=tf_(f2), in1=C[0:n, STRc, 0:n], op=ALU.mult),
                     reads=[(TF.b, f2), C], writes=[(TF.b, f2)])
                S.op("dve", lambda e: e.tensor_tensor(out=tf_(f3), in0=tf_(f3), in1=C[0:n, TRIc, 0:n], op=ALU.mult),
                     reads=[(TF.b, f3), C], writes=[(TF.b, f3)])
                yield
                slg = ps_next()
                mm_group(slg, PS[0:n, slg, 0:n], [(kT_c, kT_c)], reads=kreads)
                b1 = ui * 10
                S.op("dve", lambda e: e.scalar_tensor_tensor(out=tn_(b1), in0=PS[0:n, slg, 0:n], scalar=col(2), in1=tf_(f2),
                                                             op0=ALU.mult, op1=ALU.mult),
                     reads=[(PS, slg), (DP, 2), (TF.b, f2)], writes=[(TN.b, b1)])
                pb = ps_next()
                transpose_to(PS, pb, PS[0:n, pb, 0:n], tnf_(b1), C[0:n, IDF, 0:n], reads=[(TN.b, b1), C])
                b2 = ui * 10 + 1
                S.op("act", lambda e: e.copy(out=tn_(b2), in_=PS[0:n, pb, 0:n]), reads=[(PS, pb)], writes=[(TN.b, b2)])
                bx, by = ui * 10 + 8, ui * 10 + 9
                S.op("dve", lambda e: e.tensor_tensor(out=tn_(bx), in0=tn_(b2), in1=C[0:n, IDF, 0:n], op=ALU.add),
                     reads=[(TN.b, b2), C], writes=[(TN.b, bx)])
                S.op("dve", lambda e: e.tensor_tensor(out=tn_(by), in0=tn_(b1), in1=C[0:n, IDF, 0:n], op=ALU.add),
                     reads=[(TN.b, b1), C], writes=[(TN.b, by)])
                UD = debug is not None and 'udump' in debug and cur_mode[0] == 'A' and h == 0 and ti == 0
                if UD:
                    dbg_out("u_N", TN.b, [128, 128], F32, ap=TN.b[:, b1, :])
                    dbg_out("u_Lm", TF.b, [128, 128], F32, ap=TF.b[:, f2, :])
                    dbg_out("u_DP", DP, [128, 8 * DP.ap.shape[2] * 8], F32, ap=DP.ap.rearrange("p a b c -> p (a b c)"))
                yield
                curN, curP = b1, b2
                nsq = 1 if smp else 6
                for lv in range(nsq):
                    lastlv = lv == nsq - 1
                    slp = ps_next()
                    mm_group(slp, PS[0:n, slp, 0:n], [(tr_(curN), tr_(curP))], reads=[(TN.b, curN), (TN.b, curP)])
                    lset = ui * 10 + (2 if lv % 2 == 0 else 6)
                    bp2 = lset
                    S.op("act", lambda e, slp=slp, bp2=bp2: e.copy(out=tn_(bp2), in_=PS[0:n, slp, 0:n]), reads=[(PS, slp)],
                         writes=[(TN.b, bp2)])
                    bn2 = None
                    if not lastlv:
                        sln = ps_next()
                        mm_group(sln, PS[0:n, sln, 0:n], [(tr_(curP), tr_(curN))], reads=[(TN.b, curN), (TN.b, curP)])
                        bn2 = lset + 1
                        S.op("act", lambda e, sln=sln, bn2=bn2: e.copy(out=tn_(bn2), in_=PS[0:n, sln, 0:n]),
                             reads=[(PS, sln)], writes=[(TN.b, bn2)])
                    yield
                    slx = ps_next()
                    mm_group(slx, PS[0:n, slx, 0:n], [(tr_(by), tr_(bp2))], reads=[(TN.b, by), (TN.b, bp2)])
                    bx2 = lset + 2
                    S.op("dve", lambda e, slx=slx, bx2=bx2, bx=bx: e.tensor_tensor(out=tn_(bx2), in0=PS[0:n, slx, 0:n],
                                                                                  in1=tn_(bx), op=ALU.add),
                         reads=[(PS, slx), (TN.b, bx)], writes=[(TN.b, bx2)])
                    if not lastlv:
                        sly = ps_next()
                        mm_group(sly, PS[0:n, sly, 0:n], [(tr_(bx), tr_(bn2))], reads=[(TN.b, bx), (TN.b, bn2)])
                        by2 = lset + 3
                        S.op("dve", lambda e, sly=sly, by2=by2, by=by: e.tensor_tensor(out=tn_(by2), in0=PS[0:n, sly, 0:n],
                                                                                      in1=tn_(by), op=ALU.add),
                             reads=[(PS, sly), (TN.b, by)], writes=[(TN.b, by2)])
                        by, curN = by2, bn2
                    bx, curP = bx2, bp2
                    yield
                if UD:
                    dbg_out("u_X", TN.b, [128, 128], F32, ap=TN.b[:, bx, :])
                yield
                bx16 = ui * 8
                S.op("act", lambda e: e.copy(out=tb_(bx16), in_=tn_(bx)), reads=[(TN.b, bx)], writes=[(TB.b, bx16)])
                bx = bx16
                bvb, bkb, bkt = ui * 8 + 1, ui * 8 + 2, ui * 8 + 3
                S.op("dve", lambda e: e.tensor_scalar_mul(out=TB.b[0:n, bvb, :], in0=vtok_c, scalar1=col(1)),
                     reads=kreads + [(DP, 1)], writes=[(TB.b, bvb)])
                S.op("dve", lambda e: e.tensor_scalar_mul(out=TB.b[0:n, bkb, :], in0=ktok_c, scalar1=col(7)),
                     reads=kreads + [(DP, 7)], writes=[(TB.b, bkb)])
                S.op("dve", lambda e: e.tensor_scalar_mul(out=TB.b[0:n, bkt, :], in0=ktok_c, scalar1=col(5)),
                     reads=kreads + [(DP, 5)], writes=[(TB.b, bkt)])
                slu = ps_next()
                mm_group(slu, PS[0:n, slu, 0:128], [(tb_(bx), TB.b[0:n, bvb, :])], reads=[(TB.b, bx), (TB.b, bvb)])
                fub = ui * 6 + 4
                S.op("act", lambda e: e.copy(out=TF.b[0:n, fub, :], in_=PS[0:n, slu, 0:128]), reads=[(PS, slu)],
                     writes=[(TF.b, fub)])
                slw = ps_next()
                mm_group(slw, PS[:, slw, 0:n], [(TB.b[0:n, bkb, :], tb_(bx))], reads=[(TB.b, bx), (TB.b, bkb)])
                bwd = ui * 8 + 4
                S.op("act", lambda e: e.copy(out=TB.b[:, bwd, 0:n], in_=PS[:, slw, 0:n]), reads=[(PS, slw)],
                     writes=[(TB.b, bwd)])
                bqd = bqk = None
                if do_o:
                    bqd, bqk = ui * 8 + 5, ui * 8 + 6
                    S.op("dve", lambda e: e.tensor_tensor(out=TB.b[:, bqd, 0:n], in0=qT_c, in1=TF.b[:, fed, 0:n], op=ALU.mult),
                         reads=kreads + [(TF.b, fed)], writes=[(TB.b, bqd)])
                    slq = ps_next()
                    mm_group(slq, PS[0:n, slq, 0:n], [(kT_c, qT_c)], reads=kreads)
                    S.op("dve", lambda e: e.tensor_tensor(out=tb_(bqk), in0=PS[0:n, slq, 0:n], in1=tf_(f3), op=ALU.mult),
                         reads=[(PS, slq), (TF.b, f3)], writes=[(TB.b, bqk)])
                if UD:
                    dbg_out("u_ub", TF.b, [128, 128], F32, ap=TF.b[:, fub, :])
                    dbg_out("u_wd", TB.b, [128, 128], BF16, ap=TB.b[:, bwd, :])
                    dbg_out("u_kt", TB.b, [128, 128], BF16, ap=TB.b[:, bkt, :])
                yield
                bu = ui * 8 + 7
                if not smp:
                    while turn[0] != ti:
                        yield
                    sl = ps_next()
                    mm_group(sl, PS[0:n, sl, 0:128], [(TB.b[:, bwd, 0:n], Sb[:, h, :])], reads=[(TB.b, bwd), (Sb, h)])
                    S.op("dve", lambda e: e.tensor_tensor(out=TB.b[0:n, bu, :], in0=TF.b[0:n, fub, :], in1=PS[0:n, sl, 0:128],
                                                          op=ALU.subtract), reads=[(PS, sl), (TF.b, fub)], writes=[(TB.b, bu)])
                    yield
                    if do_o:
                        slo = ps_next()
                        mm_group(slo, PS[0:n, slo, 0:128], [(TB.b[:, bqd, 0:n], Sb[:, h, :]), (tb_(bqk), TB.b[0:n, bu, :])],
                                 reads=[(TB.b, bqd), (Sb, h), (TB.b, bqk), (TB.b, bu)])
                        opost(slo)
                    yield
                    slk = ps_next()
                    mm_group(slk, PS[:, slk, 0:128], [(TB.b[0:n, bkt, :], TB.b[0:n, bu, :])], reads=[(TB.b, bkt), (TB.b, bu)])
                    S.op("dve", lambda e: e.scalar_tensor_tensor(out=Sb[:, h, :], in0=Sst[:, h, :], scalar=DP[:, 6, ti, h:h + 1],
                                                                 in1=PS[:, slk, 0:128], op0=ALU.mult, op1=ALU.add),
                         reads=[(Sst, h), (DP, 6), (PS, slk)], writes=[(Sb, h)])
                    S.op("dve", lambda e: e.scalar_tensor_tensor(out=Sst[:, h, :], in0=Sst[:, h, :], scalar=DP[:, 6, ti, h:h + 1],
                                                                 in1=PS[:, slk, 0:128], op0=ALU.mult, op1=ALU.add),
                         reads=[(Sst, h), (DP, 6), (PS, slk)], writes=[(Sst, h)])
                    turn[0] += 1
                    if UD:
                        dbg_out("u_S", Sst, [128, 128], F32, ap=Sst[:, 0, :])
                        dbg_out("u_u", TB.b, [128, 128], BF16, ap=TB.b[:, bu, :])
                else:
                    Ss, Ssb, wdm, qdm, ktm, cds = sm["Ss"], sm["Ssb"], sm["wdm"], sm["qdm"], sm["ktm"], sm["cds"]
                    S.op("dve", lambda e: e.tensor_tensor(out=wdm.ap, in0=TB.b[:, bwd, 0:64].unsqueeze(1).broadcast_to([128, 16, 64]),
                                                          in1=CM.ap, op=ALU.mult), reads=[(TB.b, bwd), CM], writes=[wdm])
                    S.op("dve", lambda e: e.tensor_tensor(out=qdm.ap, in0=TB.b[:, bqd, 0:64].unsqueeze(1).broadcast_to([128, 16, 64]),
                                                          in1=CM.ap, op=ALU.mult), reads=[(TB.b, bqd), CM], writes=[qdm])
                    S.op("dve", lambda e: e.tensor_tensor(out=ktm[0:64], in0=TB.b[0:64, bkt, :].unsqueeze(1).broadcast_to([64, 16, 128]),
                                                          in1=RM[0:64, :].unsqueeze(2).broadcast_to([64, 16, 128]), op=ALU.mult),
                         reads=[(TB.b, bkt), RM], writes=[ktm])
                    yield
                    sl = ps_next()
                    mm_group(sl, PS[0:64, sl, 0:128], [(wdm[:, b_, :], Ssb[:, b_, :]) for b_ in range(16)], reads=[wdm, Ssb])
                    S.op("dve", lambda e: e.tensor_tensor(out=TB.b[0:64, bu, :], in0=TF.b[0:64, fub, :], in1=PS[0:64, sl, 0:128],
                                                          op=ALU.subtract), reads=[(PS, sl), (TF.b, fub)], writes=[(TB.b, bu)])
                    yield
                    slo = ps_next()
                    mm_group(slo, PS[0:64, slo, 0:128],
                             [(qdm[:, b_, :], Ssb[:, b_, :]) for b_ in range(16)] + [(tb_(bqk), TB.b[0:64, bu, :])],
                             reads=[qdm, Ssb, (TB.b, bqk), (TB.b, bu)])
                    opost(slo)
                    for g4 in range(4):
                        yield
                        slk = ps_next()

                        def fk(e, slk=slk, g4=g4):
                            ins = None
                            for j in range(4):
                                ins = e.matmul(PS[:, slk, j * 128:(j + 1) * 128], lhsT=ktm[0:64, 4 * g4 + j, :],
                                               rhs=TB.b[0:64, bu, :], start=True, stop=True)
                            return ins
                        S.op("pe", fk, reads=[ktm, (TB.b, bu)], writes=[(PS, slk)])
                        sv = Ss[:, 4 * g4:4 * g4 + 4, :]
                        S.op("dve", lambda e, sv=sv, g4=g4: e.tensor_tensor(
                            out=sv, in0=sv, in1=cds[:, 4 * g4:4 * g4 + 4].unsqueeze(2).broadcast_to([128, 4, 128]), op=ALU.mult),
                            reads=[Ss, cds], writes=[Ss])
                        S.op("dve", lambda e, sv=sv, slk=slk: e.tensor_tensor(
                            out=sv, in0=sv, in1=PS[:, slk, :].rearrange("p (j v) -> p j v", v=128), op=ALU.add),
                            reads=[Ss, (PS, slk)], writes=[Ss])
                    S.dma("sp", o_delta_s[:, h].rearrange("b k v -> k b v"), Ss.ap, reads=[Ss], is_out=True)

            sconvT = S.sb("sconvT", [128, 24, 48], F32, es=scD)
            nconvT = S.sb("nconvT", [128, 24, 51], F32, nslots=24, es=scD)

            def head_pass(mode, uT, DP, sc):
                nonlocal TF, TB, TN, SST
                cur_mode[0] = mode
                KU_ = KUNITS if mode == "B" else KUNITS_A
                TF = Ring(S, "tf", [128, 128], F32, 6 * KU_, es=sc)
                TB = Ring(S, "tb", [128, 128], BF16, 8 * KU_, es=sc)
                TN = Ring(S, "tn", [128, 128], F32, 10 * KU_, es=sc)
                SST = Ring(S, "sst", [128, 4], F32, KU_, es=sc)
                UDH = debug is not None and 'udump' in debug and mode == 'A'
                B_ = mode == "B"
                L = 1152 if B_ else 1024
                o0 = 128 if B_ else 0
                W_ = L + (112 if B_ else 0)
                RW = 3 + W_
                stg = S.sb("stg" + mode, [128, 1, RW], F32, nslots=1, es=sc)
                yrow = S.sb("yrow" + mode, [128, W_], F32, es=sc)
                crow = S.sb("crow" + mode, [128, 2, W_], BF16, nslots=2, es=sc)
                cmap = {0: 0, 1: 0, 2: 1}
                nrmP = [S.sb("nrm" + mode, [128, 2, W_], BF16, nslots=2, es=sc) for _ in range(2)]
                ktokP = [S.sb("ktok" + mode, [128, 9, 128], BF16, nslots=9, es=sc) for _ in range(2)]
                vtokP = [S.sb("vtok" + mode, [128, 9, 128], BF16, nslots=9, es=sc) for _ in range(2)]
                cmpP = [None, None]
                wzs = {}
                sm = None
                if B_:
                    cmpP = [S.sb("cmp", [128, 3, 64], BF16, nslots=3, es=sc) for _ in range(2)]
                    gz = S.sb("gz", [128, KUNITS, 128], F32, nslots=KUNITS, es=sc)
                    ogtok = S.sb("ogtok", [128, KUNITS, 128], BF16, nslots=KUNITS, es=sc)
                    sm = {"Ss": S.sb("Ss", [128, 16, 128], F32, es=sc), "Ssb": S.sb("Ssb", [128, 16, 128], BF16, es=sc),
                          "wdm": S.sb("wdm", [128, 16, 64], BF16, es=sc), "qdm": S.sb("qdm", [128, 16, 64], BF16, es=sc),
                          "ktm": S.sb("ktm", [128, 16, 128], BF16, es=sc), "cds": S.sb("cds", [128, 16], F32, es=sc)}
                WZ = S.sb("WZ", [128, 16, 128], BF16, es=sc) if B_ else None
                S.op("dve", lambda e: e.memset(stg.ap, 0.0), writes=[stg])
                comps = [("q", 0, QOFF), ("k", 1, KOFF), ("v", 2, VOFF)] if B_ else [("k", 1, KOFF), ("v", 2, VOFF)]
                ptiles = tok_tiles(L)
                ext = lambda ci: stg[:, 0, 3 + L:3 + L + 112].rearrange("p (b j) -> p b j", j=7)
                def prep(h):
                    nrm, ktok, vtok, cmp_ = nrmP[h % 2], ktokP[h % 2], vtokP[h % 2], cmpP[h % 2]
                    for (nm, ci, off) in comps:
                        chn = ci * 8 + h
                        ws = load_w(w_in[:, off + h * 128:off + (h + 1) * 128], 16)
                        for (a, b) in ptiles:
                            sl = ps_next()
                            mm_group(sl, PS[:, sl, 0:b - a], [(WR.b[:, ws, k, :], uT[:, k, a:b]) for k in range(16)],
                                     reads=[(WR.b, ws), uT])
                            S.op("act", lambda e, sl=sl, a=a, b=b, ci=ci: e.copy(out=stg[:, 0, 3 + a:3 + b], in_=PS[:, sl, 0:b - a]),
                                 reads=[(PS, sl)], writes=[(stg, 0)])
                            yield
                        if B_:
                            sl = ps_next()
                            mm_group(sl, PS[:, sl, 0:64], [(WR.b[:, ws, k, :], uT[:, k, 1152:1216]) for k in range(16)],
                                     reads=[(WR.b, ws), uT])
                            S.op("act", lambda e, sl=sl, ci=ci: e.copy(out=ext(ci)[:, :, 3:7],
                                                                       in_=PS[:, sl, 0:64].rearrange("p (b t) -> p b t", t=4)),
                                 reads=[(PS, sl)], writes=[(stg, 0)])
                            S.op("dve", lambda e, ci=ci, chn=chn: e.tensor_copy(
                                out=ext(ci)[:, :, 0:3], in_=sconvT[:, chn, :].rearrange("p (b j) -> p b j", j=3)),
                                reads=[sconvT], writes=[(stg, 0)])
                            S.op("dve", lambda e, ci=ci, chn=chn: e.tensor_copy(
                                out=nconvT[:, chn, 0:48].rearrange("p (b j) -> p b j", j=3), in_=ext(ci)[:, :, 4:7]),
                                reads=[(stg, 0)], writes=[(nconvT, chn)])
                            S.op("dve", lambda e, ci=ci, chn=chn: e.tensor_copy(out=nconvT[:, chn, 48:51], in_=stg[:, 0, L:L + 3]),
                                 reads=[(stg, 0)], writes=[(nconvT, chn)])
                        S.op("dve", lambda e, ci=ci, chn=chn: e.tensor_scalar_mul(out=yrow.ap, in0=stg[:, 0, 0:W_],
                                                                                 scalar1=cwT[:, chn:chn + 1]),
                             reads=[(stg, 0), cwT], writes=[yrow])
                        for j in range(1, 4):
                            S.op("dve", lambda e, ci=ci, chn=chn, j=j: e.scalar_tensor_tensor(
                                out=yrow.ap, in0=stg[:, 0, j:j + W_], scalar=cwT[:, j * 24 + chn:j * 24 + chn + 1], in1=yrow.ap,
                                op0=ALU.mult, op1=ALU.add), reads=[(stg, 0), cwT, yrow], writes=[yrow])
                            yield
                        S.op("act", lambda e, ci=ci: e.activation(out=crow[:, cmap[ci], :], in_=yrow.ap, func=AF.Silu), reads=[yrow],
                             writes=[(crow, cmap[ci])])
                        yield
                        if nm in ("q", "k"):
                            S.op("act", lambda e, ci=ci: e.activation(out=nrm[:, ci, :], in_=crow[:, cmap[ci], :], func=AF.Square),
                                 reads=[(crow, cmap[ci])], writes=[(nrm, ci)])
                            for (a, b) in tok_tiles(W_):
                                sl = ps_next()
                                mm_group(sl, PS[:, sl, 0:b - a], [(CB[:, 1, :], nrm[:, ci, a:b])], reads=[CB, (nrm, ci)])
                                S.op("act", lambda e, sl=sl, a=a, b=b: e.activation(
                                    out=yrow[:, a:b], in_=PS[:, sl, 0:b - a], func=AF.Ln, bias=EPS),
                                    reads=[(PS, sl)], writes=[yrow])
                                S.op("act", lambda e, a=a, b=b: e.activation(
                                    out=yrow[:, a:b], in_=yrow[:, a:b], func=AF.Exp, scale=-0.5),
                                    reads=[yrow], writes=[yrow])
                                yield
                            S.op("dve", lambda e, ci=ci, nm=nm: e.scalar_tensor_tensor(
                                out=nrm[:, ci, :], in0=crow[:, cmap[ci], :], scalar=(128.0 ** -0.5 if nm == "q" else 1.0), in1=yrow.ap,
                                op0=ALU.mult, op1=ALU.mult), reads=[(crow, cmap[ci]), yrow], writes=[(nrm, ci)])
                    ckpt(mode + '61')
                    for c in range(8):
                        kc = o0 + 128 * c
                        for (src, dstb) in ((nrm[:, 1, kc:kc + 128], ktok), (crow[:, 1, kc:kc + 128], vtok)):
                            pb = ps_next()
                            transpose_to(PS, pb, PS[:, pb, 0:128], src, CB[:, 0, :], reads=[(nrm, 1), (crow, 1), CB])
                            S.op("act", lambda e, pb=pb, dstb=dstb, c=c: e.copy(out=dstb[:, c, :], in_=PS[:, pb, 0:128]),
                                 reads=[(PS, pb)], writes=[(dstb, c)])
                        yield
                    if B_:
                        for i, src in enumerate((nrm[:, 0, :], nrm[:, 1, :], crow[:, 1, :])):
                            S.op("dve", lambda e, i=i, src=src: e.tensor_copy(
                                out=cmp_[:, i, :].rearrange("p (b t) -> p b t", t=4),
                                in_=src[:, L:L + 112].rearrange("p (b j) -> p b j", j=7)[:, :, 3:7]),
                                reads=[(nrm, 0), (nrm, 1), (crow, 1)], writes=[(cmp_, i)])
                        for (i, dstb) in ((1, ktok), (2, vtok)):
                            pb = ps_next()
                            transpose_to(PS, pb, PS[0:64, pb, 0:128], cmp_[:, i, :], CB[:, 0, :], reads=[(cmp_, i), CB])
                            S.op("act", lambda e, pb=pb, dstb=dstb: e.copy(out=dstb[0:64, 8, :], in_=PS[0:64, pb, 0:128]),
                                 reads=[(PS, pb)], writes=[(dstb, 8)])

                    yield

                for _g in prep(0):
                    pass
                for h in range(NH):
                    nrm, ktok, vtok, cmp_ = nrmP[h % 2], ktokP[h % 2], vtokP[h % 2], cmpP[h % 2]
                    if B_:
                        zsrc = wsrc(w_in[:, ZOFF + h * 128:ZOFF + (h + 1) * 128])
                        for k0 in range(0, 16, 4):
                            S.dma("pool", WZ[:, k0:k0 + 4, :], zsrc[:, k0:k0 + 4, :], writes=[WZ])
                    nxt = prep(h + 1) if h + 1 < NH else None

                    def mk_opost(ui, n, ucol, ogcol, h=h):
                        def opost(slo):
                            gs = ui
                            zsl = ps_next()
                            mm_group(zsl, PS[0:n, zsl, 0:128], [(uT[:, k, ucol:ucol + n], WZ[:, k, :]) for k in range(16)],
                                     reads=[uT, WZ])
                            S.op("act", lambda e: e.activation(out=gz[0:n, gs, :], in_=PS[0:n, zsl, 0:128], func=AF.Silu),
                                 reads=[(PS, zsl)], writes=[(gz, gs)])
                            S.op("dve", lambda e: e.tensor_tensor(out=gz[0:n, gs, :], in0=gz[0:n, gs, :], in1=ong_t[0:n, :], op=ALU.mult),
                                 reads=[(gz, gs), ong_t], writes=[(gz, gs)])
                            ss = ui
                            fj = ui * 6 + 5
                            S.op("dve", lambda e: e.memset(SST.b[:, ss, :], 0.0), writes=[(SST.b, ss)])
                            S.op("act", lambda e: e.activation(out=TF.b[0:n, fj, :], in_=PS[0:n, slo, 0:128], func=AF.Square,
                                                               accum_out=SST.b[0:n, ss, 0:1]),
                                 reads=[(PS, slo), (SST.b, ss)], writes=[(TF.b, fj), (SST.b, ss)])
                            S.op("dve", lambda e: e.tensor_scalar(out=SST.b[0:n, ss, 1:2], in0=SST.b[0:n, ss, 0:1], scalar1=1.0 / 128,
                                                                  scalar2=EPS, op0=ALU.mult, op1=ALU.add), reads=[(SST.b, ss)],
                                 writes=[(SST.b, ss)])
                            S.op("act", lambda e: e.activation(out=SST.b[0:n, ss, 1:2], in_=SST.b[0:n, ss, 1:2], func=AF.Ln),
                                 reads=[(SST.b, ss)], writes=[(SST.b, ss)])
                            S.op("act", lambda e: e.activation(out=SST.b[0:n, ss, 1:2], in_=SST.b[0:n, ss, 1:2], func=AF.Exp,
                                                               scale=-0.5), reads=[(SST.b, ss)], writes=[(SST.b, ss)])
                            S.op("dve", lambda e: e.scalar_tensor_tensor(out=ogtok[0:n, gs, :], in0=PS[0:n, slo, 0:128],
                                                                         scalar=SST.b[0:n, ss, 1:2], in1=gz[0:n, gs, :],
                                                                         op0=ALU.mult, op1=ALU.mult),
                                 reads=[(PS, slo), (SST.b, ss), (gz, gs)], writes=[(ogtok, gs)])
                            pb = ps_next()
                            transpose_to(PS, pb, PS[:, pb, 0:n], ogtok[0:n, gs, :], CB[0:n, 0, 0:n], reads=[(ogtok, gs), CB])
                            S.op("act", lambda e: e.copy(out=ogT[:, h, ogcol:ogcol + n], in_=PS[:, pb, 0:n]), reads=[(PS, pb)],
                                 writes=[(ogT, h)])
                        return opost
                    kreads = [(nrm, 0), (nrm, 1), ktok, vtok]
                    ckpt(mode + '62')
                    pending = []
                    for c in range(8):
                        kc = o0 + 128 * c
                        pending.append(dict(c=c, n=128, smp=False, kT=nrm[:, 1, kc:kc + 128], qT=nrm[:, 0, kc:kc + 128] if B_ else None,
                                            ktok=ktok[:, c, :], vtok=vtok[:, c, :], ucol=kc, ogcol=128 * c, kr=kreads))
                    if B_:
                        pending.append(dict(c=8, n=64, smp=True, kT=cmp_[:, 1, :], qT=cmp_[:, 0, :], ktok=ktok[0:64, 8, :],
                                            vtok=vtok[0:64, 8, :], ucol=1152, ogcol=1024, kr=kreads + [cmp_]))
                    gens = []
                    turn[0] = 0
                    free_ids = list(range(KU_))
                    while pending or gens:
                        if pending and free_ids:
                            a_ = pending.pop(0)
                            ui = free_ids.pop(0)
                            if a_["smp"]:
                                S.dma("sp", sm["Ss"].ap, sdelta[:, h].rearrange("b k v -> k b v"), writes=[sm["Ss"]])
                                S.op("act", lambda e: e.copy(out=sm["Ssb"].ap, in_=sm["Ss"].ap), reads=[sm["Ss"]], writes=[sm["Ssb"]])
                            g_ = unit(ui, DP, h, a_["c"], a_["n"], a_["smp"], a_["kT"], a_["qT"], a_["ktok"], a_["vtok"], B_, a_["kr"],
                                      mk_opost(ui, a_["n"], a_["ucol"], a_["ogcol"]) if B_ else None, sm if a_["smp"] else None)
                            gens.append((g_, ui))
                        for (g_, ui) in list(gens):
                            try:
                                next(g_)
                            except StopIteration:
                                gens.remove((g_, ui))
                                free_ids.append(ui)
                        if nxt is not None:
                            try:
                                next(nxt)
                            except StopIteration:
                                nxt = None
                    if nxt is not None:
                        for _g in nxt:
                            pass

            ckpt(4)
            with ExitStack() as sc:
                uTA = S.sb("uTA", [128, 16, 1024], BF16, nslots=16, es=sc)
                with ExitStack() as scu:
                    xt = S.sb("xtA", [128, 2, D], F32, nslots=2, es=scu)
                    xb = S.sb("xbA", [128, 2, D], BF16, nslots=2, es=scu)
                    st = S.sb("stA", [128, 2, 2], F32, nslots=2, es=scu)
                    xr = Ring.__new__(Ring)
                    xr.n, xr.i = 2, 0
                    for t in range(8):
                        make_u(xpre[t * 128:(t + 1) * 128, :], 128, uTA, t * 128, "pre", xt, xb, st, xr)
                    S.barrier()
                dbg_out("uTA", uTA, [128, 16 * 1024], BF16, ap=uTA.ap.rearrange("p k t -> p (k t)"))
                ckpt(5)
                DPA = decay_prep(uTA, [(128 * c, 128, False) for c in range(8)], sc)
                ckpt(6)
                head_pass("A", uTA, DPA, sc)
                dbg_out("Smid", Sst, [128, 1024], ap=Sst.ap.rearrange("p h v -> p (h v)"))
                S.barrier()
            ckpt(7)
            with ExitStack() as sc:
                uTB = S.sb("uTB", [128, 16, 1216], BF16, nslots=16, es=sc)
                with ExitStack() as sc2:
                    xt = S.sb("xtB", [128, 2, D], F32, nslots=2, es=sc2)
                    xb = S.sb("xbB", [128, 2, D], BF16, nslots=2, es=sc2)
                    st = S.sb("stB", [128, 2, 2], F32, nslots=2, es=sc2)
                    xr = Ring.__new__(Ring)
                    xr.n, xr.i = 2, 0
                    make_u(xpre[896:1024, :], 128, uTB, 0, "pre", xt, xb, st, xr)
                    for t in range(8):
                        make_u(xown[t * 128:(t + 1) * 128, :], 128, uTB, 128 + t * 128, "own", xt, xb, st, xr)
                    make_u(xsm[:, :], 64, uTB, 1152, "smp", xt, xb, st, xr)
                    sct = S.sb("sct", [48, 3072], F32, es=sc2)
                    S.dma("sp", sct.ap, sconv, writes=[sct])
                    for chn in range(24):
                        sl = ps_next()
                        transpose_to(PS, sl, PS[:, sl, 0:48], sct[:, chn * 128:(chn + 1) * 128], C[0:48, IDF, 0:48],
                                     reads=[sct, C])
                        S.op("act", lambda e, sl=sl, chn=chn: e.copy(out=sconvT[:, chn, :], in_=PS[:, sl, 0:48]),
                             reads=[(PS, sl)], writes=[sconvT])
                    S.barrier()
                ckpt(72)
                DPB = decay_prep(uTB, [(128 + 128 * c, 128, False) for c in range(8)] + [(1152, 64, True)], sc)
                ckpt(73)
                if debug is not None and 'shift' in debug:
                    for _ in range(3):
                        S.op("act", lambda e: e.copy(out=flag_t.ap, in_=flag_t.ap), reads=[flag_t], writes=[flag_t])
                with ExitStack() as scH:
                    head_pass("B", uTB, DPB, scH)
                    S.barrier()
                ckpt(74)

                with ExitStack() as scP:
                    PW = 16 + 1152 + 304
                    xrow = S.sb("xrow", [128, 4, PW], F32, nslots=4, es=scP)
                    hist = S.sb("hist", [128, 2, 1024], F32, nslots=2, es=scP)
                    histT = S.sb("histT", [128, 240], F32, es=scP)
                    pwt = S.sb("pwt", [128, 4, 2, 256], BF16, es=scP)
                    ypl = S.sb("ypl", [128, 2, 1088], BF16, nslots=2, es=scP)
                    npoolT = S.sb("npoolT", [128, 8, 256], F32, nslots=8, es=scP)
                    invc = S.sb("invc", [128, 4, 16], F32, es=scP)
                    io_i = S.sb("io_i", [128, 16], I32, es=scP)
                    t16 = S.sb("t16", [128, 16], F32, es=scP)
                    otok = hist
                    S.dma("sp", hist[:, 0, :], spool[0:128, :], writes=[(hist, 0)])
                    S.dma("sp", hist[0:112, 1, :], spool[128:240, :], writes=[(hist, 1)])
                    for g in range(4):
                        S.dma("pool", pwt[:, g], pool_w[g].rearrange("(c p) d -> p c d", p=128), writes=[pwt])
                    S.op("dve", lambda e: e.memset(xrow.ap, 0.0), writes=[xrow])
                    S.op("dve", lambda e: e.memset(npoolT.ap, 0.0), writes=[npoolT])
                    S.op("pool", lambda e: e.iota(io_i.ap, pattern=[[1, 16]], base=1, channel_multiplier=0), writes=[io_i])
                    S.op("dve", lambda e: e.tensor_copy(out=invc[:, 0, :], in_=io_i.ap), reads=[io_i], writes=[invc])
                    S.op("dve", lambda e: e.tensor_scalar_add(out=invc[:, 0, :], in0=invc[:, 0, :], scalar1=pos0_t[:, 0:1]),
                         reads=[invc, pos0_t], writes=[invc])
                    for g in (3, 2, 1, 0):
                        S.op("dve", lambda e, g=g: e.tensor_scalar_min(out=invc[:, g, :], in0=invc[:, 0, :],
                                                                       scalar1=float((2, 4, 8, 16)[g])), reads=[invc], writes=[invc])
                    S.op("dve", lambda e: e.reciprocal(out=invc.ap, in_=invc.ap), reads=[invc], writes=[invc])
                    xe = lambda sl_: xrow[:, sl_, 16 + 1152:PW].rearrange("p (b j) -> p b j", j=19)
                    for g in range(4):
                        wwin = (2, 4, 8, 16)[g]
                        for c2 in range(2):
                            ch = 2 * g + c2
                            xs = 0 if ch % 2 == 0 else 3
                            ws = load_w(w_in[:, POFF + ch * 128:POFF + (ch + 1) * 128], 16)
                            for (a, b) in tok_tiles(1152):
                                sl = ps_next()
                                mm_group(sl, PS[:, sl, 0:b - a], [(WR.b[:, ws, k, :], uTB[:, k, a:b]) for k in range(16)],
                                         reads=[(WR.b, ws), uTB])
                                S.op("act", lambda e, sl=sl, a=a, b=b: e.copy(out=xrow[:, xs, 16 + a:16 + b], in_=PS[:, sl, 0:b - a]),
                                     reads=[(PS, sl)], writes=[(xrow, xs)])
                            sl = ps_next()
                            mm_group(sl, PS[:, sl, 0:64], [(WR.b[:, ws, k, :], uTB[:, k, 1152:1216]) for k in range(16)],
                                     reads=[(WR.b, ws), uTB])
                            S.op("act", lambda e, sl=sl: e.copy(out=xe(xs)[:, :, 15:19],
                                                               in_=PS[:, sl, 0:64].rearrange("p (b t) -> p b t", t=4)),
                                 reads=[(PS, sl)], writes=[(xrow, xs)])
                            for t2, rows in ((0, 128), (1, 112)):
                                sl = ps_next()
                                transpose_to(PS, sl, PS[:, sl, 0:rows], hist[0:rows, t2, ch * 128:(ch + 1) * 128],
                                             C[0:rows, IDF, 0:rows], reads=[(hist, t2), C])
                                S.op("act", lambda e, sl=sl, t2=t2, rows=rows: e.copy(out=histT[:, t2 * 128:t2 * 128 + rows],
                                                                                    in_=PS[:, sl, 0:rows]),
                                     reads=[(PS, sl)], writes=[histT])
                            S.op("dve", lambda e: e.tensor_copy(out=xe(xs)[:, :, 0:15],
                                                                in_=histT.ap.rearrange("p (b j) -> p b j", j=15)),
                                 reads=[histT], writes=[(xrow, xs)])
                            S.op("dve", lambda e, ch=ch: e.tensor_copy(out=npoolT[:, ch, 0:240].rearrange("p (b j) -> p b j", j=15),
                                                                      in_=xe(xs)[:, :, 4:19]), reads=[(xrow, xs)], writes=[(npoolT, ch)])
                            S.op("dve", lambda e, ch=ch: e.tensor_copy(out=npoolT[:, ch, 240:255], in_=xrow[:, xs, 16 + 1152 - 15:16 + 1152]),
                                 reads=[(xrow, xs)], writes=[(npoolT, ch)])
                            cur = xs
                            for lv in range(g + 1):
                                sh = 1 << lv
                                new = 1 + (lv % 2)
                                S.op("dve", lambda e, cur=cur, new=new, sh=sh: e.tensor_tensor(
                                    out=xrow[:, new, 16:PW], in0=xrow[:, cur, 16:PW], in1=xrow[:, cur, 16 - sh:PW - sh], op=ALU.add),
                                    reads=[(xrow, cur)], writes=[(xrow, new)])
                                cur = new
                            S.op("dve", lambda e, cur=cur, c2=c2: e.scalar_tensor_tensor(
                                out=ypl[:, c2, 0:1024], in0=xrow[:, cur, 144:1168], scalar=1.0 / wwin, in1=xrow[:, xs, 144:1168],
                                op0=ALU.mult, op1=ALU.subtract), reads=[(xrow, cur), (xrow, xs)], writes=[(ypl, c2)])
                            S.op("dve", lambda e, cur=cur, g=g: e.tensor_tensor(out=t16.ap, in0=xrow[:, cur, 144:160], in1=invc[:, g, :],
                                                                                 op=ALU.mult), reads=[(xrow, cur), invc], writes=[t16])
                            S.op("dve", lambda e, c2=c2: e.tensor_tensor(out=ypl[:, c2, 0:16], in0=t16.ap, in1=xrow[:, xs, 144:160],
                                                                         op=ALU.subtract), reads=[t16, (xrow, xs)], writes=[(ypl, c2)])
                            S.op("dve", lambda e, cur=cur, c2=c2: e.scalar_tensor_tensor(
                                out=ypl[:, c2, 1024:1088].rearrange("p (b t) -> p b t", t=4), in0=xe(cur)[:, :, 15:19],
                                scalar=1.0 / wwin, in1=xe(xs)[:, :, 15:19], op0=ALU.mult, op1=ALU.subtract),
                                reads=[(xrow, cur), (xrow, xs)], writes=[(ypl, c2)])
                        for dc in range(2):
                            for (a, b) in ((0, 512), (512, 1024), (1024, 1088)):
                                sl = ps_next()
                                mm_group(sl, PS[:, sl, 0:b - a],
                                         [(pwt[:, g, c2, dc * 128:(dc + 1) * 128], ypl[:, c2, a:b]) for c2 in range(2)],
                                         reads=[pwt, ypl])
                                S.op("act", lambda e, sl=sl, a=a, b=b, g=g, dc=dc: e.activation(
                                    out=ypsT[:, 2 * g + dc, a:b], in_=PS[:, sl, 0:b - a], func=AF.Copy,
                                    scale=pscT[:, 2 * g + dc:2 * g + dc + 1]), reads=[(PS, sl), pscT], writes=[(ypsT, 2 * g + dc)])
                    for ch in range(8):
                        for half in range(2):
                            sl = ps_next()
                            transpose_to(PS, sl, PS[:, sl, 0:128], npoolT[:, ch, half * 128:(half + 1) * 128], C[:, IDF, :],
                                         reads=[(npoolT, ch), C])
                            S.op("act", lambda e, sl=sl, ch=ch, half=half: e.copy(out=otok[:, half, ch * 128:(ch + 1) * 128],
                                                                                 in_=PS[:, sl, 0:128]),
                                 reads=[(PS, sl)], writes=[(otok, half)])
                    S.dma("sp", o_pool[0:128, :], otok[:, 0, :], reads=[(otok, 0)], is_out=True)
                    S.dma("sp", o_pool[128:255, :], otok[0:127, 1, :], reads=[(otok, 1)], is_out=True)
                    S.barrier()
                S.dma("sp", o_delta_p.rearrange("h k v -> k h v"), Sst.ap, reads=[Sst], is_out=True)
                octok = S.sb("octok", [51, 3072], F32, es=sc)
                for chn in range(24):
                    sl = ps_next()
                    transpose_to(PS, sl, PS[0:51, sl, 0:128], nconvT[:, chn, :], C[:, IDF, :], reads=[(nconvT, chn), C])
                    S.op("act", lambda e, sl=sl, chn=chn: e.copy(out=octok[:, chn * 128:(chn + 1) * 128], in_=PS[0:51, sl, 0:128]),
                         reads=[(PS, sl)], writes=[octok])
                S.dma("sp", o_conv, octok.ap, reads=[octok], is_out=True)
                S.barrier()
            scD.close()
            ckpt(8)
            T2 = [(0, 512), (512, 1024), (1024, 1088)]
            with ExitStack() as s4:
                MH = S.sb("MH", [128, 16, 1088], BF16, nslots=16, es=s4)
                with ExitStack() as sa:
                    uT2 = S.sb("uT2", [128, 16, 1088], BF16, nslots=16, es=sa)
                    xt = S.sb("xt4", [128, 1, D], F32, nslots=1, es=sa)
                    xb = S.sb("xb4", [128, 1, D], BF16, nslots=1, es=sa)
                    st = S.sb("st4", [128, 1, 2], F32, nslots=1, es=sa)
                    sg = S.sb("sg", [128, 3, 1088], F32, nslots=3, es=sa)
                    xr = Ring.__new__(Ring)
                    xr.n, xr.i = 1, 0
                    for t in range(8):
                        make_u(xown[t * 128:(t + 1) * 128, :], 128, uT2, t * 128, "own", xt, xb, st, xr)
                    make_u(xsm[:, :], 64, uT2, 1024, "smp", xt, xb, st, xr)
                    for n in range(16):
                        for (i, off) in ((0, GAOFF), (1, GBOFF)):
                            ws = load_w(w_in[:, off + n * 128:off + (n + 1) * 128], 16)
                            for (a, b) in T2:
                                sl = ps_next()
                                mm_group(sl, PS[:, sl, 0:b - a], [(WR.b[:, ws, k, :], uT2[:, k, a:b]) for k in range(16)],
                                         reads=[(WR.b, ws), uT2])
                                S.op("act", lambda e, sl=sl, a=a, b=b, i=i: e.activation(out=sg[:, i, a:b], in_=PS[:, sl, 0:b - a],
                                                                                        func=AF.Sigmoid),
                                     reads=[(PS, sl)], writes=[(sg, i)])
                        ws = load_w(w_proj_a[:, n * 128:(n + 1) * 128], 8)
                        for (a, b) in T2:
                            sl = ps_next()
                            mm_group(sl, PS[:, sl, 0:b - a], [(WR.b[:, ws, k, :], ogT[:, k, a:b]) for k in range(8)],
                                     reads=[(WR.b, ws), ogT])
                            S.op("dve", lambda e, sl=sl, a=a, b=b: e.tensor_tensor(out=sg[:, 2, a:b], in0=PS[:, sl, 0:b - a],
                                                                                  in1=sg[:, 0, a:b], op=ALU.mult),
                                 reads=[(PS, sl), (sg, 0)], writes=[(sg, 2)])
                        ws = load_w(w_proj_b[:, n * 128:(n + 1) * 128], 8)
                        for (a, b) in T2:
                            sl = ps_next()
                            mm_group(sl, PS[:, sl, 0:b - a], [(WR.b[:, ws, k, :], ypsT[:, k, a:b]) for k in range(8)],
                                     reads=[(WR.b, ws), ypsT])
                            S.op("dve", lambda e, sl=sl, a=a, b=b: e.tensor_tensor(out=sg[:, 1, a:b], in0=PS[:, sl, 0:b - a],
                                                                                  in1=sg[:, 1, a:b], op=ALU.mult),
                                 reads=[(PS, sl), (sg, 1)], writes=[(sg, 1)])
                            S.op("dve", lambda e, a=a, b=b, n=n: e.tensor_tensor(out=MH[:, n, a:b], in0=sg[:, 1, a:b],
                                                                                in1=sg[:, 2, a:b], op=ALU.add),
                                 reads=[(sg, 1), (sg, 2)], writes=[(MH, n)])
                    S.barrier()
                ckpt(81)
                xT = S.sb("xT", [128, 16, 1088], F32, nslots=16, es=s4)
                rsb = S.sb("rsb", [128, 1088], F32, es=s4)
                tmpf = S.sb("tmpf", [128, 2, 1088], F32, nslots=2, es=s4)

                def mod_add(n, sl, a, b, chunk0):
                    if b <= 1024:
                        S.op("dve", lambda e: e.scalar_tensor_tensor(out=xT[:, n, a:b], in0=PS[:, sl, 0:b - a],
                                                                     scalar=modT[:, chunk0 + n, 0:1], in1=xT[:, n, a:b],
                                                                     op0=ALU.mult, op1=ALU.add),
                             reads=[(PS, sl), modT, (xT, n)], writes=[(xT, n)])
                    else:
                        v4 = lambda ap_: ap_.rearrange("p (b t) -> p b t", t=4)
                        S.op("dve", lambda e: e.tensor_tensor(out=v4(tmpf[:, 0, 0:64]), in0=v4(PS[:, sl, 0:64]),
                                                              in1=modT[:, chunk0 + n, 1:17].unsqueeze(2).broadcast_to([128, 16, 4]),
                                                              op=ALU.mult), reads=[(PS, sl), modT], writes=[(tmpf, 0)])
                        S.op("dve", lambda e: e.tensor_tensor(out=xT[:, n, a:b], in0=xT[:, n, a:b], in1=tmpf[:, 0, 0:64], op=ALU.add),
                             reads=[(tmpf, 0), (xT, n)], writes=[(xT, n)])

                with ExitStack() as sb_:
                    xl = S.sb("xl", [128, D], F32, es=sb_)
                    for t in range(9):
                        ntok = 128 if t < 8 else 64
                        src = xown[t * 128:(t + 1) * 128, :] if t < 8 else xsm[:, :]
                        S.dma("sp", xl[0:ntok, :], src, writes=[xl])
                        for k in range(16):
                            sl = ps_next()
                            transpose_to(PS, sl, PS[:, sl, 0:ntok], xl[0:ntok, k * 128:(k + 1) * 128], C[0:ntok, IDF, 0:ntok],
                                         reads=[xl, C])
                            S.op("act", lambda e, sl=sl, k=k, t=t, ntok=ntok: e.copy(out=xT[:, k, t * 128:t * 128 + ntok],
                                                                                    in_=PS[:, sl, 0:ntok]),
                                 reads=[(PS, sl)], writes=[(xT, k)])
                    for n in range(16):
                        ws = load_w(w_out[:, n * 128:(n + 1) * 128], 16)
                        for (a, b) in T2:
                            sl = ps_next()
                            mm_group(sl, PS[:, sl, 0:b - a], [(WR.b[:, ws, k, :], MH[:, k, a:b]) for k in range(16)],
                                     reads=[(WR.b, ws), MH])
                            mod_add(n, sl, a, b, 32)
                    S.barrier()
                ckpt(82)

                def rstd_bc():
                    sls = [ps_next() for _ in T2]
                    for n in range(16):
                        S.op("act", lambda e, n=n: e.activation(out=tmpf[:, n % 2, :], in_=xT[:, n, :], func=AF.Square),
                             reads=[(xT, n)], writes=[(tmpf, n % 2)])

                        def fn(e, n=n):
                            ins = None
                            for sl_, (a, b) in zip(sls, T2):
                                ins = e.matmul(PS[:, sl_, 0:b - a], lhsT=C[:, ONESF, :], rhs=tmpf[:, n % 2, a:b], start=(n == 0),
                                               stop=(n == 15))
                            return ins
                        S.op("pe", fn, reads=[(tmpf, n % 2), C], writes=[(PS, s_) for s_ in sls])
                    for sl_, (a, b) in zip(sls, T2):
                        S.op("act", lambda e, sl_=sl_, a=a, b=b: e.activation(out=rsb[:, a:b], in_=PS[:, sl_, 0:b - a], func=AF.Ln,
                                                                             scale=1.0 / D, bias=EPS), reads=[(PS, sl_)], writes=[rsb])
                    S.op("act", lambda e: e.activation(out=rsb.ap, in_=rsb.ap, func=AF.Exp, scale=-0.5), reads=[rsb], writes=[rsb])

                rstd_bc()
                for n in range(16):
                    S.op("dve", lambda e, n=n: e.tensor_tensor(out=tmpf[:, n % 2, :], in0=xT[:, n, :], in1=rsb.ap, op=ALU.mult),
                         reads=[(xT, n), rsb], writes=[(tmpf, n % 2)])
                    S.op("act", lambda e, n=n: e.activation(out=MH[:, n, 0:1024], in_=tmpf[:, n % 2, 0:1024], func=AF.Identity,
                                                            scale=GG[:, 1, n, 0:1], bias=modT[:, 48 + n, 0:1]),
                         reads=[(tmpf, n % 2), (GG, 1), modT], writes=[(MH, n)])
                    v4 = lambda ap_: ap_.rearrange("p (b t) -> p b t", t=4)
                    S.op("dve", lambda e, n=n: e.tensor_tensor(out=v4(tmpf[:, n % 2, 1024:1088]), in0=v4(tmpf[:, n % 2, 1024:1088]),
                                                               in1=GG[:, 1, n, 1:17].unsqueeze(2).broadcast_to([128, 16, 4]),
                                                               op=ALU.mult), reads=[(tmpf, n % 2), (GG, 1)], writes=[(tmpf, n % 2)])
                    S.op("dve", lambda e, n=n: e.tensor_tensor(out=v4(MH[:, n, 1024:1088]), in0=v4(tmpf[:, n % 2, 1024:1088]),
                                                               in1=modT[:, 48 + n, 1:17].unsqueeze(2).broadcast_to([128, 16, 4]),
                                                               op=ALU.add), reads=[(tmpf, n % 2), modT], writes=[(MH, n)])
                ckpt(83)
                actq = S.sb("actq", [128, 4, 1088], BF16, nslots=4, es=s4)
                sgf = S.sb("sgf", [128, 1088], F32, es=s4)
                for qi in range(11):
                    for j in range(4):
                        f_ = qi * 4 + j
                        wg = load_w(w_gate_up[:, f_ * 128:(f_ + 1) * 128], 16)
                        for (a, b) in T2:
                            sl = ps_next()
                            mm_group(sl, PS[:, sl, 0:b - a], [(WR.b[:, wg, k, :], MH[:, k, a:b]) for k in range(16)],
                                     reads=[(WR.b, wg), MH])
                            S.op("act", lambda e, sl=sl, a=a, b=b: e.activation(out=sgf[:, a:b], in_=PS[:, sl, 0:b - a], func=AF.Silu),
                                 reads=[(PS, sl)], writes=[sgf])
                        wu = load_w(w_gate_up[:, DFF + f_ * 128:DFF + (f_ + 1) * 128], 16)
                        for (a, b) in T2:
                            sl = ps_next()
                            mm_group(sl, PS[:, sl, 0:b - a], [(WR.b[:, wu, k, :], MH[:, k, a:b]) for k in range(16)],
                                     reads=[(WR.b, wu), MH])
                            S.op("dve", lambda e, sl=sl, a=a, b=b, j=j: e.tensor_tensor(out=actq[:, j, a:b], in0=PS[:, sl, 0:b - a],
                                                                                       in1=sgf[:, a:b], op=ALU.mult),
                                 reads=[(PS, sl), sgf], writes=[(actq, j)])
                    for n in range(16):
                        wd = load_w(w_down[qi * 512:(qi + 1) * 512, n * 128:(n + 1) * 128], 4)
                        for (a, b) in T2:
                            sl = ps_next()
                            mm_group(sl, PS[:, sl, 0:b - a], [(WR.b[:, wd, j, :], actq[:, j, a:b]) for j in range(4)],
                                     reads=[(WR.b, wd), actq])
                            mod_add(n, sl, a, b, 80)
                ckpt(84)
                rstd_bc()
                for n in range(16):
                    S.op("dve", lambda e, n=n: e.scalar_tensor_tensor(out=xT[:, n, :], in0=xT[:, n, :], scalar=fgT[:, n:n + 1],
                                                                      in1=rsb.ap, op0=ALU.mult, op1=ALU.mult),
                         reads=[(xT, n), fgT, rsb], writes=[(xT, n)])
                ytok = S.sb("ytok", [128, 1, D], F32, nslots=1, es=s4)
                for t in range(9):
                    ntok = 128 if t < 8 else 64
                    ys = 0
                    for k4 in range(4):
                        sl = ps_next()

                        def ft(e, sl=sl, k4=k4, t=t, ntok=ntok):
                            ins = None
                            for j in range(4):
                                ins = e.matmul(PS[0:ntok, sl, j * 128:(j + 1) * 128], lhsT=xT[:, k4 * 4 + j, t * 128:t * 128 + ntok],
                                               rhs=C[:, IDF, :], start=True, stop=True)
                            return ins
                        S.op("pe", ft, reads=[xT, C], writes=[(PS, sl)])
                        S.op("act", lambda e, sl=sl, k4=k4, ys=ys, ntok=ntok: e.copy(out=ytok[0:ntok, ys, k4 * 512:(k4 + 1) * 512],
                                                                                    in_=PS[0:ntok, sl, :]),
                             reads=[(PS, sl)], writes=[(ytok, ys)])
                    dst = y_own[t * 128:(t + 1) * 128, :] if t < 8 else y_sm[:, :]
                    S.dma("sp", dst, ytok[0:ntok, ys, :], reads=[(ytok, ys)], is_out=True)
                S.barrier()

        try:
            body()
        except _Stop:
            pass
        S.finish()
    return nc, dbg


def prep_inputs(inp):
    f = lambda a: np.ascontiguousarray(a, dtype=np.float32)
    shared = {
        "w_ada": f(inp["w_ada"][0]), "b_ada": f(inp["b_ada"][0].reshape(96, 128)),
        "norm1_g": f(inp["norm1_g"][0].reshape(16, 128)), "w_in": f(inp["w_in"][0]),
        "conv_w": f(inp["conv_w"][0].reshape(96, 128)), "a_log": f(inp["a_log"][0].reshape(1, 8)),
        "dt_bias": f(inp["dt_bias"][0].reshape(1, 8)), "o_norm_g": f(inp["o_norm_g"][0].reshape(1, 128)),
        "pool_w": f(inp["pool_w"][0]), "pool_scale": f(inp["pool_scale"][0].reshape(8, 128)),
        "w_proj_a": f(inp["w_proj_a"][0]), "w_proj_b": f(inp["w_proj_b"][0]), "w_out": f(inp["w_out"][0]),
        "norm2_g": f(inp["norm2_g"][0].reshape(16, 128)), "w_gate_up": f(inp["w_gate_up"][0]),
        "w_down": f(inp["w_down"][0]), "final_g": f(inp["final_g"].reshape(16, 128)),
    }
    maps = []
    for c in range(8):
        b, hh = c // 2, c % 2
        sb = slice(16 * c, 16 * c + 16)
        m = dict(shared)
        xp = inp["x_prompt"][b]
        m["xpre"] = f(xp[0:1024]) if hh == 1 else np.zeros((1024, D), np.float32)
        m["xown"] = f(xp[1024 * hh:1024 * hh + 1024])
        m["xsm"] = f(inp["x_sample"][sb].reshape(64, D))
        m["cc"] = f(np.concatenate([inp["c_prompt"][b:b + 1], inp["c_sample"][sb]], axis=0))
        m["sdelta"] = f(inp["state_delta"][0, sb])
        m["sconv"] = f(inp["state_conv"][0, sb].reshape(48, 3072))
        m["spool"] = f(inp["state_pool"][0, sb].reshape(240, 1024))
        m["flag"] = np.full((128, 1), float(hh), np.float32)
        m["pos0"] = np.full((128, 1), float(1024 * hh), np.float32)
        maps.append(m)
    return maps


_NC_CACHE = {}


def kernel(**inputs):
    if "nc" not in _NC_CACHE:
        _NC_CACHE["nc"] = build_program()[0]
    nc = _NC_CACHE["nc"]
    maps = prep_inputs(inputs)
    res = run_bass_kernel_spmd(nc, maps, core_ids=list(range(8))).results
    y_prompt = np.zeros((4, 2048, D), np.float32)
    y_sample = np.zeros((128, 4, D), np.float32)
    ndp = np.zeros((1, 4, NH, 128, 128), np.float32)
    ncp = np.zeros((1, 4, 3, 3072), np.float32)
    npp = np.zeros((1, 4, 15, 1024), np.float32)
    nds = np.zeros((1, 128, NH, 128, 128), np.float32)
    ncs = np.zeros((1, 128, 3, 3072), np.float32)
    nps = np.zeros((1, 128, 15, 1024), np.float32)
    for c in range(8):
        b, hh = c // 2, c % 2
        r = res[c]
        sb = slice(16 * c, 16 * c + 16)
        y_prompt[b, 1024 * hh:1024 * hh + 1024] = r["y_own"]
        y_sample[sb] = r["y_sm"].reshape(16, 4, D)
        nds[0, sb] = r["o_delta_s"]
        ncs[0, sb] = r["o_conv"][0:48].reshape(16, 3, 3072)
        nps[0, sb] = r["o_pool"][0:240].reshape(16, 15, 1024)
        if hh == 1:
            ndp[0, b] = r["o_delta_p"]
            ncp[0, b] = r["o_conv"][48:51]
            npp[0, b] = r["o_pool"][240:255]
    return (y_prompt, y_sample, ndp, ncp, npp, nds, ncs, nps)
```

```python
import numpy as np
from contextlib import ExitStack
import concourse.bass as bass
import concourse.mybir as mybir
from concourse.bass_utils import run_bass_kernel_spmd

F32 = mybir.dt.float32
BF16 = mybir.dt.bfloat16
I32 = mybir.dt.int32
F32R = mybir.dt.float32r
NEUMANN_F32R = False
AF = mybir.ActivationFunctionType
ALU = mybir.AluOpType
AX = mybir.AxisListType

D = 2048
NH = 8
DFF = 5632
QOFF, KOFF, VOFF, ZOFF, AOFF, BOFF, POFF, GAOFF, GBOFF = 0, 1024, 2048, 3072, 4096, 4104, 4112, 5136, 7184
INW = 9232
EPS = 1e-6
NDS = 24
SAME_ENGINE_SYNC = True
KUNITS = 2
KUNITS_A = 5
RAW_ONLY_SELF = False


class Buf:
    _n = 0

    def __init__(self, ap, nslots=1, name=""):
        self.ap = ap if type(ap).__name__ == 'AP' else ap[:]
        self.n = nslots
        self.name = name
        Buf._n += 1
        self.id = Buf._n

    def __getitem__(self, k):
        return self.ap[k]


class Sched:
    def __init__(self, nc, es):
        self.nc = nc
        self.es = es
        self.E = {"pe": nc.tensor, "act": nc.scalar, "dve": nc.vector, "pool": nc.gpsimd, "sp": nc.sync}
        self.sem = {e: es.enter_context(nc.semaphore("S_" + e)) for e in ["pe", "act", "dve", "pool"]}
        self.cnt = {e: 0 for e in self.sem}
        self.dsem = [es.enter_context(nc.semaphore(f"D{i}")) for i in range(NDS)]
        self.dval = [0] * NDS
        self.dnext = 0
        self.dq = {}
        self.seen = {e: {} for e in self.E}
        self.lastw = {}
        self.readers = {}
        self.out_toks = []
        self.nops = 0
        self.off = False

    def sb(self, name, shape, dt, nslots=1, es=None):
        self._uid = getattr(self, "_uid", 0) + 1
        name = f"{name}_{self._uid}"
        return Buf((es or self.es).enter_context(self.nc.sbuf_tensor(name, shape, dt)), nslots, name)

    def barrier(self):
        if self.off:
            return
        for e in self.E:
            for j in range(NDS):
                if self.dval[j] > 0:
                    self._wait(e, ("dma", j, self.dval[j]))
            for e2 in self.sem:
                if self.cnt[e2] > 0 and e2 != e:
                    self._wait(e, ("eng", e2, self.cnt[e2]))

    def _keys(self, acc):
        ks = []
        for a in acc:
            if isinstance(a, Buf):
                a = (a, None)
            b, s = a
            if s is None:
                ks.extend((b.id, i) for i in range(b.n))
            elif isinstance(s, (list, tuple, range)):
                ks.extend((b.id, i) for i in s)
            else:
                ks.append((b.id, s))
        return ks

    def _wait(self, e, tok):
        kind, key, val = tok
        if self.seen[e].get((kind, key), 0) >= val:
            return
        sem = self.sem[key] if kind == "eng" else self.dsem[key]
        self.E[e].wait_ge(sem, val)
        self.seen[e][(kind, key)] = val

    def _deps(self, e, reads, writes):
        rk, wk = self._keys(reads), self._keys(writes)
        deps = set()
        deps2 = set()
        for k in rk:
            if k in self.lastw:
                deps.add(self.lastw[k])
        for k in wk:
            if k in self.lastw:
                deps2.add(self.lastw[k])
            for t in self.readers.get(k, ()):
                deps2.add(t)
        for t in sorted(deps | deps2, key=lambda t: (t[0], str(t[1]), t[2])):
            if t[0] == "eng" and t[1] == e:
                if e == "pe" or not SAME_ENGINE_SYNC or (RAW_ONLY_SELF and t not in deps):
                    continue
            self._wait(e, t)
        return rk, wk

    def _commit(self, tok, rk, wk):
        for k in rk:
            self.readers.setdefault(k, []).append(tok)
        for k in wk:
            self.lastw[k] = tok
            self.readers[k] = []

    def op(self, e, fn, reads=(), writes=()):
        if self.off:
            return
        rk, wk = self._deps(e, reads, writes)
        inst = fn(self.E[e])
        self.cnt[e] += 1
        inst.then_inc(self.sem[e], 1)
        self._commit(("eng", e, self.cnt[e]), rk, wk)
        self.nops += 1

    def dma(self, q, out, in_, reads=(), writes=(), is_out=False, **kw):
        if self.off:
            return
        rk, wk = self._deps(q, reads, writes)
        half = NDS // 2
        base = 0 if q == "pool" else half
        cur = self.dq.get(q, 0)
        j = base + cur
        self.dq[q] = (cur + 1) % half
        if self.dval[j] > 0:
            self._wait(q, ("dma", j, self.dval[j]))
        inst = self.E[q].dma_start(out=out, in_=in_, **kw)
        self.dval[j] += 16
        inst.then_inc(self.dsem[j], 16)
        tok = ("dma", j, self.dval[j])
        self._commit(tok, rk, wk)
        if is_out:
            self.out_toks.append(tok)
        self.nops += 1

    def finish(self):
        for j in range(NDS):
            if self.dval[j] > 0:
                self._wait("sp", ("dma", j, self.dval[j]))
        for e in self.sem:
            if self.cnt[e] > 0:
                self._wait("sp", ("eng", e, self.cnt[e]))


class Ring:
    def __init__(self, sch, name, shape, dt, n, es=None):
        self.b = sch.sb(name, [shape[0], n] + list(shape[1:]), dt, nslots=n, es=es)
        self.n = n
        self.i = 0

    def next(self):
        s = self.i % self.n
        self.i += 1
        return s


def tok_tiles(n, step=512):
    return [(a, min(a + step, n)) for a in range(0, n, step)]


def build_program(debug=None):
    nc = bass.Bass("TRN2", target_bir_lowering=False)
    dbg = {}

    def din(name, shape, dt=F32):
        return nc.dram_tensor(name, list(shape), dt, kind="ExternalInput").ap()

    def dout(name, shape, dt=F32):
        return nc.dram_tensor(name, list(shape), dt, kind="ExternalOutput").ap()

    xpre = din("xpre", [1024, D])
    xown = din("xown", [1024, D])
    xsm = din("xsm", [64, D])
    cc = din("cc", [17, D])
    sdelta = din("sdelta", [16, NH, 128, 128])
    sconv = din("sconv", [48, 3072])
    spool = din("spool", [240, 1024])
    flag = din("flag", [128, 1])
    pos0 = din("pos0", [128, 1])
    w_ada = din("w_ada", [D, 6 * D])
    b_ada = din("b_ada", [96, 128])
    norm1_g = din("norm1_g", [16, 128])
    w_in = din("w_in", [D, INW])
    conv_w = din("conv_w", [96, 128])
    a_log = din("a_log", [1, 8])
    dt_bias = din("dt_bias", [1, 8])
    o_norm_g = din("o_norm_g", [1, 128])
    pool_w = din("pool_w", [4, 256, 256])
    pool_scale = din("pool_scale", [8, 128])
    w_proj_a = din("w_proj_a", [1024, D])
    w_proj_b = din("w_proj_b", [1024, D])
    w_out = din("w_out", [D, D])
    norm2_g = din("norm2_g", [16, 128])
    w_gate_up = din("w_gate_up", [D, 2 * DFF])
    w_down = din("w_down", [DFF, D])
    final_g = din("final_g", [16, 128])

    y_own = dout("y_own", [1024, D])
    y_sm = dout("y_sm", [64, D])
    o_delta_p = dout("o_delta_p", [NH, 128, 128])
    o_conv = dout("o_conv", [51, 3072])
    o_pool = dout("o_pool", [255, 1024])
    o_delta_s = dout("o_delta_s", [16, NH, 128, 128])

    with ExitStack() as es:
        S = Sched(nc, es)
        E = S.E

        def dbg_out(name, buf, shape, dt=F32, ap=None):
            if debug is None or name not in debug:
                return
            t = dout("dbg_" + name, shape, dt)
            S.dma("sp", t, ap if ap is not None else buf.ap, reads=[buf], is_out=True)
            dbg[name] = shape

        PS = Buf(es.enter_context(nc.psum_tensor("PS", [128, 8, 512], F32)), 8, "PS")
        ps_i = [0]
        psb_i = [0]

        def ps_next():
            s = ps_i[0] % 8
            ps_i[0] += 1
            return s

        def ps_next():
            s = psb_i[0] % 8
            psb_i[0] += 1
            return s

        C = S.sb("consts_f", [128, 8, 128], F32, nslots=8)
        CB = S.sb("consts_b", [128, 2, 128], BF16, nslots=2)
        IDF, ONESF, TRI, STRICT, BTRI, BSTRICT, BLK = 0, 1, 2, 3, 5, 6, 7

        def mk_const(slot, fn):
            S.op("pool", fn, writes=[(C, slot)])

        mk_const(ONESF, lambda e: e.memset(C[:, ONESF, :], 1.0))
        for slot, cmp_, base in ((IDF, ALU.is_equal, 0), (TRI, ALU.is_ge, 0), (STRICT, ALU.is_gt, 0)):
            S.op("pool", lambda e, slot=slot: e.memset(C[:, slot, :], 1.0), writes=[(C, slot)])
            if slot == STRICT:
                S.op("pool", lambda e, slot=slot, cmp_=cmp_: e.affine_select(
                    out=C[:, slot, :], in_=C[:, slot, :], pattern=[[-1, 128]], compare_op=cmp_, fill=0.0,
                    base=0, channel_multiplier=1), reads=[(C, slot)], writes=[(C, slot)])
            else:
                S.op("pool", lambda e, slot=slot, cmp_=cmp_: e.affine_select(
                    out=C[:, slot, :], in_=C[:, slot, :], pattern=[[1, 128]], compare_op=cmp_, fill=0.0,
                    base=0, channel_multiplier=-1), reads=[(C, slot)], writes=[(C, slot)])
        blk_i = S.sb("blk_i", [128, 2, 128], I32, nslots=2)
        S.op("pool", lambda e: e.iota(blk_i[:, 0, :], pattern=[[1, 128]], base=0, channel_multiplier=0),
             writes=[(blk_i, 0)])
        S.op("pool", lambda e: e.iota(blk_i[:, 1, :], pattern=[[0, 128]], base=0, channel_multiplier=1),
             writes=[(blk_i, 1)])
        S.op("dve", lambda e: e.tensor_single_scalar(out=blk_i[:, 0, :], in_=blk_i[:, 0, :], scalar=2,
                                                      op=ALU.arith_shift_right), reads=[(blk_i, 0)], writes=[(blk_i, 0)])
        S.op("dve", lambda e: e.tensor_single_scalar(out=blk_i[:, 1, :], in_=blk_i[:, 1, :], scalar=2,
                                                      op=ALU.arith_shift_right), reads=[(blk_i, 1)], writes=[(blk_i, 1)])
        S.op("dve", lambda e: e.tensor_tensor(out=blk_i[:, 0, :], in0=blk_i[:, 0, :], in1=blk_i[:, 1, :],
                                               op=ALU.is_equal), reads=[blk_i], writes=[(blk_i, 0)])
        S.op("dve", lambda e: e.tensor_copy(out=C[:, BLK, :], in_=blk_i[:, 0, :]), reads=[(blk_i, 0)],
             writes=[(C, BLK)])
        S.op("pool", lambda e: e.tensor_tensor(out=C[:, BTRI, :], in0=C[:, TRI, :], in1=C[:, BLK, :], op=ALU.mult),
             reads=[(C, TRI), (C, BLK)], writes=[(C, BTRI)])
        S.op("pool", lambda e: e.tensor_tensor(out=C[:, BSTRICT, :], in0=C[:, STRICT, :], in1=C[:, BLK, :],
                                               op=ALU.mult), reads=[(C, STRICT), (C, BLK)], writes=[(C, BSTRICT)])
        S.op("pool", lambda e: e.tensor_copy(out=CB[:, 0, :], in_=C[:, IDF, :]), reads=[(C, IDF)], writes=[(CB, 0)])
        S.op("pool", lambda e: e.memset(CB[:, 1, :], 1.0), writes=[(CB, 1)])
        dbg_out("consts", C, [128, 8, 128])

        class _Stop(Exception):
            pass

        def ckpt(k):
            if debug is not None and f'stop{k}' in debug:
                S.off = True

        def body():
            def mm_group(slot, out_ap, terms, reads, psbuf=PS):
                def fn(e):
                    n_ = len(terms)
                    ins = None
                    for i, (l, r) in enumerate(terms):
                        ins = e.matmul(out_ap, lhsT=l, rhs=r, start=(i == 0), stop=(i == n_ - 1))
                    return ins
                S.op("pe", fn, reads=reads, writes=[(psbuf, slot)])

            def transpose_to(psbuf, slot, out_ap, in_ap, ident_ap, reads):
                S.op("pe", lambda e: e.matmul(out_ap, lhsT=in_ap, rhs=ident_ap, start=True, stop=True), reads=reads,
                     writes=[(psbuf, slot)])

            def load_vecT(name, dram, n):
                tmp = S.sb(name + "_tm", [n, 128], F32)
                dst = S.sb(name, [128, n], F32)
                S.dma("sp", tmp.ap, dram, writes=[tmp])
                sl = ps_next()
                transpose_to(PS, sl, PS[:, sl, 0:n], tmp.ap, C[0:n, IDF, 0:n], reads=[tmp, (C, IDF)])
                S.op("act", lambda e: e.copy(out=dst.ap, in_=PS[:, sl, 0:n]), reads=[(PS, sl)], writes=[dst])
                return dst

            def wsrc(dram_ap):
                return dram_ap.rearrange("(k p) c -> p k c", p=128)

            WR = Ring(S, "wr", [128, 16, 128], BF16, 4)

            def load_w(dram_ap, kc, ncols=128, ring=None):
                ring = ring or WR
                s_ = ring.next()
                src = wsrc(dram_ap)
                step = 4 if ncols > 128 else 16
                for k0 in range(0, kc, step):
                    k1 = min(kc, k0 + step)
                    S.dma("pool", ring.b[:, s_, k0:k1, 0:ncols], src[:, k0:k1, :], writes=[(ring.b, s_)])
                return s_

            b_adaT = load_vecT("b_adaT", b_ada, 96)
            g1T = load_vecT("g1T", norm1_g, 16)
            g2T = load_vecT("g2T", norm2_g, 16)
            fgT = load_vecT("fgT", final_g, 16)
            pscT = load_vecT("pscT", pool_scale, 8)
            cwT = load_vecT("cwT", conv_w, 96)
            flag_t = S.sb("flag_t", [128, 1], F32)
            S.dma("sp", flag_t.ap, flag, writes=[flag_t])
            pos0_t = S.sb("pos0_t", [128, 1], F32)
            S.dma("sp", pos0_t.ap, pos0, writes=[pos0_t])
            nea_t = S.sb("nea_t", [128, 8], F32)
            S.dma("sp", nea_t.ap, a_log[0:1, :].broadcast_to([128, 8]), writes=[nea_t])
            dtb_t = S.sb("dtb_t", [128, 8], F32)
            S.dma("sp", dtb_t.ap, dt_bias[0:1, :].broadcast_to([128, 8]), writes=[dtb_t])
            ong_t = S.sb("ong_t", [128, 128], F32)
            S.dma("sp", ong_t.ap, o_norm_g[0:1, :].broadcast_to([128, 128]), writes=[ong_t])
            S.op("act", lambda e: e.activation(out=nea_t.ap, in_=nea_t.ap, func=AF.Exp), reads=[nea_t], writes=[nea_t])
            S.op("dve", lambda e: e.tensor_scalar_mul(out=nea_t.ap, in0=nea_t.ap, scalar1=-1.0), reads=[nea_t], writes=[nea_t])

            ckpt(1)
            CM = S.sb("CM", [128, 16, 64], BF16)
            RM = S.sb("RM", [128, 16], BF16)
            scM = ExitStack()
            scM.__enter__()
            CMf = S.sb("CMf", [128, 16, 64], F32, es=scM)
            RMf = S.sb("RMf", [128, 16], F32, es=scM)
            S.op("pool", lambda e: e.memset(CMf.ap, 1.0), writes=[CMf])
            S.op("pool", lambda e: e.affine_select(out=CMf.ap, in_=CMf.ap, pattern=[[-4, 16], [1, 64]], compare_op=ALU.is_ge,
                                                   fill=0.0, base=0, channel_multiplier=0), reads=[CMf], writes=[CMf])
            S.op("pool", lambda e: e.affine_select(out=CMf.ap, in_=CMf.ap, pattern=[[4, 16], [-1, 64]], compare_op=ALU.is_ge,
                                                   fill=0.0, base=3, channel_multiplier=0), reads=[CMf], writes=[CMf])
            S.op("pool", lambda e: e.memset(RMf.ap, 1.0), writes=[RMf])
            S.op("pool", lambda e: e.affine_select(out=RMf.ap, in_=RMf.ap, pattern=[[-4, 16]], compare_op=ALU.is_ge,
                                                   fill=0.0, base=0, channel_multiplier=1), reads=[RMf], writes=[RMf])
            S.op("pool", lambda e: e.affine_select(out=RMf.ap, in_=RMf.ap, pattern=[[4, 16]], compare_op=ALU.is_ge,
                                                   fill=0.0, base=3, channel_multiplier=-1), reads=[RMf], writes=[RMf])
            S.op("pool", lambda e: e.tensor_copy(out=CM.ap, in_=CMf.ap), reads=[CMf], writes=[CM])
            S.op("pool", lambda e: e.tensor_copy(out=RM.ap, in_=RMf.ap), reads=[RMf], writes=[RM])
            S.barrier()
            scM.close()

            ckpt(2)
            modT = S.sb("modT", [128, 96, 17], F32)
            GG = S.sb("GG", [128, 2, 16, 17], F32, nslots=2)
            FV = S.sb("FV", [128, 2, 16], F32, nslots=2)
            with ExitStack() as sc:
                cc_t = S.sb("cc_t", [17, D], F32, es=sc)
                cc_b = S.sb("cc_b", [17, D], BF16, es=sc)
                cT = S.sb("cT", [128, 16, 17], BF16, es=sc)
                mtok = S.sb("mtok", [17, 2, 512], F32, nslots=2, es=sc)
                WA = Ring(S, "wa", [128, 16, 512], BF16, 3, es=sc)
                WS32 = Ring(S, "ws32", [128, 16, 512], F32, 2, es=sc)
                S.dma("sp", cc_t.ap, cc, writes=[cc_t])
                S.op("act", lambda e: e.activation(out=cc_b.ap, in_=cc_t.ap, func=AF.Silu), reads=[cc_t], writes=[cc_b])
                ckpt(20)
                for k in range(16):
                    if k == 2:
                        ckpt(202)
                    if k == 9:
                        ckpt(209)
                    sl = ps_next()
                    transpose_to(PS, sl, PS[:, sl, 0:17], cc_b[:, k * 128:(k + 1) * 128], CB[0:17, 0, 0:17],
                                 reads=[cc_b, (CB, 0)])
                    S.op("dve", lambda e, sl=sl, k=k: e.tensor_copy(out=cT[:, k, :], in_=PS[:, sl, 0:17]),
                         reads=[(PS, sl)], writes=[cT])
                ckpt(21)
                for blk in range(24):
                    if blk == 1:
                        ckpt(25)
                    if blk == 3:
                        ckpt(26)
                    if blk == 8:
                        ckpt(27)
                    if blk % 2 == 0:
                        ws = load_w(w_ada[:, blk * 512:(blk + 1) * 512], 16, 512, ring=WA)
                    else:
                        ws = WA.next()
                        s32 = WS32.next()
                        src = wsrc(w_ada[:, blk * 512:(blk + 1) * 512])
                        for k0 in range(0, 16, 4):
                            S.dma("sp", WS32.b[:, s32, k0:k0 + 4, :], src[:, k0:k0 + 4, :], writes=[(WS32.b, s32)])
                        S.op("dve", lambda e, ws=ws, s32=s32: e.tensor_copy(out=WA.b[:, ws, 0:8, :], in_=WS32.b[:, s32, 0:8, :]),
                             reads=[(WS32.b, s32)], writes=[(WA.b, ws)])
                        S.op("act", lambda e, ws=ws, s32=s32: e.copy(out=WA.b[:, ws, 8:16, :], in_=WS32.b[:, s32, 8:16, :]),
                             reads=[(WS32.b, s32)], writes=[(WA.b, ws)])
                    sl = ps_next()
                    mm_group(sl, PS[0:17, sl, :], [(cT[:, k, :], WA.b[:, ws, k, :]) for k in range(16)],
                             reads=[cT, (WA.b, ws)])
                    ms = blk % 2
                    ckpt(22)
                    S.op("act", lambda e, sl=sl, ms=ms: e.copy(out=mtok[:, ms, :], in_=PS[0:17, sl, :]),
                         reads=[(PS, sl)], writes=[(mtok, ms)])
                    ckpt(23)
                    sl2 = ps_next()

                    def tf(e, sl2=sl2, ms=ms):
                        ins = None
                        for j in range(4):
                            ins = e.matmul(PS[:, sl2, j * 32:j * 32 + 17], lhsT=mtok[:, ms, j * 128:(j + 1) * 128],
                                           rhs=C[0:17, IDF, 0:17], start=True, stop=True)
                        return ins
                    S.op("pe", tf, reads=[(mtok, ms), (C, IDF)], writes=[(PS, sl2)])
                    ckpt(24)
                    S.op("dve", lambda e, sl2=sl2, blk=blk: e.tensor_tensor(
                        out=modT[:, blk * 4:(blk + 1) * 4, :],
                        in0=PS[:, sl2, 0:128].rearrange("p (j c) -> p j c", c=32)[:, :, 0:17],
                        in1=b_adaT[:, blk * 4:(blk + 1) * 4].unsqueeze(2).broadcast_to([128, 4, 17]), op=ALU.add),
                        reads=[(PS, sl2), b_adaT], writes=[modT])
                ckpt(28)
                for i, (gT, pc) in enumerate(((g1T, 1), (g2T, 4))):
                    S.op("dve", lambda e, i=i, gT=gT, pc=pc: e.scalar_tensor_tensor(
                        out=GG[:, i], in0=modT[:, pc * 16:(pc + 1) * 16, :], scalar=1.0,
                        in1=gT.ap.unsqueeze(2).broadcast_to([128, 16, 17]), op0=ALU.add, op1=ALU.mult),
                        reads=[modT, gT], writes=[(GG, i)])
                S.op("dve", lambda e: e.tensor_scalar_mul(out=FV[:, 0, :], in0=GG[:, 0, :, 0], scalar1=flag_t[:, 0:1]),
                     reads=[(GG, 0), flag_t], writes=[(FV, 0)])
                S.op("dve", lambda e: e.tensor_scalar_mul(out=FV[:, 1, :], in0=modT[:, 0:16, 0], scalar1=flag_t[:, 0:1]),
                     reads=[modT, flag_t], writes=[(FV, 1)])
                ckpt(29)
                dbg_out("modT", modT, [128, 96, 17])
                S.barrier()

            ckpt(3)
            def make_u(xsrc, ntok, uT, col0, kind, xt, xb, st, xr):
                s_ = xr.next()
                S.dma("sp", xt[0:ntok, s_, :], xsrc, writes=[(xt, s_)])
                S.op("dve", lambda e: e.memset(st[:, s_, :], 0.0), writes=[(st, s_)])
                S.op("act", lambda e: e.activation(out=xb[0:ntok, s_, :], in_=xt[0:ntok, s_, :], func=AF.Square,
                                                   accum_out=st[0:ntok, s_, 0:1]), reads=[(xt, s_), (st, s_)],
                     writes=[(xb, s_), (st, s_)])
                S.op("dve", lambda e: e.tensor_scalar(out=st[0:ntok, s_, 1:2], in0=st[0:ntok, s_, 0:1], scalar1=1.0 / D,
                                                      scalar2=EPS, op0=ALU.mult, op1=ALU.add), reads=[(st, s_)],
                     writes=[(st, s_)])
                S.op("act", lambda e: e.activation(out=st[0:ntok, s_, 1:2], in_=st[0:ntok, s_, 1:2], func=AF.Ln),
                     reads=[(st, s_)], writes=[(st, s_)])
                S.op("act", lambda e: e.activation(out=st[0:ntok, s_, 1:2], in_=st[0:ntok, s_, 1:2], func=AF.Exp, scale=-0.5),
                     reads=[(st, s_)], writes=[(st, s_)])
                S.op("act", lambda e: e.activation(out=xb[0:ntok, s_, :], in_=xt[0:ntok, s_, :], func=AF.Copy,
                                                   scale=st[0:ntok, s_, 1:2]), reads=[(xt, s_), (st, s_)],
                     writes=[(xb, s_)])
                for k in range(16):
                    sl = ps_next()
                    transpose_to(PS, sl, PS[:, sl, 0:ntok], xb[0:ntok, s_, k * 128:(k + 1) * 128],
                                 CB[0:ntok, 0, 0:ntok], reads=[(xb, s_), (CB, 0)])
                    dst = uT[:, k, col0:col0 + ntok]
                    if kind == "own":
                        S.op("act", lambda e, sl=sl, k=k, dst=dst: e.activation(
                            out=dst, in_=PS[:, sl, 0:ntok], func=AF.Identity, scale=GG[:, 0, k, 0:1],
                            bias=modT[:, k, 0:1]), reads=[(PS, sl), (GG, 0), modT], writes=[(uT, k)])
                    elif kind == "pre":
                        S.op("act", lambda e, sl=sl, k=k, dst=dst: e.activation(
                            out=dst, in_=PS[:, sl, 0:ntok], func=AF.Identity, scale=FV[:, 0, k:k + 1],
                            bias=FV[:, 1, k:k + 1]), reads=[(PS, sl), FV], writes=[(uT, k)])
                    else:
                        dv = dst.rearrange("p (b t) -> p b t", t=4)
                        S.op("dve", lambda e, sl=sl, k=k, dv=dv: e.tensor_tensor(
                            out=dv, in0=PS[:, sl, 0:64].rearrange("p (b t) -> p b t", t=4),
                            in1=GG[:, 0, k, 1:17].unsqueeze(2).broadcast_to([128, 16, 4]), op=ALU.mult),
                            reads=[(PS, sl), (GG, 0)], writes=[(uT, k)])
                        S.op("dve", lambda e, k=k, dv=dv: e.tensor_tensor(
                            out=dv, in0=dv, in1=modT[:, k, 1:17].unsqueeze(2).broadcast_to([128, 16, 4]), op=ALU.add),
                            reads=[(uT, k), modT], writes=[(uT, k)])

            ogT = S.sb("ogT", [128, 8, 1088], BF16, nslots=8)
            ypsT = S.sb("ypsT", [128, 8, 1088], BF16, nslots=8)
            scD = ExitStack()
            scD.__enter__()
            Sst = S.sb("Sst", [128, NH, 128], F32, nslots=NH, es=scD)
            Sb = S.sb("Sb", [128, NH, 128], BF16, nslots=NH, es=scD)
            S.op("dve", lambda e: e.memset(Sst.ap, 0.0), writes=[Sst])
            S.op("dve", lambda e: e.memset(Sb.ap, 0.0), writes=[Sb])
            TF = TB = TN = SST = None
            WAB = S.sb("wab", [128, 16, 16], BF16, es=scD)
            S.dma("pool", WAB.ap, wsrc(w_in[:, AOFF:AOFF + 16]), writes=[WAB])

            def decay_prep(uT, tiles, sc):
                nt = len(tiles)
                DP = S.sb("dp", [128, 8, nt, 8], F32, nslots=8, es=sc)
                ab = S.sb("ab", [128, nt, 16], F32, es=sc)
                S.op("dve", lambda e: e.memset(DP.ap, 0.0), writes=[DP])
                S.op("dve", lambda e: e.memset(ab.ap, 0.0), writes=[ab])
                sl = ps_next()

                def fn(e):
                    ins = None
                    for ti, (c0, n, smp) in enumerate(tiles):
                        for k in range(16):
                            ins = e.matmul(PS[0:n, sl, ti * 16:(ti + 1) * 16], lhsT=uT[:, k, c0:c0 + n], rhs=WAB[:, k, :],
                                           start=(k == 0), stop=(k == 15))
                    return ins
                S.op("pe", fn, reads=[uT, WAB], writes=[(PS, sl)])
                for ti, (c0, n, smp) in enumerate(tiles):
                    S.op("act", lambda e, ti=ti, n=n: e.copy(out=ab[0:n, ti, :], in_=PS[0:n, sl, ti * 16:(ti + 1) * 16]),
                         reads=[(PS, sl)], writes=[ab])
                bc = lambda t: t.ap.unsqueeze(1).broadcast_to([128, nt, 8])
                S.op("dve", lambda e: e.tensor_tensor(out=DP[:, 0], in0=ab[:, :, 0:8], in1=bc(dtb_t), op=ALU.add),
                     reads=[ab, dtb_t], writes=[(DP, 0)])
                S.op("act", lambda e: e.activation(out=DP[:, 0], in_=DP[:, 0], func=AF.Exp), reads=[(DP, 0)], writes=[(DP, 0)])
                S.op("act", lambda e: e.activation(out=DP[:, 0], in_=DP[:, 0], func=AF.Ln, bias=1.0), reads=[(DP, 0)],
                     writes=[(DP, 0)])
                S.op("dve", lambda e: e.tensor_tensor(out=DP[:, 0], in0=DP[:, 0], in1=bc(nea_t), op=ALU.mult),
                     reads=[(DP, 0), nea_t], writes=[(DP, 0)])
                S.op("act", lambda e: e.activation(out=DP[:, 1], in_=ab[:, :, 8:16], func=AF.Sigmoid), reads=[ab],
                     writes=[(DP, 1)])
                S.op("dve", lambda e: e.tensor_scalar_mul(out=DP[:, 2], in0=DP[:, 1], scalar1=-1.0), reads=[(DP, 1)],
                     writes=[(DP, 2)])
                sl2 = ps_next()

                def fn2(e):
                    ins = None
                    for ti, (c0, n, smp) in enumerate(tiles):
                        e.matmul(PS[0:n, sl2, ti * 8:(ti + 1) * 8], lhsT=C[0:n, BTRI if smp else TRI, 0:n],
                                 rhs=DP[0:n, 0, ti, :], start=True, stop=True)
                        ins = e.matmul(PS[0:n, sl2, 256 + ti * 8:256 + (ti + 1) * 8], lhsT=C[0:n, BLK if smp else ONESF, 0:n],
                                       rhs=DP[0:n, 0, ti, :], start=True, stop=True)
                    return ins
                S.op("pe", fn2, reads=[(DP, 0), C], writes=[(PS, sl2)])
                for ti, (c0, n, smp) in enumerate(tiles):
                    S.op("act", lambda e, ti=ti, n=n: e.copy(out=DP[0:n, 3, ti, :], in_=PS[0:n, sl2, ti * 8:(ti + 1) * 8]),
                         reads=[(PS, sl2)], writes=[(DP, 3)])
                    S.op("act", lambda e, ti=ti, n=n: e.copy(out=DP[0:n, 4, ti, :],
                                                             in_=PS[0:n, sl2, 256 + ti * 8:256 + (ti + 1) * 8]),
                         reads=[(PS, sl2)], writes=[(DP, 4)])
                S.op("dve", lambda e: e.tensor_tensor(out=DP[:, 5], in0=DP[:, 4], in1=DP[:, 3], op=ALU.subtract),
                     reads=[(DP, 3), (DP, 4)], writes=[(DP, 5)])
                S.op("act", lambda e: e.activation(out=DP[:, 5], in_=DP[:, 5], func=AF.Exp), reads=[(DP, 5)], writes=[(DP, 5)])
                S.op("act", lambda e: e.activation(out=DP[:, 6], in_=DP[:, 4], func=AF.Exp), reads=[(DP, 4)], writes=[(DP, 6)])
                S.op("act", lambda e: e.activation(out=DP[:, 7], in_=DP[:, 3], func=AF.Exp), reads=[(DP, 3)], writes=[(DP, 7)])
                S.op("dve", lambda e: e.tensor_tensor(out=DP[:, 7], in0=DP[:, 7], in1=DP[:, 1], op=ALU.mult),
                     reads=[(DP, 7), (DP, 1)], writes=[(DP, 7)])
                return DP

            cur_mode = ['A']
            turn = [0]

            def unit(ui, DP, h, ti, n, smp, kT_c, qT_c, ktok_c, vtok_c, do_o, kreads, opost, sm):
                TRIc, STRc = (BTRI, BSTRICT) if smp else (TRI, STRICT)
                col = lambda j: DP[0:n, j, ti, h:h + 1]
                tf_ = lambda s_: TF.b[0:n, s_, 0:n]
                tb_ = lambda s_: TB.b[0:n, s_, 0:n]
                tnf_ = lambda s_: TN.b[0:n, s_, 0:n]
                tn_ = (lambda s_: TN.b[0:n, s_, 0:n].bitcast(F32R)) if NEUMANN_F32R else tnf_
                tr_ = tn_
                f1 = ui * 6
                S.op("dve", lambda e: e.tensor_scalar_mul(out=tf_(f1), in0=C[0:n, TRIc, 0:n], scalar1=col(0)),
                     reads=[C, (DP, 0)], writes=[(TF.b, f1)])
                sld = ps_next()
                mm_group(sld, PS[:, sld, 0:n], [(C[0:n, ONESF, :], tf_(f1))], reads=[C, (TF.b, f1)])
                f2, f3 = ui * 6 + 1, ui * 6 + 2
                S.op("dve", lambda e: e.tensor_scalar(out=tf_(f2), in0=PS[0:n, sld, 0:n], scalar1=col(3), scalar2=0.0,
                                                      op0=ALU.subtract, op1=ALU.max), reads=[(PS, sld), (DP, 3)],
                     writes=[(TF.b, f2)])
                S.op("dve", lambda e: e.tensor_scalar(out=tf_(f3), in0=PS[0:n, sld, 0:n], scalar1=col(3), scalar2=0.0,
                                                      op0=ALU.subtract, op1=ALU.min), reads=[(PS, sld), (DP, 3)],
                     writes=[(TF.b, f3)])
                if debug is not None and 'trace' in debug and cur_mode[0] == 'B' and h == 0:
                    print("UNIT", ti, "sld", sld, "f1,f2,f3", f1, f2, f3, "cnt", dict(S.cnt), flush=True)
                fed = None
                if do_o:
                    fed = ui * 6 + 3
                    S.op("dve", lambda e: e.tensor_copy(out=TF.b[:, fed, 0:n], in_=PS[:, sld, 0:n]),
                         reads=[(PS, sld)], writes=[(TF.b, fed)])
                    S.op("act", lambda e: e.activation(out=TF.b[:, fed, 0:n], in_=TF.b[:, fed, 0:n], func=AF.Exp),
                         reads=[(TF.b, fed)], writes=[(TF.b, fed)])
                yield
                if smp:
                    S.op("dve", lambda e: e.tensor_copy(
                        out=sm["cds"].ap, in_=TF.b[:, fed, 0:64].rearrange("p (b t) -> p b t", t=4)[:, :, 3]),
                        reads=[(TF.b, fed)], writes=[sm["cds"]])
                S.op("act", lambda e: e.activation(out=tf_(f2), in_=tf_(f2), func=AF.Exp, scale=-1.0), reads=[(TF.b, f2)],
                     writes=[(TF.b, f2)])
                S.op("act", lambda e: e.activation(out=tf_(f3), in_=tf_(f3), func=AF.Exp), reads=[(TF.b, f3)],
                     writes=[(TF.b, f3)])
                S.op("dve", lambda e: e.tensor_tensor(out=tf_(f2), in0=tf_(f2), in1=C[0:n, STRc, 0:n], op=ALU.mult),
                     reads=[(TF.b, f2), C], writes=[(TF.b, f2)])
                S.op("dve", lambda e: e.tensor_tensor(out=tf_(f3), in0=tf_(f3), in1=C[0:n, TRIc, 0:n], op=ALU.mult),
                     reads=[(TF.b, f3), C], writes=[(TF.b, f3)])
                yield
                slg = ps_next()
                mm_group(slg, PS[0:n, slg, 0:n], [(kT_c, kT_c)], reads=kreads)
                b1 = ui * 10
                S.op("dve", lambda e: e.scalar_tensor_tensor(out=tn_(b1), in0=PS[0:n, slg, 0:n], scalar=col(2), in1=tf_(f2),
                                                             op0=ALU.mult, op1=ALU.mult),
                     reads=[(PS, slg), (DP, 2), (TF.b, f2)], writes=[(TN.b, b1)])
                pb = ps_next()
                transpose_to(PS, pb, PS[0:n, pb, 0:n], tnf_(b1), C[0:n, IDF, 0:n], reads=[(TN.b, b1), C])
                b2 = ui * 10 + 1
                S.op("act", lambda e: e.copy(out=tn_(b2), in_=PS[0:n, pb, 0:n]), reads=[(PS, pb)], writes=[(TN.b, b2)])
                bx, by = ui * 10 + 8, ui * 10 + 9
                S.op("dve", lambda e: e.tensor_tensor(out=tn_(bx), in0=tn_(b2), in1=C[0:n, IDF, 0:n], op=ALU.add),
                     reads=[(TN.b, b2), C], writes=[(TN.b, bx)])
                S.op("dve", lambda e: e.tensor_tensor(out=tn_(by), in0=tn_(b1), in1=C[0:n, IDF, 0:n], op=ALU.add),
                     reads=[(TN.b, b1), C], writes=[(TN.b, by)])
                UD = debug is not None and 'udump' in debug and cur_mode[0] == 'A' and h == 0 and ti == 0
                if UD:
                    dbg_out("u_N", TN.b, [128, 128], F32, ap=TN.b[:, b1, :])
                    dbg_out("u_Lm", TF.b, [128, 128], F32, ap=TF.b[:, f2, :])
                    dbg_out("u_DP", DP, [128, 8 * DP.ap.shape[2] * 8], F32, ap=DP.ap.rearrange("p a b c -> p (a b c)"))
                yield
                curN, curP = b1, b2
                nsq = 1 if smp else 6
                for lv in range(nsq):
                    lastlv = lv == nsq - 1
                    slp = ps_next()
                    mm_group(slp, PS[0:n, slp, 0:n], [(tr_(curN), tr_(curP))], reads=[(TN.b, curN), (TN.b, curP)])
                    lset = ui * 10 + (2 if lv % 2 == 0 else 6)
                    bp2 = lset
                    S.op("act", lambda e, slp=slp, bp2=bp2: e.copy(out=tn_(bp2), in_=PS[0:n, slp, 0:n]), reads=[(PS, slp)],
                         writes=[(TN.b, bp2)])
                    bn2 = None
                    if not lastlv:
                        sln = ps_next()
                        mm_group(sln, PS[0:n, sln, 0:n], [(tr_(curP), tr_(curN))], reads=[(TN.b, curN), (TN.b, curP)])
                        bn2 = lset + 1
                        S.op("act", lambda e, sln=sln, bn2=bn2: e.copy(out=tn_(bn2), in_=PS[0:n, sln, 0:n]),
                             reads=[(PS, sln)], writes=[(TN.b, bn2)])
                    yield
                    slx = ps_next()
                    mm_group(slx, PS[0:n, slx, 0:n], [(tr_(by), tr_(bp2))], reads=[(TN.b, by), (TN.b, bp2)])
                    bx2 = lset + 2
                    S.op("dve", lambda e, slx=slx, bx2=bx2, bx=bx: e.tensor_tensor(out=tn_(bx2), in0=PS[0:n, slx, 0:n],
                                                                                  in1=tn_(bx), op=ALU.add),
                         reads=[(PS, slx), (TN.b, bx)], writes=[(TN.b, bx2)])
                    if not lastlv:
                        sly = ps_next()
                        mm_group(sly, PS[0:n, sly, 0:n], [(tr_(bx), tr_(bn2))], reads=[(TN.b, bx), (TN.b, bn2)])
                        by2 = lset + 3
                        S.op("dve", lambda e, sly=sly, by2=by2, by=by: e.tensor_tensor(out=tn_(by2), in0=PS[0:n, sly, 0:n],
                                                                                      in1=tn_(by), op=ALU.add),
                             reads=[(PS, sly), (TN.b, by)], writes=[(TN.b, by2)])
                        by, curN = by2, bn2
                    bx, curP = bx2, bp2
                    yield
                if UD:
                    dbg_out("u_X", TN.b, [128, 128], F32, ap=TN.b[:, bx, :])
                yield
                bx16 = ui * 8
                S.op("act", lambda e: e.copy(out=tb_(bx16), in_=tn_(bx)), reads=[(TN.b, bx)], writes=[(TB.b, bx16)])
                bx = bx16
                bvb, bkb, bkt = ui * 8 + 1, ui * 8 + 2, ui * 8 + 3
                S.op("dve", lambda e: e.tensor_scalar_mul(out=TB.b[0:n, bvb, :], in0=vtok_c, scalar1=col(1)),
                     reads=kreads + [(DP, 1)], writes=[(TB.b, bvb)])
                S.op("dve", lambda e: e.tensor_scalar_mul(out=TB.b[0:n, bkb, :], in0=ktok_c, scalar1=col(7)),
                     reads=kreads + [(DP, 7)], writes=[(TB.b, bkb)])
                S.op("dve", lambda e: e.tensor_scalar_mul(out=TB.b[0:n, bkt, :], in0=ktok_c, scalar1=col(5)),
                     reads=kreads + [(DP, 5)], writes=[(TB.b, bkt)])
                slu = ps_next()
                mm_group(slu, PS[0:n, slu, 0:128], [(tb_(bx), TB.b[0:n, bvb, :])], reads=[(TB.b, bx), (TB.b, bvb)])
                fub = ui * 6 + 4
                S.op("act", lambda e: e.copy(out=TF.b[0:n, fub, :], in_=PS[0:n, slu, 0:128]), reads=[(PS, slu)],
                     writes=[(TF.b, fub)])
                slw = ps_next()
                mm_group(slw, PS[:, slw, 0:n], [(TB.b[0:n, bkb, :], tb_(bx))], reads=[(TB.b, bx), (TB.b, bkb)])
                bwd = ui * 8 + 4
                S.op("act", lambda e: e.copy(out=TB.b[:, bwd, 0:n], in_=PS[:, slw, 0:n]), reads=[(PS, slw)],
                     writes=[(TB.b, bwd)])
                bqd = bqk = None
                if do_o:
                    bqd, bqk = ui * 8 + 5, ui * 8 + 6
                    S.op("dve", lambda e: e.tensor_tensor(out=TB.b[:, bqd, 0:n], in0=qT_c, in1=TF.b[:, fed, 0:n], op=ALU.mult),
                         reads=kreads + [(TF.b, fed)], writes=[(TB.b, bqd)])
                    slq = ps_next()
                    mm_group(slq, PS[0:n, slq, 0:n], [(kT_c, qT_c)], reads=kreads)
                    S.op("dve", lambda e: e.tensor_tensor(out=tb_(bqk), in0=PS[0:n, slq, 0:n], in1=tf_(f3), op=ALU.mult),
                         reads=[(PS, slq), (TF.b, f3)], writes=[(TB.b, bqk)])
                if UD:
                    dbg_out("u_ub", TF.b, [128, 128], F32, ap=TF.b[:, fub, :])
                    dbg_out("u_wd", TB.b, [128, 128], BF16, ap=TB.b[:, bwd, :])
                    dbg_out("u_kt", TB.b, [128, 128], BF16, ap=TB.b[:, bkt, :])
                yield
                bu = ui * 8 + 7
                if not smp:
                    while turn[0] != ti:
                        yield
                    sl = ps_next()
                    mm_group(sl, PS[0:n, sl, 0:128], [(TB.b[:, bwd, 0:n], Sb[:, h, :])], reads=[(TB.b, bwd), (Sb, h)])
                    S.op("dve", lambda e: e.tensor_tensor(out=TB.b[0:n, bu, :], in0=TF.b[0:n, fub, :], in1=PS[0:n, sl, 0:128],
                                                          op=ALU.subtract), reads=[(PS, sl), (TF.b, fub)], writes=[(TB.b, bu)])
                    yield
                    if do_o:
                        slo = ps_next()
                        mm_group(slo, PS[0:n, slo, 0:128], [(TB.b[:, bqd, 0:n], Sb[:, h, :]), (tb_(bqk), TB.b[0:n, bu, :])],
                                 reads=[(TB.b, bqd), (Sb, h), (TB.b, bqk), (TB.b, bu)])
                        opost(slo)
                    yield
                    slk = ps_next()
                    mm_group(slk, PS[:, slk, 0:128], [(TB.b[0:n, bkt, :], TB.b[0:n, bu, :])], reads=[(TB.b, bkt), (TB.b, bu)])
                    S.op("dve", lambda e: e.scalar_tensor_tensor(out=Sb[:, h, :], in0=Sst[:, h, :], scalar=DP[:, 6, ti, h:h + 1],
                                                                 in1=PS[:, slk, 0:128], op0=ALU.mult, op1=ALU.add),
                         reads=[(Sst, h), (DP, 6), (PS, slk)], writes=[(Sb, h)])
                    S.op("dve", lambda e: e.scalar_tensor_tensor(out=Sst[:, h, :], in0=Sst[:, h, :], scalar=DP[:, 6, ti, h:h + 1],
                                                                 in1=PS[:, slk, 0:128], op0=ALU.mult, op1=ALU.add),
                         reads=[(Sst, h), (DP, 6), (PS, slk)], writes=[(Sst, h)])
                    turn[0] += 1
                    if UD:
                        dbg_out("u_S", Sst, [128, 128], F32, ap=Sst[:, 0, :])
                        dbg_out("u_u", TB.b, [128, 128], BF16, ap=TB.b[:, bu, :])
                else:
                    Ss, Ssb, wdm, qdm, ktm, cds = sm["Ss"], sm["Ssb"], sm["wdm"], sm["qdm"], sm["ktm"], sm["cds"]
                    S.op("dve", lambda e: e.tensor_tensor(out=wdm.ap, in0=TB.b[:, bwd, 0:64].unsqueeze(1).broadcast_to([128, 16, 64]),
                                                          in1=CM.ap, op=ALU.mult), reads=[(TB.b, bwd), CM], writes=[wdm])
                    S.op("dve", lambda e: e.tensor_tensor(out=qdm.ap, in0=TB.b[:, bqd, 0:64].unsqueeze(1).broadcast_to([128, 16, 64]),
                                                          in1=CM.ap, op=ALU.mult), reads=[(TB.b, bqd), CM], writes=[qdm])
                    S.op("dve", lambda e: e.tensor_tensor(out=ktm[0:64], in0=TB.b[0:64, bkt, :].unsqueeze(1).broadcast_to([64, 16, 128]),
                                                          in1=RM[0:64, :].unsqueeze(2).broadcast_to([64, 16, 128]), op=ALU.mult),
                         reads=[(TB.b, bkt), RM], writes=[ktm])
                    yield
                    sl = ps_next()
                    mm_group(sl, PS[0:64, sl, 0:128], [(wdm[:, b_, :], Ssb[:, b_, :]) for b_ in range(16)], reads=[wdm, Ssb])
                    S.op("dve", lambda e: e.tensor_tensor(out=TB.b[0:64, bu, :], in0=TF.b[0:64, fub, :], in1=PS[0:64, sl, 0:128],
                                                          op=ALU.subtract), reads=[(PS, sl), (TF.b, fub)], writes=[(TB.b, bu)])
                    yield
                    slo = ps_next()
                    mm_group(slo, PS[0:64, slo, 0:128],
                             [(qdm[:, b_, :], Ssb[:, b_, :]) for b_ in range(16)] + [(tb_(bqk), TB.b[0:64, bu, :])],
                             reads=[qdm, Ssb, (TB.b, bqk), (TB.b, bu)])
                    opost(slo)
                    for g4 in range(4):
                        yield
                        slk = ps_next()

                        def fk(e, slk=slk, g4=g4):
                            ins = None
                            for j in range(4):
                                ins = e.matmul(PS[:, slk, j * 128:(j + 1) * 128], lhsT=ktm[0:64, 4 * g4 + j, :],
                                               rhs=TB.b[0:64, bu, :], start=True, stop=True)
                            return ins
                        S.op("pe", fk, reads=[ktm, (TB.b, bu)], writes=[(PS, slk)])
                        sv = Ss[:, 4 * g4:4 * g4 + 4, :]
                        S.op("dve", lambda e, sv=sv, g4=g4: e.tensor_tensor(
                            out=sv, in0=sv, in1=cds[:, 4 * g4:4 * g4 + 4].unsqueeze(2).broadcast_to([128, 4, 128]), op=ALU.mult),
                            reads=[Ss, cds], writes=[Ss])
                        S.op("dve", lambda e, sv=sv, slk=slk: e.tensor_tensor(
                            out=sv, in0=sv, in1=PS[:, slk, :].rearrange("p (j v) -> p j v", v=128), op=ALU.add),
                            reads=[Ss, (PS, slk)], writes=[Ss])
                    S.dma("sp", o_delta_s[:, h].rearrange("b k v -> k b v"), Ss.ap, reads=[Ss], is_out=True)

            sconvT = S.sb("sconvT", [128, 24, 48], F32, es=scD)
            nconvT = S.sb("nconvT", [128, 24, 51], F32, nslots=24, es=scD)

            def head_pass(mode, uT, DP, sc):
                nonlocal TF, TB, TN, SST
                cur_mode[0] = mode
                KU_ = KUNITS if mode == "B" else KUNITS_A
                TF = Ring(S, "tf", [128, 128], F32, 6 * KU_, es=sc)
                TB = Ring(S, "tb", [128, 128], BF16, 8 * KU_, es=sc)
                TN = Ring(S, "tn", [128, 128], F32, 10 * KU_, es=sc)
                SST = Ring(S, "sst", [128, 4], F32, KU_, es=sc)
                UDH = debug is not None and 'udump' in debug and mode == 'A'
                B_ = mode == "B"
                L = 1152 if B_ else 1024
                o0 = 128 if B_ else 0
                W_ = L + (112 if B_ else 0)
                RW = 3 + W_
                stg = S.sb("stg" + mode, [128, 1, RW], F32, nslots=1, es=sc)
                yrow = S.sb("yrow" + mode, [128, W_], F32, es=sc)
                crow = S.sb("crow" + mode, [128, 2, W_], BF16, nslots=2, es=sc)
                cmap = {0: 0, 1: 0, 2: 1}
                nrmP = [S.sb("nrm" + mode, [128, 2, W_], BF16, nslots=2, es=sc) for _ in range(2)]
                ktokP = [S.sb("ktok" + mode, [128, 9, 128], BF16, nslots=9, es=sc) for _ in range(2)]
                vtokP = [S.sb("vtok" + mode, [128, 9, 128], BF16, nslots=9, es=sc) for _ in range(2)]
                cmpP = [None, None]
                wzs = {}
                sm = None
                if B_:
                    cmpP = [S.sb("cmp", [128, 3, 64], BF16, nslots=3, es=sc) for _ in range(2)]
                    gz = S.sb("gz", [128, KUNITS, 128], F32, nslots=KUNITS, es=sc)
                    ogtok = S.sb("ogtok", [128, KUNITS, 128], BF16, nslots=KUNITS, es=sc)
                    sm = {"Ss": S.sb("Ss", [128, 16, 128], F32, es=sc), "Ssb": S.sb("Ssb", [128, 16, 128], BF16, es=sc),
                          "wdm": S.sb("wdm", [128, 16, 64], BF16, es=sc), "qdm": S.sb("qdm", [128, 16, 64], BF16, es=sc),
                          "ktm": S.sb("ktm", [128, 16, 128], BF16, es=sc), "cds": S.sb("cds", [128, 16], F32, es=sc)}
                WZ = S.sb("WZ", [128, 16, 128], BF16, es=sc) if B_ else None
                S.op("dve", lambda e: e.memset(stg.ap, 0.0), writes=[stg])
                comps = [("q", 0, QOFF), ("k", 1, KOFF), ("v", 2, VOFF)] if B_ else [("k", 1, KOFF), ("v", 2, VOFF)]
                ptiles = tok_tiles(L)
                ext = lambda ci: stg[:, 0, 3 + L:3 + L + 112].rearrange("p (b j) -> p b j", j=7)
                def prep(h):
                    nrm, ktok, vtok, cmp_ = nrmP[h % 2], ktokP[h % 2], vtokP[h % 2], cmpP[h % 2]
                    for (nm, ci, off) in comps:
                        chn = ci * 8 + h
                        ws = load_w(w_in[:, off + h * 128:off + (h + 1) * 128], 16)
                        for (a, b) in ptiles:
                            sl = ps_next()
                            mm_group(sl, PS[:, sl, 0:b - a], [(WR.b[:, ws, k, :], uT[:, k, a:b]) for k in range(16)],
                                     reads=[(WR.b, ws), uT])
                            S.op("act", lambda e, sl=sl, a=a, b=b, ci=ci: e.copy(out=stg[:, 0, 3 + a:3 + b], in_=PS[:, sl, 0:b - a]),
                                 reads=[(PS, sl)], writes=[(stg, 0)])
                            yield
                        if B_:
                            sl = ps_next()
                            mm_group(sl, PS[:, sl, 0:64], [(WR.b[:, ws, k, :], uT[:, k, 1152:1216]) for k in range(16)],
                                     reads=[(WR.b, ws), uT])
                            S.op("act", lambda e, sl=sl, ci=ci: e.copy(out=ext(ci)[:, :, 3:7],
                                                                       in_=PS[:, sl, 0:64].rearrange("p (b t) -> p b t", t=4)),
                                 reads=[(PS, sl)], writes=[(stg, 0)])
                            S.op("dve", lambda e, ci=ci, chn=chn: e.tensor_copy(
                                out=ext(ci)[:, :, 0:3], in_=sconvT[:, chn, :].rearrange("p (b j) -> p b j", j=3)),
                                reads=[sconvT], writes=[(stg, 0)])
                            S.op("dve", lambda e, ci=ci, chn=chn: e.tensor_copy(
                                out=nconvT[:, chn, 0:48].rearrange("p (b j) -> p b j", j=3), in_=ext(ci)[:, :, 4:7]),
                                reads=[(stg, 0)], writes=[(nconvT, chn)])
                            S.op("dve", lambda e, ci=ci, chn=chn: e.tensor_copy(out=nconvT[:, chn, 48:51], in_=stg[:, 0, L:L + 3]),
                                 reads=[(stg, 0)], writes=[(nconvT, chn)])
                        S.op("dve", lambda e, ci=ci, chn=chn: e.tensor_scalar_mul(out=yrow.ap, in0=stg[:, 0, 0:W_],
                                                                                 scalar1=cwT[:, chn:chn + 1]),
                             reads=[(stg, 0), cwT], writes=[yrow])
                        for j in range(1, 4):
                            S.op("dve", lambda e, ci=ci, chn=chn, j=j: e.scalar_tensor_tensor(
                                out=yrow.ap, in0=stg[:, 0, j:j + W_], scalar=cwT[:, j * 24 + chn:j * 24 + chn + 1], in1=yrow.ap,
                                op0=ALU.mult, op1=ALU.add), reads=[(stg, 0), cwT, yrow], writes=[yrow])
                            yield
                        S.op("act", lambda e, ci=ci: e.activation(out=crow[:, cmap[ci], :], in_=yrow.ap, func=AF.Silu), reads=[yrow],
                             writes=[(crow, cmap[ci])])
                        yield
                        if nm in ("q", "k"):
                            S.op("act", lambda e, ci=ci: e.activation(out=nrm[:, ci, :], in_=crow[:, cmap[ci], :], func=AF.Square),
                                 reads=[(crow, cmap[ci])], writes=[(nrm, ci)])
                            for (a, b) in tok_tiles(W_):
                                sl = ps_next()
                                mm_group(sl, PS[:, sl, 0:b - a], [(CB[:, 1, :], nrm[:, ci, a:b])], reads=[CB, (nrm, ci)])
                                S.op("act", lambda e, sl=sl, a=a, b=b: e.activation(
                                    out=yrow[:, a:b], in_=PS[:, sl, 0:b - a], func=AF.Ln, bias=EPS),
                                    reads=[(PS, sl)], writes=[yrow])
                                S.op("act", lambda e, a=a, b=b: e.activation(
                                    out=yrow[:, a:b], in_=yrow[:, a:b], func=AF.Exp, scale=-0.5),
                                    reads=[yrow], writes=[yrow])
                                yield
                            S.op("dve", lambda e, ci=ci, nm=nm: e.scalar_tensor_tensor(
                                out=nrm[:, ci, :], in0=crow[:, cmap[ci], :], scalar=(128.0 ** -0.5 if nm == "q" else 1.0), in1=yrow.ap,
                                op0=ALU.mult, op1=ALU.mult), reads=[(crow, cmap[ci]), yrow], writes=[(nrm, ci)])
                    ckpt(mode + '61')
                    for c in range(8):
                        kc = o0 + 128 * c
                        for (src, dstb) in ((nrm[:, 1, kc:kc + 128], ktok), (crow[:, 1, kc:kc + 128], vtok)):
                            pb = ps_next()
                            transpose_to(PS, pb, PS[:, pb, 0:128], src, CB[:, 0, :], reads=[(nrm, 1), (crow, 1), CB])
                            S.op("act", lambda e, pb=pb, dstb=dstb, c=c: e.copy(out=dstb[:, c, :], in_=PS[:, pb, 0:128]),
                                 reads=[(PS, pb)], writes=[(dstb, c)])
                        yield
                    if B_:
                        for i, src in enumerate((nrm[:, 0, :], nrm[:, 1, :], crow[:, 1, :])):
                            S.op("dve", lambda e, i=i, src=src: e.tensor_copy(
                                out=cmp_[:, i, :].rearrange("p (b t) -> p b t", t=4),
                                in_=src[:, L:L + 112].rearrange("p (b j) -> p b j", j=7)[:, :, 3:7]),
                                reads=[(nrm, 0), (nrm, 1), (crow, 1)], writes=[(cmp_, i)])
                        for (i, dstb) in ((1, ktok), (2, vtok)):
                            pb = ps_next()
                            transpose_to(PS, pb, PS[0:64, pb, 0:128], cmp_[:, i, :], CB[:, 0, :], reads=[(cmp_, i), CB])
                            S.op("act", lambda e, pb=pb, dstb=dstb: e.copy(out=dstb[0:64, 8, :], in_=PS[0:64, pb, 0:128]),
                                 reads=[(PS, pb)], writes=[(dstb, 8)])

                    yield

                for _g in prep(0):
                    pass
                for h in range(NH):
                    nrm, ktok, vtok, cmp_ = nrmP[h % 2], ktokP[h % 2], vtokP[h % 2], cmpP[h % 2]
                    if B_:
                        zsrc = wsrc(w_in[:, ZOFF + h * 128:ZOFF + (h + 1) * 128])
                        for k0 in range(0, 16, 4):
                            S.dma("pool", WZ[:, k0:k0 + 4, :], zsrc[:, k0:k0 + 4, :], writes=[WZ])
                    nxt = prep(h + 1) if h + 1 < NH else None

                    def mk_opost(ui, n, ucol, ogcol, h=h):
                        def opost(slo):
                            gs = ui
                            zsl = ps_next()
                            mm_group(zsl, PS[0:n, zsl, 0:128], [(uT[:, k, ucol:ucol + n], WZ[:, k, :]) for k in range(16)],
                                     reads=[uT, WZ])
                            S.op("act", lambda e: e.activation(out=gz[0:n, gs, :], in_=PS[0:n, zsl, 0:128], func=AF.Silu),
                                 reads=[(PS, zsl)], writes=[(gz, gs)])
                            S.op("dve", lambda e: e.tensor_tensor(out=gz[0:n, gs, :], in0=gz[0:n, gs, :], in1=ong_t[0:n, :], op=ALU.mult),
                                 reads=[(gz, gs), ong_t], writes=[(gz, gs)])
                            ss = ui
                            fj = ui * 6 + 5
                            S.op("dve", lambda e: e.memset(SST.b[:, ss, :], 0.0), writes=[(SST.b, ss)])
                            S.op("act", lambda e: e.activation(out=TF.b[0:n, fj, :], in_=PS[0:n, slo, 0:128], func=AF.Square,
                                                               accum_out=SST.b[0:n, ss, 0:1]),
                                 reads=[(PS, slo), (SST.b, ss)], writes=[(TF.b, fj), (SST.b, ss)])
                            S.op("dve", lambda e: e.tensor_scalar(out=SST.b[0:n, ss, 1:2], in0=SST.b[0:n, ss, 0:1], scalar1=1.0 / 128,
                                                                  scalar2=EPS, op0=ALU.mult, op1=ALU.add), reads=[(SST.b, ss)],
                                 writes=[(SST.b, ss)])
                            S.op("act", lambda e: e.activation(out=SST.b[0:n, ss, 1:2], in_=SST.b[0:n, ss, 1:2], func=AF.Ln),
                                 reads=[(SST.b, ss)], writes=[(SST.b, ss)])
                            S.op("act", lambda e: e.activation(out=SST.b[0:n, ss, 1:2], in_=SST.b[0:n, ss, 1:2], func=AF.Exp,
                                                               scale=-0.5), reads=[(SST.b, ss)], writes=[(SST.b, ss)])
                            S.op("dve", lambda e: e.scalar_tensor_tensor(out=ogtok[0:n, gs, :], in0=PS[0:n, slo, 0:128],
                                                                         scalar=SST.b[0:n, ss, 1:2], in1=gz[0:n, gs, :],
                                                                         op0=ALU.mult, op1=ALU.mult),
                                 reads=[(PS, slo), (SST.b, ss), (gz, gs)], writes=[(ogtok, gs)])
                            pb = ps_next()
                            transpose_to(PS, pb, PS[:, pb, 0:n], ogtok[0:n, gs, :], CB[0:n, 0, 0:n], reads=[(ogtok, gs), CB])
                            S.op("act", lambda e: e.copy(out=ogT[:, h, ogcol:ogcol + n], in_=PS[:, pb, 0:n]), reads=[(PS, pb)],
                                 writes=[(ogT, h)])
                        return opost
                    kreads = [(nrm, 0), (nrm, 1), ktok, vtok]
                    ckpt(mode + '62')
                    pending = []
                    for c in range(8):
                        kc = o0 + 128 * c
                        pending.append(dict(c=c, n=128, smp=False, kT=nrm[:, 1, kc:kc + 128], qT=nrm[:, 0, kc:kc + 128] if B_ else None,
                                            ktok=ktok[:, c, :], vtok=vtok[:, c, :], ucol=kc, ogcol=128 * c, kr=kreads))
                    if B_:
                        pending.append(dict(c=8, n=64, smp=True, kT=cmp_[:, 1, :], qT=cmp_[:, 0, :], ktok=ktok[0:64, 8, :],
                                            vtok=vtok[0:64, 8, :], ucol=1152, ogcol=1024, kr=kreads + [cmp_]))
                    gens = []
                    turn[0] = 0
                    free_ids = list(range(KU_))
                    while pending or gens:
                        if pending and free_ids:
                            a_ = pending.pop(0)
                            ui = free_ids.pop(0)
                            if a_["smp"]:
                                S.dma("sp", sm["Ss"].ap, sdelta[:, h].rearrange("b k v -> k b v"), writes=[sm["Ss"]])
                                S.op("act", lambda e: e.copy(out=sm["Ssb"].ap, in_=sm["Ss"].ap), reads=[sm["Ss"]], writes=[sm["Ssb"]])
                            g_ = unit(ui, DP, h, a_["c"], a_["n"], a_["smp"], a_["kT"], a_["qT"], a_["ktok"], a_["vtok"], B_, a_["kr"],
                                      mk_opost(ui, a_["n"], a_["ucol"], a_["ogcol"]) if B_ else None, sm if a_["smp"] else None)
                            gens.append((g_, ui))
                        for (g_, ui) in list(gens):
                            try:
                                next(g_)
                            except StopIteration:
                                gens.remove((g_, ui))
                                free_ids.append(ui)
                        if nxt is not None:
                            try:
                                next(nxt)
                            except StopIteration:
                                nxt = None
                    if nxt is not None:
                        for _g in nxt:
                            pass

            ckpt(4)
            with ExitStack() as sc:
                uTA = S.sb("uTA", [128, 16, 1024], BF16, nslots=16, es=sc)
                with ExitStack() as scu:
                    xt = S.sb("xtA", [128, 2, D], F32, nslots=2, es=scu)
                    xb = S.sb("xbA", [128, 2, D], BF16, nslots=2, es=scu)
                    st = S.sb("stA", [128, 2, 2], F32, nslots=2, es=scu)
                    xr = Ring.__new__(Ring)
                    xr.n, xr.i = 2, 0
                    for t in range(8):
                        make_u(xpre[t * 128:(t + 1) * 128, :], 128, uTA, t * 128, "pre", xt, xb, st, xr)
                    S.barrier()
                dbg_out("uTA", uTA, [128, 16 * 1024], BF16, ap=uTA.ap.rearrange("p k t -> p (k t)"))
                ckpt(5)
                DPA = decay_prep(uTA, [(128 * c, 128, False) for c in range(8)], sc)
                ckpt(6)
                head_pass("A", uTA, DPA, sc)
                dbg_out("Smid", Sst, [128, 1024], ap=Sst.ap.rearrange("p h v -> p (h v)"))
                S.barrier()
            ckpt(7)
            with ExitStack() as sc:
                uTB = S.sb("uTB", [128, 16, 1216], BF16, nslots=16, es=sc)
                with ExitStack() as sc2:
                    xt = S.sb("xtB", [128, 2, D], F32, nslots=2, es=sc2)
                    xb = S.sb("xbB", [128, 2, D], BF16, nslots=2, es=sc2)
                    st = S.sb("stB", [128, 2, 2], F32, nslots=2, es=sc2)
                    xr = Ring.__new__(Ring)
                    xr.n, xr.i = 2, 0
                    make_u(xpre[896:1024, :], 128, uTB, 0, "pre", xt, xb, st, xr)
                    for t in range(8):
                        make_u(xown[t * 128:(t + 1) * 128, :], 128, uTB, 128 + t * 128, "own", xt, xb, st, xr)
                    make_u(xsm[:, :], 64, uTB, 1152, "smp", xt, xb, st, xr)
                    sct = S.sb("sct", [48, 3072], F32, es=sc2)
                    S.dma("sp", sct.ap, sconv, writes=[sct])
                    for chn in range(24):
                        sl = ps_next()
                        transpose_to(PS, sl, PS[:, sl, 0:48], sct[:, chn * 128:(chn + 1) * 128], C[0:48, IDF, 0:48],
                                     reads=[sct, C])
                        S.op("act", lambda e, sl=sl, chn=chn: e.copy(out=sconvT[:, chn, :], in_=PS[:, sl, 0:48]),
                             reads=[(PS, sl)], writes=[sconvT])
                    S.barrier()
                ckpt(72)
                DPB = decay_prep(uTB, [(128 + 128 * c, 128, False) for c in range(8)] + [(1152, 64, True)], sc)
                ckpt(73)
                if debug is not None and 'shift' in debug:
                    for _ in range(3):
                        S.op("act", lambda e: e.copy(out=flag_t.ap, in_=flag_t.ap), reads=[flag_t], writes=[flag_t])
                with ExitStack() as scH:
                    head_pass("B", uTB, DPB, scH)
                    S.barrier()
                ckpt(74)

                with ExitStack() as scP:
                    PW = 16 + 1152 + 304
                    xrow = S.sb("xrow", [128, 4, PW], F32, nslots=4, es=scP)
                    hist = S.sb("hist", [128, 2, 1024], F32, nslots=2, es=scP)
                    histT = S.sb("histT", [128, 240], F32, es=scP)
                    pwt = S.sb("pwt", [128, 4, 2, 256], BF16, es=scP)
                    ypl = S.sb("ypl", [128, 2, 1088], BF16, nslots=2, es=scP)
                    npoolT = S.sb("npoolT", [128, 8, 256], F32, nslots=8, es=scP)
                    invc = S.sb("invc", [128, 4, 16], F32, es=scP)
                    io_i = S.sb("io_i", [128, 16], I32, es=scP)
                    t16 = S.sb("t16", [128, 16], F32, es=scP)
                    otok = hist
                    S.dma("sp", hist[:, 0, :], spool[0:128, :], writes=[(hist, 0)])
                    S.dma("sp", hist[0:112, 1, :], spool[128:240, :], writes=[(hist, 1)])
                    for g in range(4):
                        S.dma("pool", pwt[:, g], pool_w[g].rearrange("(c p) d -> p c d", p=128), writes=[pwt])
                    S.op("dve", lambda e: e.memset(xrow.ap, 0.0), writes=[xrow])
                    S.op("dve", lambda e: e.memset(npoolT.ap, 0.0), writes=[npoolT])
                    S.op("pool", lambda e: e.iota(io_i.ap, pattern=[[1, 16]], base=1, channel_multiplier=0), writes=[io_i])
                    S.op("dve", lambda e: e.tensor_copy(out=invc[:, 0, :], in_=io_i.ap), reads=[io_i], writes=[invc])
                    S.op("dve", lambda e: e.tensor_scalar_add(out=invc[:, 0, :], in0=invc[:, 0, :], scalar1=pos0_t[:, 0:1]),
                         reads=[invc, pos0_t], writes=[invc])
                    for g in (3, 2, 1, 0):
                        S.op("dve", lambda e, g=g: e.tensor_scalar_min(out=invc[:, g, :], in0=invc[:, 0, :],
                                                                       scalar1=float((2, 4, 8, 16)[g])), reads=[invc], writes=[invc])
                    S.op("dve", lambda e: e.reciprocal(out=invc.ap, in_=invc.ap), reads=[invc], writes=[invc])
                    xe = lambda sl_: xrow[:, sl_, 16 + 1152:PW].rearrange("p (b j) -> p b j", j=19)
                    for g in range(4):
                        wwin = (2, 4, 8, 16)[g]
                        for c2 in range(2):
                            ch = 2 * g + c2
                            xs = 0 if ch % 2 == 0 else 3
                            ws = load_w(w_in[:, POFF + ch * 128:POFF + (ch + 1) * 128], 16)
                            for (a, b) in tok_tiles(1152):
                                sl = ps_next()
                                mm_group(sl, PS[:, sl, 0:b - a], [(WR.b[:, ws, k, :], uTB[:, k, a:b]) for k in range(16)],
                                         reads=[(WR.b, ws), uTB])
                                S.op("act", lambda e, sl=sl, a=a, b=b: e.copy(out=xrow[:, xs, 16 + a:16 + b], in_=PS[:, sl, 0:b - a]),
                                     reads=[(PS, sl)], writes=[(xrow, xs)])
                            sl = ps_next()
                            mm_group(sl, PS[:, sl, 0:64], [(WR.b[:, ws, k, :], uTB[:, k, 1152:1216]) for k in range(16)],
                                     reads=[(WR.b, ws), uTB])
                            S.op("act", lambda e, sl=sl: e.copy(out=xe(xs)[:, :, 15:19],
                                                               in_=PS[:, sl, 0:64].rearrange("p (b t) -> p b t", t=4)),
                                 reads=[(PS, sl)], writes=[(xrow, xs)])
                            for t2, rows in ((0, 128), (1, 112)):
                                sl = ps_next()
                                transpose_to(PS, sl, PS[:, sl, 0:rows], hist[0:rows, t2, ch * 128:(ch + 1) * 128],
                                             C[0:rows, IDF, 0:rows], reads=[(hist, t2), C])
                                S.op("act", lambda e, sl=sl, t2=t2, rows=rows: e.copy(out=histT[:, t2 * 128:t2 * 128 + rows],
                                                                                    in_=PS[:, sl, 0:rows]),
                                     reads=[(PS, sl)], writes=[histT])
                            S.op("dve", lambda e: e.tensor_copy(out=xe(xs)[:, :, 0:15],
                                                                in_=histT.ap.rearrange("p (b j) -> p b j", j=15)),
                                 reads=[histT], writes=[(xrow, xs)])
                            S.op("dve", lambda e, ch=ch: e.tensor_copy(out=npoolT[:, ch, 0:240].rearrange("p (b j) -> p b j", j=15),
                                                                      in_=xe(xs)[:, :, 4:19]), reads=[(xrow, xs)], writes=[(npoolT, ch)])
                            S.op("dve", lambda e, ch=ch: e.tensor_copy(out=npoolT[:, ch, 240:255], in_=xrow[:, xs, 16 + 1152 - 15:16 + 1152]),
                                 reads=[(xrow, xs)], writes=[(npoolT, ch)])
                            cur = xs
                            for lv in range(g + 1):
                                sh = 1 << lv
                                new = 1 + (lv % 2)
                                S.op("dve", lambda e, cur=cur, new=new, sh=sh: e.tensor_tensor(
                                    out=xrow[:, new, 16:PW], in0=xrow[:, cur, 16:PW], in1=xrow[:, cur, 16 - sh:PW - sh], op=ALU.add),
                                    reads=[(xrow, cur)], writes=[(xrow, new)])
                                cur = new
                            S.op("dve", lambda e, cur=cur, c2=c2: e.scalar_tensor_tensor(
                                out=ypl[:, c2, 0:1024], in0=xrow[:, cur, 144:1168], scalar=1.0 / wwin, in1=xrow[:, xs, 144:1168],
                                op0=ALU.mult, op1=ALU.subtract), reads=[(xrow, cur), (xrow, xs)], writes=[(ypl, c2)])
                            S.op("dve", lambda e, cur=cur, g=g: e.tensor_tensor(out=t16.ap, in0=xrow[:, cur, 144:160], in1=invc[:, g, :],
                                                                                 op=ALU.mult), reads=[(xrow, cur), invc], writes=[t16])
                            S.op("dve", lambda e, c2=c2: e.tensor_tensor(out=ypl[:, c2, 0:16], in0=t16.ap, in1=xrow[:, xs, 144:160],
                                                                         op=ALU.subtract), reads=[t16, (xrow, xs)], writes=[(ypl, c2)])
                            S.op("dve", lambda e, cur=cur, c2=c2: e.scalar_tensor_tensor(
                                out=ypl[:, c2, 1024:1088].rearrange("p (b t) -> p b t", t=4), in0=xe(cur)[:, :, 15:19],
                                scalar=1.0 / wwin, in1=xe(xs)[:, :, 15:19], op0=ALU.mult, op1=ALU.subtract),
                                reads=[(xrow, cur), (xrow, xs)], writes=[(ypl, c2)])
                        for dc in range(2):
                            for (a, b) in ((0, 512), (512, 1024), (1024, 1088)):
                                sl = ps_next()
                                mm_group(sl, PS[:, sl, 0:b - a],
                                         [(pwt[:, g, c2, dc * 128:(dc + 1) * 128], ypl[:, c2, a:b]) for c2 in range(2)],
                                         reads=[pwt, ypl])
                                S.op("act", lambda e, sl=sl, a=a, b=b, g=g, dc=dc: e.activation(
                                    out=ypsT[:, 2 * g + dc, a:b], in_=PS[:, sl, 0:b - a], func=AF.Copy,
                                    scale=pscT[:, 2 * g + dc:2 * g + dc + 1]), reads=[(PS, sl), pscT], writes=[(ypsT, 2 * g + dc)])
                    for ch in range(8):
                        for half in range(2):
                            sl = ps_next()
                            transpose_to(PS, sl, PS[:, sl, 0:128], npoolT[:, ch, half * 128:(half + 1) * 128], C[:, IDF, :],
                                         reads=[(npoolT, ch), C])
                            S.op("act", lambda e, sl=sl, ch=ch, half=half: e.copy(out=otok[:, half, ch * 128:(ch + 1) * 128],
                                                                                 in_=PS[:, sl, 0:128]),
                                 reads=[(PS, sl)], writes=[(otok, half)])
                    S.dma("sp", o_pool[0:128, :], otok[:, 0, :], reads=[(otok, 0)], is_out=True)
                    S.dma("sp", o_pool[128:255, :], otok[0:127, 1, :], reads=[(otok, 1)], is_out=True)
                    S.barrier()
                S.dma("sp", o_delta_p.rearrange("h k v -> k h v"), Sst.ap, reads=[Sst], is_out=True)
                octok = S.sb("octok", [51, 3072], F32, es=sc)
                for chn in range(24):
                    sl = ps_next()
                    transpose_to(PS, sl, PS[0:51, sl, 0:128], nconvT[:, chn, :], C[:, IDF, :], reads=[(nconvT, chn), C])
                    S.op("act", lambda e, sl=sl, chn=chn: e.copy(out=octok[:, chn * 128:(chn + 1) * 128], in_=PS[0:51, sl, 0:128]),
                         reads=[(PS, sl)], writes=[octok])
                S.dma("sp", o_conv, octok.ap, reads=[octok], is_out=True)
                S.barrier()
            scD.close()
            ckpt(8)
            T2 = [(0, 512), (512, 1024), (1024, 1088)]
            with ExitStack() as s4:
                MH = S.sb("MH", [128, 16, 1088], BF16, nslots=16, es=s4)
                with ExitStack() as sa:
                    uT2 = S.sb("uT2", [128, 16, 1088], BF16, nslots=16, es=sa)
                    xt = S.sb("xt4", [128, 1, D], F32, nslots=1, es=sa)
                    xb = S.sb("xb4", [128, 1, D], BF16, nslots=1, es=sa)
                    st = S.sb("st4", [128, 1, 2], F32, nslots=1, es=sa)
                    sg = S.sb("sg", [128, 3, 1088], F32, nslots=3, es=sa)
                    xr = Ring.__new__(Ring)
                    xr.n, xr.i = 1, 0
                    for t in range(8):
                        make_u(xown[t * 128:(t + 1) * 128, :], 128, uT2, t * 128, "own", xt, xb, st, xr)
                    make_u(xsm[:, :], 64, uT2, 1024, "smp", xt, xb, st, xr)
                    for n in range(16):
                        for (i, off) in ((0, GAOFF), (1, GBOFF)):
                            ws = load_w(w_in[:, off + n * 128:off + (n + 1) * 128], 16)
                            for (a, b) in T2:
                                sl = ps_next()
                                mm_group(sl, PS[:, sl, 0:b - a], [(WR.b[:, ws, k, :], uT2[:, k, a:b]) for k in range(16)],
                                         reads=[(WR.b, ws), uT2])
                                S.op("act", lambda e, sl=sl, a=a, b=b, i=i: e.activation(out=sg[:, i, a:b], in_=PS[:, sl, 0:b - a],
                                                                                        func=AF.Sigmoid),
                                     reads=[(PS, sl)], writes=[(sg, i)])
                        ws = load_w(w_proj_a[:, n * 128:(n + 1) * 128], 8)
                        for (a, b) in T2:
                            sl = ps_next()
                            mm_group(sl, PS[:, sl, 0:b - a], [(WR.b[:, ws, k, :], ogT[:, k, a:b]) for k in range(8)],
                                     reads=[(WR.b, ws), ogT])
                            S.op("dve", lambda e, sl=sl, a=a, b=b: e.tensor_tensor(out=sg[:, 2, a:b], in0=PS[:, sl, 0:b - a],
                                                                                  in1=sg[:, 0, a:b], op=ALU.mult),
                                 reads=[(PS, sl), (sg, 0)], writes=[(sg, 2)])
                        ws = load_w(w_proj_b[:, n * 128:(n + 1) * 128], 8)
                        for (a, b) in T2:
                            sl = ps_next()
                            mm_group(sl, PS[:, sl, 0:b - a], [(WR.b[:, ws, k, :], ypsT[:, k, a:b]) for k in range(8)],
                                     reads=[(WR.b, ws), ypsT])
                            S.op("dve", lambda e, sl=sl, a=a, b=b: e.tensor_tensor(out=sg[:, 1, a:b], in0=PS[:, sl, 0:b - a],
                                                                                  in1=sg[:, 1, a:b], op=ALU.mult),
                                 reads=[(PS, sl), (sg, 1)], writes=[(sg, 1)])
                            S.op("dve", lambda e, a=a, b=b, n=n: e.tensor_tensor(out=MH[:, n, a:b], in0=sg[:, 1, a:b],
                                                                                in1=sg[:, 2, a:b], op=ALU.add),
                                 reads=[(sg, 1), (sg, 2)], writes=[(MH, n)])
                    S.barrier()
                ckpt(81)
                xT = S.sb("xT", [128, 16, 1088], F32, nslots=16, es=s4)
                rsb = S.sb("rsb", [128, 1088], F32, es=s4)
                tmpf = S.sb("tmpf", [128, 2, 1088], F32, nslots=2, es=s4)

                def mod_add(n, sl, a, b, chunk0):
                    if b <= 1024:
                        S.op("dve", lambda e: e.scalar_tensor_tensor(out=xT[:, n, a:b], in0=PS[:, sl, 0:b - a],
                                                                     scalar=modT[:, chunk0 + n, 0:1], in1=xT[:, n, a:b],
                                                                     op0=ALU.mult, op1=ALU.add),
                             reads=[(PS, sl), modT, (xT, n)], writes=[(xT, n)])
                    else:
                        v4 = lambda ap_: ap_.rearrange("p (b t) -> p b t", t=4)
                        S.op("dve", lambda e: e.tensor_tensor(out=v4(tmpf[:, 0, 0:64]), in0=v4(PS[:, sl, 0:64]),
                                                              in1=modT[:, chunk0 + n, 1:17].unsqueeze(2).broadcast_to([128, 16, 4]),
                                                              op=ALU.mult), reads=[(PS, sl), modT], writes=[(tmpf, 0)])
                        S.op("dve", lambda e: e.tensor_tensor(out=xT[:, n, a:b], in0=xT[:, n, a:b], in1=tmpf[:, 0, 0:64], op=ALU.add),
                             reads=[(tmpf, 0), (xT, n)], writes=[(xT, n)])

                with ExitStack() as sb_:
                    xl = S.sb("xl", [128, D], F32, es=sb_)
                    for t in range(9):
                        ntok = 128 if t < 8 else 64
                        src = xown[t * 128:(t + 1) * 128, :] if t < 8 else xsm[:, :]
                        S.dma("sp", xl[0:ntok, :], src, writes=[xl])
                        for k in range(16):
                            sl = ps_next()
                            transpose_to(PS, sl, PS[:, sl, 0:ntok], xl[0:ntok, k * 128:(k + 1) * 128], C[0:ntok, IDF, 0:ntok],
                                         reads=[xl, C])
                            S.op("act", lambda e, sl=sl, k=k, t=t, ntok=ntok: e.copy(out=xT[:, k, t * 128:t * 128 + ntok],
                                                                                    in_=PS[:, sl, 0:ntok]),
                                 reads=[(PS, sl)], writes=[(xT, k)])
                    for n in range(16):
                        ws = load_w(w_out[:, n * 128:(n + 1) * 128], 16)
                        for (a, b) in T2:
                            sl = ps_next()
                            mm_group(sl, PS[:, sl, 0:b - a], [(WR.b[:, ws, k, :], MH[:, k, a:b]) for k in range(16)],
                                     reads=[(WR.b, ws), MH])
                            mod_add(n, sl, a, b, 32)
                    S.barrier()
                ckpt(82)

                def rstd_bc():
                    sls = [ps_next() for _ in T2]
                    for n in range(16):
                        S.op("act", lambda e, n=n: e.activation(out=tmpf[:, n % 2, :], in_=xT[:, n, :], func=AF.Square),
                             reads=[(xT, n)], writes=[(tmpf, n % 2)])

                        def fn(e, n=n):
                            ins = None
                            for sl_, (a, b) in zip(sls, T2):
                                ins = e.matmul(PS[:, sl_, 0:b - a], lhsT=C[:, ONESF, :], rhs=tmpf[:, n % 2, a:b], start=(n == 0),
                                               stop=(n == 15))
                            return ins
                        S.op("pe", fn, reads=[(tmpf, n % 2), C], writes=[(PS, s_) for s_ in sls])
                    for sl_, (a, b) in zip(sls, T2):
                        S.op("act", lambda e, sl_=sl_, a=a, b=b: e.activation(out=rsb[:, a:b], in_=PS[:, sl_, 0:b - a], func=AF.Ln,
                                                                             scale=1.0 / D, bias=EPS), reads=[(PS, sl_)], writes=[rsb])
                    S.op("act", lambda e: e.activation(out=rsb.ap, in_=rsb.ap, func=AF.Exp, scale=-0.5), reads=[rsb], writes=[rsb])

                rstd_bc()
                for n in range(16):
                    S.op("dve", lambda e, n=n: e.tensor_tensor(out=tmpf[:, n % 2, :], in0=xT[:, n, :], in1=rsb.ap, op=ALU.mult),
                         reads=[(xT, n), rsb], writes=[(tmpf, n % 2)])
                    S.op("act", lambda e, n=n: e.activation(out=MH[:, n, 0:1024], in_=tmpf[:, n % 2, 0:1024], func=AF.Identity,
                                                            scale=GG[:, 1, n, 0:1], bias=modT[:, 48 + n, 0:1]),
                         reads=[(tmpf, n % 2), (GG, 1), modT], writes=[(MH, n)])
                    v4 = lambda ap_: ap_.rearrange("p (b t) -> p b t", t=4)
                    S.op("dve", lambda e, n=n: e.tensor_tensor(out=v4(tmpf[:, n % 2, 1024:1088]), in0=v4(tmpf[:, n % 2, 1024:1088]),
                                                               in1=GG[:, 1, n, 1:17].unsqueeze(2).broadcast_to([128, 16, 4]),
                                                               op=ALU.mult), reads=[(tmpf, n % 2), (GG, 1)], writes=[(tmpf, n % 2)])
                    S.op("dve", lambda e, n=n: e.tensor_tensor(out=v4(MH[:, n, 1024:1088]), in0=v4(tmpf[:, n % 2, 1024:1088]),
                                                               in1=modT[:, 48 + n, 1:17].unsqueeze(2).broadcast_to([128, 16, 4]),
                                                               op=ALU.add), reads=[(tmpf, n % 2), modT], writes=[(MH, n)])
                ckpt(83)
                actq = S.sb("actq", [128, 4, 1088], BF16, nslots=4, es=s4)
                sgf = S.sb("sgf", [128, 1088], F32, es=s4)
                for qi in range(11):
                    for j in range(4):
                        f_ = qi * 4 + j
                        wg = load_w(w_gate_up[:, f_ * 128:(f_ + 1) * 128], 16)
                        for (a, b) in T2:
                            sl = ps_next()
                            mm_group(sl, PS[:, sl, 0:b - a], [(WR.b[:, wg, k, :], MH[:, k, a:b]) for k in range(16)],
                                     reads=[(WR.b, wg), MH])
                            S.op("act", lambda e, sl=sl, a=a, b=b: e.activation(out=sgf[:, a:b], in_=PS[:, sl, 0:b - a], func=AF.Silu),
                                 reads=[(PS, sl)], writes=[sgf])
                        wu = load_w(w_gate_up[:, DFF + f_ * 128:DFF + (f_ + 1) * 128], 16)
                        for (a, b) in T2:
                            sl = ps_next()
                            mm_group(sl, PS[:, sl, 0:b - a], [(WR.b[:, wu, k, :], MH[:, k, a:b]) for k in range(16)],
                                     reads=[(WR.b, wu), MH])
                            S.op("dve", lambda e, sl=sl, a=a, b=b, j=j: e.tensor_tensor(out=actq[:, j, a:b], in0=PS[:, sl, 0:b - a],
                                                                                       in1=sgf[:, a:b], op=ALU.mult),
                                 reads=[(PS, sl), sgf], writes=[(actq, j)])
                    for n in range(16):
                        wd = load_w(w_down[qi * 512:(qi + 1) * 512, n * 128:(n + 1) * 128], 4)
                        for (a, b) in T2:
                            sl = ps_next()
                            mm_group(sl, PS[:, sl, 0:b - a], [(WR.b[:, wd, j, :], actq[:, j, a:b]) for j in range(4)],
                                     reads=[(WR.b, wd), actq])
                            mod_add(n, sl, a, b, 80)
                ckpt(84)
                rstd_bc()
                for n in range(16):
                    S.op("dve", lambda e, n=n: e.scalar_tensor_tensor(out=xT[:, n, :], in0=xT[:, n, :], scalar=fgT[:, n:n + 1],
                                                                      in1=rsb.ap, op0=ALU.mult, op1=ALU.mult),
                         reads=[(xT, n), fgT, rsb], writes=[(xT, n)])
                ytok = S.sb("ytok", [128, 1, D], F32, nslots=1, es=s4)
                for t in range(9):
                    ntok = 128 if t < 8 else 64
                    ys = 0
                    for k4 in range(4):
                        sl = ps_next()

                        def ft(e, sl=sl, k4=k4, t=t, ntok=ntok):
                            ins = None
                            for j in range(4):
                                ins = e.matmul(PS[0:ntok, sl, j * 128:(j + 1) * 128], lhsT=xT[:, k4 * 4 + j, t * 128:t * 128 + ntok],
                                               rhs=C[:, IDF, :], start=True, stop=True)
                            return ins
                        S.op("pe", ft, reads=[xT, C], writes=[(PS, sl)])
                        S.op("act", lambda e, sl=sl, k4=k4, ys=ys, ntok=ntok: e.copy(out=ytok[0:ntok, ys, k4 * 512:(k4 + 1) * 512],
                                                                                    in_=PS[0:ntok, sl, :]),
                             reads=[(PS, sl)], writes=[(ytok, ys)])
                    dst = y_own[t * 128:(t + 1) * 128, :] if t < 8 else y_sm[:, :]
                    S.dma("sp", dst, ytok[0:ntok, ys, :], reads=[(ytok, ys)], is_out=True)
                S.barrier()

        try:
            body()
        except _Stop:
            pass
        S.finish()
    return nc, dbg


def prep_inputs(inp):
    f = lambda a: np.ascontiguousarray(a, dtype=np.float32)
    shared = {
        "w_ada": f(inp["w_ada"][0]), "b_ada": f(inp["b_ada"][0].reshape(96, 128)),
        "norm1_g": f(inp["norm1_g"][0].reshape(16, 128)), "w_in": f(inp["w_in"][0]),
        "conv_w": f(inp["conv_w"][0].reshape(96, 128)), "a_log": f(inp["a_log"][0].reshape(1, 8)),
        "dt_bias": f(inp["dt_bias"][0].reshape(1, 8)), "o_norm_g": f(inp["o_norm_g"][0].reshape(1, 128)),
        "pool_w": f(inp["pool_w"][0]), "pool_scale": f(inp["pool_scale"][0].reshape(8, 128)),
        "w_proj_a": f(inp["w_proj_a"][0]), "w_proj_b": f(inp["w_proj_b"][0]), "w_out": f(inp["w_out"][0]),
        "norm2_g": f(inp["norm2_g"][0].reshape(16, 128)), "w_gate_up": f(inp["w_gate_up"][0]),
        "w_down": f(inp["w_down"][0]), "final_g": f(inp["final_g"].reshape(16, 128)),
    }
    maps = []
    for c in range(8):
        b, hh = c // 2, c % 2
        sb = slice(16 * c, 16 * c + 16)
        m = dict(shared)
        xp = inp["x_prompt"][b]
        m["xpre"] = f(xp[0:1024]) if hh == 1 else np.zeros((1024, D), np.float32)
        m["xown"] = f(xp[1024 * hh:1024 * hh + 1024])
        m["xsm"] = f(inp["x_sample"][sb].reshape(64, D))
        m["cc"] = f(np.concatenate([inp["c_prompt"][b:b + 1], inp["c_sample"][sb]], axis=0))
        m["sdelta"] = f(inp["state_delta"][0, sb])
        m["sconv"] = f(inp["state_conv"][0, sb].reshape(48, 3072))
        m["spool"] = f(inp["state_pool"][0, sb].reshape(240, 1024))
        m["flag"] = np.full((128, 1), float(hh), np.float32)
        m["pos0"] = np.full((128, 1), float(1024 * hh), np.float32)
        maps.append(m)
    return maps


_NC_CACHE = {}


def kernel(**inputs):
    if "nc" not in _NC_CACHE:
        _NC_CACHE["nc"] = build_program()[0]
    nc = _NC_CACHE["nc"]
    maps = prep_inputs(inputs)
    res = run_bass_kernel_spmd(nc, maps, core_ids=list(range(8))).results
    y_prompt = np.zeros((4, 2048, D), np.float32)
    y_sample = np.zeros((128, 4, D), np.float32)
    ndp = np.zeros((1, 4, NH, 128, 128), np.float32)
    ncp = np.zeros((1, 4, 3, 3072), np.float32)
    npp = np.zeros((1, 4, 15, 1024), np.float32)
    nds = np.zeros((1, 128, NH, 128, 128), np.float32)
    ncs = np.zeros((1, 128, 3, 3072), np.float32)
    nps = np.zeros((1, 128, 15, 1024), np.float32)
    for c in range(8):
        b, hh = c // 2, c % 2
        r = res[c]
        sb = slice(16 * c, 16 * c + 16)
        y_prompt[b, 1024 * hh:1024 * hh + 1024] = r["y_own"]
        y_sample[sb] = r["y_sm"].reshape(16, 4, D)
        nds[0, sb] = r["o_delta_s"]
        ncs[0, sb] = r["o_conv"][0:48].reshape(16, 3, 3072)
        nps[0, sb] = r["o_pool"][0:240].reshape(16, 15, 1024)
        if hh == 1:
            ndp[0, b] = r["o_delta_p"]
            ncp[0, b] = r["o_conv"][48:51]
            npp[0, b] = r["o_pool"][240:255]
    return (y_prompt, y_sample, ndp, ncp, npp, nds, ncs, nps)
```
